# Optimizing a Trainium2 kernel written in Bass

```python
import math
import jax, jax.numpy as jnp
from jax import lax
import numpy as np

D_MODEL = 1024
BATCH = 8
SEQ = 2048
DEPTH = 1

CHUNK = 64
GM_GROUPS = 8
GM_GROUP_DIM = D_MODEL // GM_GROUPS
GM_WIDTH = GM_GROUPS * GM_GROUP_DIM
GM_BLOCK = 128
DN_HEADS = 8
DN_DK = 128
DN_DV = 128
DN_KEY = DN_HEADS * DN_DK
DN_VAL = DN_HEADS * DN_DV
DN_CONV = 4
MOE_GROUPS = 4
MOE_EXPERTS_PER_GROUP = 8
N_EXPERTS = MOE_GROUPS * MOE_EXPERTS_PER_GROUP
MOE_TOPK = 2
D_EXPERT = D_MODEL // 2
MOE_BLOCK = 128
LN_EPS = 1e-5
RMS_EPS = 1e-6
L2_EPS = 1e-6
DEEPNORM_ALPHA = (2 * DEPTH) ** 0.25
DEEPNORM_BETA = (8 * DEPTH) ** -0.25
IN_SIZES = (2 * GM_WIDTH, 2 * DN_KEY + DN_VAL, DN_VAL, DN_HEADS, DN_HEADS, D_MODEL, D_MODEL)
IN_WIDTH = 2 * GM_WIDTH + 2 * DN_KEY + 2 * DN_VAL + 2 * DN_HEADS + 2 * D_MODEL

kernel_name = "hybrid_gmlp_gdn_hiermoe_deepnorm"


def _layer_norm(x, g, b):
    xf = x.astype(jnp.float32)
    mu = jnp.mean(xf, -1, keepdims=True)
    var = jnp.mean(jnp.square(xf - mu), -1, keepdims=True)
    return ((xf - mu) * lax.rsqrt(var + LN_EPS) * g + b).astype(x.dtype)


def _causal_dwconv(x, w):
    c = x.shape[-1]
    return lax.conv_general_dilated(
        x, w[:, None, :].astype(x.dtype), window_strides=(1,),
        padding=[(DN_CONV - 1, 0)], dimension_numbers=('NWC', 'WIO', 'NWC'),
        feature_group_count=c)


def _spatial_gating(u, v, ln_g, ln_b, w_s, b_s):
    bsz, t, _ = u.shape
    nb = t // GM_BLOCK
    v = _layer_norm(v, ln_g, ln_b).reshape(bsz, nb, GM_BLOCK, GM_GROUPS, GM_GROUP_DIM)
    chunk_id = jnp.arange(GM_BLOCK) // CHUNK
    mask = chunk_id[None, :] <= chunk_id[:, None]
    w = jnp.where(mask[None], w_s, 0).astype(v.dtype)
    s = jnp.einsum('gts,bnsgc->bntgc', w, v) + b_s.T[None, None, :, :, None].astype(v.dtype)
    return u * s.reshape(bsz, t, GM_WIDTH)


def _gated_delta_rule(q, k, v, g, beta):
    bsz, t, nh, dk = q.shape
    dv = v.shape[-1]
    n = t // CHUNK

    def to_chunks(a):
        a = a.reshape((bsz, n, CHUNK, nh) + a.shape[3:])
        return jnp.moveaxis(a, 3, 1)

    q, k, v, g, beta = (to_chunks(a) for a in (q, k, v, g, beta))
    q = q * (dk ** -0.5)
    G = jnp.cumsum(g, axis=-1)
    idx = jnp.arange(CHUNK)
    causal = idx[:, None] >= idx[None, :]
    strict = idx[:, None] > idx[None, :]
    decay = jnp.exp(jnp.where(causal, G[..., :, None] - G[..., None, :], -jnp.inf))
    kk = jnp.einsum('bhnik,bhnjk->bhnij', k, k)
    lmat = jnp.where(strict, beta[..., :, None] * kk * decay, 0.0)
    amat = jnp.eye(CHUNK, dtype=q.dtype) + lmat
    rhs = jnp.concatenate([v * beta[..., None], k * (beta * jnp.exp(G))[..., None]], -1)
    sol = lax.linalg.triangular_solve(amat, rhs, left_side=True, lower=True, unit_diagonal=True)
    u_c, w_c = sol[..., :dv], sol[..., dv:]
    attn = jnp.einsum('bhnik,bhnjk->bhnij', q, k) * decay
    q_dec = q * jnp.exp(G)[..., None]
    g_last = G[..., -1:]
    k_dec = k * jnp.exp(g_last - G)[..., None]
    chunk_decay = jnp.exp(g_last)[..., None]

    def step(state, xs):
        uu, ww, aa, qd, kd, cd = xs
        v_new = uu - jnp.einsum('bhck,bhkv->bhcv', ww, state)
        o = jnp.einsum('bhck,bhkv->bhcv', qd, state) + jnp.einsum('bhij,bhjv->bhiv', aa, v_new)
        state = state * cd + jnp.einsum('bhck,bhcv->bhkv', kd, v_new)
        return state, o

    xs = tuple(jnp.moveaxis(a, 2, 0) for a in (u_c, w_c, attn, q_dec, k_dec, chunk_decay))
    s0 = jnp.zeros((bsz, nh, dk, dv), q.dtype)
    _, o = lax.scan(step, s0, xs)
    return o.transpose(1, 0, 3, 2, 4).reshape(bsz, t, nh, dv)


def _token_mixer(h, w_in, b_in, gm_ln_g, gm_ln_b, gm_w_s, gm_b_s,
                 dn_conv_w, dn_a_log, dn_dt_bias, dn_norm_w, w_pa, w_pb, w_o):
    bsz, t, _ = h.shape
    proj = h @ w_in + b_in
    split_at = list(np.cumsum(IN_SIZES)[:-1])
    gm_uv, dn_qkv, dn_z, dn_a, dn_b, gate_a, gate_b = jnp.split(proj, split_at, axis=-1)

    u, v = jnp.split(jax.nn.gelu(gm_uv, approximate=False), 2, axis=-1)
    y_a = _spatial_gating(u, v, gm_ln_g, gm_ln_b, gm_w_s, gm_b_s)

    qkv = jax.nn.silu(_causal_dwconv(dn_qkv, dn_conv_w)).astype(jnp.float32)
    q, k, vv = jnp.split(qkv, [DN_KEY, 2 * DN_KEY], axis=-1)
    q = q.reshape(bsz, t, DN_HEADS, DN_DK)
    k = k.reshape(bsz, t, DN_HEADS, DN_DK)
    vv = vv.reshape(bsz, t, DN_HEADS, DN_DV)
    q = q * lax.rsqrt(jnp.sum(q * q, -1, keepdims=True) + L2_EPS)
    k = k * lax.rsqrt(jnp.sum(k * k, -1, keepdims=True) + L2_EPS)
    beta = jax.nn.sigmoid(dn_b.astype(jnp.float32))
    g = -jnp.exp(dn_a_log.astype(jnp.float32)) * jax.nn.softplus(
        dn_a.astype(jnp.float32) + dn_dt_bias.astype(jnp.float32))
    o = _gated_delta_rule(q, k, vv, g, beta)
    z = dn_z.astype(jnp.float32).reshape(bsz, t, DN_HEADS, DN_DV)
    o = o * lax.rsqrt(jnp.mean(o * o, -1, keepdims=True) + RMS_EPS) * dn_norm_w * jax.nn.silu(z)
    y_b = o.reshape(bsz, t, DN_VAL).astype(h.dtype)

    merged = jax.nn.sigmoid(gate_a) * (y_a @ w_pa) + jax.nn.sigmoid(gate_b) * (y_b @ w_pb)
    return merged @ w_o


def _hier_moe(x, w_rg, b_rg, w_re, b_re, w1, w3, w2):
    bsz, t, d = x.shape
    n_tok = bsz * t
    xt = x.reshape(n_tok, d)
    pg = jax.nn.softmax((xt @ w_rg + b_rg).astype(jnp.float32), -1)
    grp = jnp.argmax(pg, -1)
    p_grp = jnp.take_along_axis(pg, grp[:, None], -1)
    le = (xt @ w_re + b_re).astype(jnp.float32).reshape(n_tok, MOE_GROUPS, MOE_EXPERTS_PER_GROUP)
    le = jnp.take_along_axis(le, grp[:, None, None], 1)[:, 0]
    top_v, top_i = lax.top_k(le, MOE_TOPK)
    gate = (p_grp * jax.nn.softmax(top_v, -1)).reshape(-1)
    eid = (grp[:, None] * MOE_EXPERTS_PER_GROUP + top_i).reshape(-1).astype(jnp.int32)
    tok = jnp.repeat(jnp.arange(n_tok, dtype=jnp.int32), MOE_TOPK)
    m = n_tok * MOE_TOPK
    order = jnp.argsort(eid)
    e_s, tok_s, gate_s = eid[order], tok[order], gate[order]
    counts = jnp.zeros((N_EXPERTS,), jnp.int32).at[eid].add(1)
    start = jnp.cumsum(counts) - counts
    padded = (counts + MOE_BLOCK - 1) // MOE_BLOCK * MOE_BLOCK
    pad_end = jnp.cumsum(padded)
    pad_start = pad_end - padded
    dest = pad_start[e_s] + (jnp.arange(m, dtype=jnp.int32) - start[e_s])
    n_blocks = (m + N_EXPERTS * (MOE_BLOCK - 1) + MOE_BLOCK - 1) // MOE_BLOCK
    rows = n_blocks * MOE_BLOCK
    x_pad = jnp.zeros((rows, d), x.dtype).at[dest].set(xt[tok_s])
    blk_start = jnp.arange(n_blocks, dtype=jnp.int32) * MOE_BLOCK
    blk_e = jnp.minimum(jnp.sum(pad_end[None, :] <= blk_start[:, None], -1), N_EXPERTS - 1)

    def expert_block(args):
        xb, e = args
        hb = jax.nn.silu(xb @ w1[e]) * (xb @ w3[e])
        return hb @ w2[e]

    y_pad = lax.map(expert_block, (x_pad.reshape(n_blocks, MOE_BLOCK, d), blk_e)).reshape(rows, d)
    y = y_pad[dest] * gate_s[:, None].astype(x.dtype)
    out = jnp.zeros((n_tok, d), x.dtype).at[tok_s].add(y)
    return out.reshape(bsz, t, d)


def setup_inputs(seed: int = 0) -> dict:
    key = jax.random.key(seed)
    ks = list(jax.random.split(key, 48))
    kit = iter(ks)

    def nrm(shape, scale):
        return scale * jax.random.normal(next(kit), shape, jnp.float32)

    def gain(shape):
        return 1.0 + nrm(shape, 0.05)

    L, D = DEPTH, D_MODEL
    s_in = D ** -0.5
    w_in = jnp.concatenate([
        nrm((L, D, 2 * GM_WIDTH), s_in * DEEPNORM_BETA),
        nrm((L, D, 2 * DN_KEY), s_in),
        nrm((L, D, DN_VAL), s_in * DEEPNORM_BETA),
        nrm((L, D, DN_VAL), s_in),
        nrm((L, D, 2 * DN_HEADS), s_in),
        nrm((L, D, 2 * D_MODEL), s_in),
    ], axis=-1)
    a_init = jax.random.uniform(next(kit), (L, DN_HEADS), jnp.float32, 1.0, 16.0)
    log_dt = jax.random.uniform(next(kit), (L, DN_HEADS), jnp.float32, math.log(1e-3), math.log(1e-1))
    dt = jnp.exp(log_dt)
    dt_bias = dt + jnp.log(-jnp.expm1(-dt))
    return {
        "x": jax.random.normal(next(kit), (BATCH, SEQ, D), jnp.float32),
        "ln_in_g": gain((D,)),
        "ln_in_b": nrm((D,), 0.02),
        "w_in": w_in,
        "b_in": nrm((L, IN_WIDTH), 0.01),
        "gm_ln_g": gain((L, GM_WIDTH)),
        "gm_ln_b": nrm((L, GM_WIDTH), 0.02),
        "gm_w_s": nrm((L, GM_GROUPS, GM_BLOCK, GM_BLOCK), 0.5 * GM_BLOCK ** -0.5),
        "gm_b_s": 1.0 + nrm((L, GM_GROUPS, GM_BLOCK), 0.1),
        "dn_conv_w": nrm((L, DN_CONV, 2 * DN_KEY + DN_VAL), DN_CONV ** -0.5),
        "dn_a_log": jnp.log(a_init),
        "dn_dt_bias": dt_bias,
        "dn_norm_w": gain((L, DN_DV)),
        "w_pa": nrm((L, GM_WIDTH, D), GM_WIDTH ** -0.5 * DEEPNORM_BETA),
        "w_pb": nrm((L, DN_VAL, D), DN_VAL ** -0.5 * DEEPNORM_BETA),
        "w_o": nrm((L, D, D), D ** -0.5 * DEEPNORM_BETA),
        "ln1_g": gain((L, D)),
        "ln1_b": nrm((L, D), 0.02),
        "w_rg": nrm((L, D, MOE_GROUPS), s_in),
        "b_rg": nrm((L, MOE_GROUPS), 0.01),
        "w_re": nrm((L, D, N_EXPERTS), s_in),
        "b_re": nrm((L, N_EXPERTS), 0.01),
        "w1": nrm((L, N_EXPERTS, D, D_EXPERT), s_in * DEEPNORM_BETA),
        "w3": nrm((L, N_EXPERTS, D, D_EXPERT), s_in * DEEPNORM_BETA),
        "w2": nrm((L, N_EXPERTS, D_EXPERT, D), D_EXPERT ** -0.5 * DEEPNORM_BETA),
        "ln2_g": gain((L, D)),
        "ln2_b": nrm((L, D), 0.02),
    }


def reference(x, ln_in_g, ln_in_b, w_in, b_in, gm_ln_g, gm_ln_b, gm_w_s, gm_b_s,
              dn_conv_w, dn_a_log, dn_dt_bias, dn_norm_w, w_pa, w_pb, w_o,
              ln1_g, ln1_b, w_rg, b_rg, w_re, b_re, w1, w3, w2, ln2_g, ln2_b):
    h = _layer_norm(x, ln_in_g, ln_in_b)
    for l in range(DEPTH):
        mix = _token_mixer(h, w_in[l], b_in[l], gm_ln_g[l], gm_ln_b[l], gm_w_s[l], gm_b_s[l],
                           dn_conv_w[l], dn_a_log[l], dn_dt_bias[l], dn_norm_w[l],
                           w_pa[l], w_pb[l], w_o[l])
        h = _layer_norm(DEEPNORM_ALPHA * h + mix, ln1_g[l], ln1_b[l])
        ffn = _hier_moe(h, w_rg[l], b_rg[l], w_re[l], b_re[l], w1[l], w3[l], w2[l])
        h = _layer_norm(DEEPNORM_ALPHA * h + ffn, ln2_g[l], ln2_b[l])
    return h
```

```python
import contextlib
import numpy as np
import concourse.bass as bass
import concourse.mybir as mybir
from concourse.bass_utils import run_bass_kernel_spmd

F32 = mybir.dt.float32
BF16 = mybir.dt.bfloat16
I32 = mybir.dt.int32
AF = mybir.ActivationFunctionType
ALU = mybir.AluOpType
AX = mybir.AxisListType

T = 2048
D = 1024
NT = 16
NH = 8
INW = 8208
NE = 32
REC_STEP = 2.5
REC_OFF = 0.5
LN_ADD_ENG = "dve"
CAP = 256
DF = 512
ALPHA = 2 ** 0.25
NEG = -60000.0


class Buf:
    __slots__ = ("name", "last_w", "readers")

    def __init__(self, name=""):
        self.name = name
        self.last_w = None
        self.readers = {}


class Op:
    __slots__ = ("eng", "fn", "deps", "key", "signal", "semval", "dma", "dslot", "dval", "prewait", "fence", "pos")


ENGS = ("pe", "act", "dve", "pool", "sp")
DMA_K = 6


class Prog:
    def __init__(self, nc):
        self.nc = nc
        self.ops = {e: [] for e in ENGS}
        self.phase = 0
        self.seq = 0
        self.par = False
        self.sid = 0
        self.cnt = {}
        self.step = {}

    def begin_parallel(self):
        self.phase += 1
        self.par = True
        self.cnt = {}
        self.step = {}

    def set_stream(self, sid):
        self.sid = sid

    def end_parallel(self):
        self.phase += 1
        self.par = False
        self.sid = 0

    def _key(self):
        if self.par:
            k = self.cnt.get(self.sid, 0)
            self.cnt[self.sid] = k + self.step.get(self.sid, 1)
            return (self.phase, k, self.sid)
        self.seq += 1
        return (self.phase, self.seq, 0)

    def _deps(self, reads, writes):
        deps = set()
        for b in reads:
            if b.last_w is not None:
                deps.add(b.last_w)
        for b in writes:
            if b.last_w is not None:
                deps.add(b.last_w)
            for r in b.readers.values():
                deps.add(r)
        return deps

    def _mark(self, op, reads, writes):
        for b in reads:
            if op.dma:
                b.readers[("d" + op.eng, id(op))] = op
            else:
                b.readers[op.eng] = op
        for b in writes:
            b.last_w = op
            b.readers = {}

    @staticmethod
    def _flat(xs):
        out = []
        for x in xs:
            if isinstance(x, (list, tuple)):
                out.extend(Prog._flat(x))
            else:
                out.append(x)
        return out

    def _new(self, eng, fn, dma, reads, writes):
        reads = self._flat(reads)
        writes = self._flat(writes)
        o = Op()
        o.eng, o.fn, o.dma, o.fence = eng, fn, dma, False
        o.deps = self._deps(reads, writes)
        o.key = self._key()
        o.signal = dma
        o.semval = None
        o.prewait = None
        self.ops[eng].append(o)
        self._mark(o, reads, writes)
        return o

    def op(self, eng, fn, reads=(), writes=()):
        return self._new(eng, fn, False, reads, writes)

    def dma(self, eng, fn, reads=(), writes=()):
        return self._new(eng, fn, True, reads, writes)

    def fence(self):
        assert not self.par
        self.phase += 1
        for e in ENGS:
            o = Op()
            o.eng, o.fn, o.dma, o.fence = e, None, False, True
            o.deps = set()
            o.key = self._key()
            o.signal = False
            o.semval = None
            o.prewait = None
            self.ops[e].append(o)
        self.phase += 1

    def emit(self, final_waits=()):
        nc = self.nc
        for e in ENGS:
            self.ops[e].sort(key=lambda o: o.key)
        last_comp = {e: None for e in ENGS}
        last_dmas = {e: [] for e in ENGS}
        merged = sorted((o for e in ENGS for o in self.ops[e]), key=lambda o: o.key)
        i = 0
        while i < len(merged):
            o = merged[i]
            if o.fence:
                deps = set()
                for e in ENGS:
                    if last_comp[e] is not None:
                        deps.add(last_comp[e])
                    deps.update(last_dmas[e][-DMA_K:])
                j = i
                while j < len(merged) and merged[j].fence:
                    merged[j].deps = set(deps)
                    j += 1
                i = j
                continue
            if o.dma:
                last_dmas[o.eng].append(o)
            else:
                last_comp[o.eng] = o
            i += 1
        for e in ENGS:
            hist = [o for o in self.ops[e] if o.dma]
            for n, o in enumerate(hist):
                o.dslot = n % DMA_K
                o.dval = 16 * (n // DMA_K + 1)
                o.prewait = hist[n - DMA_K] if n >= DMA_K else None
        ndma = {e: sum(1 for o in self.ops[e] if o.dma) for e in ENGS}
        for e in ENGS:
            for o in self.ops[e]:
                for d in o.deps:
                    assert d.key < o.key, ("dependency ordered after dependent", d.eng, d.key, o.eng, o.key)
        for e in ENGS:
            for o in self.ops[e]:
                for d in o.deps:
                    if not d.dma:
                        if d.eng == "pe" and o.eng == "pe" and not o.dma and not o.fence:
                            continue
                        d.signal = True
        for e in ENGS:
            c = 0
            for o in self.ops[e]:
                if o.dma or o.fence:
                    continue
                if o.signal:
                    c += 1
                    o.semval = c
        with contextlib.ExitStack() as st:
            sem = {e: st.enter_context(nc.semaphore("s_" + e)) for e in ENGS}
            dsem = {e: [st.enter_context(nc.semaphore("d_%s%d" % (e, k))) for k in range(DMA_K)]
                    for e in ENGS if ndma[e] > 0}
            block = st.enter_context(nc.Block())

            def run(e, engine):
                waited = {}

                def wait_for(d):
                    if d.dma:
                        key = ("d", d.eng, d.dslot)
                        s, v = dsem[d.eng][d.dslot], d.dval
                    else:
                        key = d.eng
                        s, v = sem[d.eng], d.semval
                    if waited.get(key, 0) >= v:
                        return
                    waited[key] = v
                    engine.wait_ge(s, v)

                for o in self.ops[e]:
                    for d in sorted(o.deps, key=lambda d: d.key):
                        if d.eng == "pe" and e == "pe" and not d.dma and not o.dma and not o.fence:
                            continue
                        wait_for(d)
                    if o.fence:
                        continue
                    if o.dma:
                        if o.prewait is not None:
                            wait_for(o.prewait)
                        o.fn(engine).then_inc(dsem[e][o.dslot], 16)
                    else:
                        ins = o.fn(engine)
                        if o.signal:
                            ins.then_inc(sem[e], 1)
                if e == "sp":
                    for d in final_waits:
                        wait_for(d)

            @block.tensor
            def _(eng):
                run("pe", eng)

            @block.scalar
            def _(eng):
                run("act", eng)

            @block.vector
            def _(eng):
                run("dve", eng)

            @block.gpsimd
            def _(eng):
                run("pool", eng)

            @block.sync
            def _(eng):
                run("sp", eng)


class Tl:
    def __init__(self, h, name, nb=1):
        self.h = h
        self.b = Buf(name)
        self.bs = [Buf(name + str(i)) for i in range(nb)] if nb > 1 else [self.b]

    def __getitem__(self, k):
        return self.h[k]


def build(stop_after=99, dbg=False):
    nc = bass.Bass("TRN2", target_bir_lowering=False)
    P = Prog(nc)

    def din(name, shape, dt=F32):
        return nc.dram_tensor(name, list(shape), dt, kind="ExternalInput").ap()

    x_d = din("x", [T, D])
    w_in = din("w_in", [D, INW])
    w_pa = din("w_pa", [D, D]); w_pb = din("w_pb", [D, D]); w_o = din("w_o", [D, D])
    w1 = din("w1", [NE, D, DF]); w3 = din("w3", [NE, D, DF]); w2 = din("w2", [NE, DF, D])
    gm_ws = din("gm_w_s", [8, 128, 128])
    rowp = din("rowp", [10, D])
    colp = din("colp", [128, 96])
    ab_b = din("ab_b", [1, 16])
    hp = din("hp", [1, 16])
    nw_col = din("nw_col", [128, 1])
    wr = din("wr", [D, 36])
    br = din("br", [1, 36])
    cst = din("cst", [128, 9, 128])
    cw_d = din("cw", [128, 24, 4])
    selh_d = din("selh", [8, 16, 128])
    ecol_d = din("ecol", [1, 32])
    out_d = nc.dram_tensor("out", [T, D], F32, kind="ExternalOutput").ap()
    h0_s = nc.dram_tensor("h0_s", [T, D], F32).ap()
    xpad = nc.dram_tensor("xpad", [NE * CAP + 1, D], BF16).ap()
    ypad = nc.dram_tensor("ypad", [NE * CAP + 1, D], F32).ap()
    mT_s = nc.dram_tensor("mT_s", [8, 128, T], BF16).ap()
    yb_s = nc.dram_tensor("yb_s", [8, 128, T], BF16).ap()
    b_ybs = [Buf("yb_s%d" % i) for i in range(8)]
    h1_s = nc.dram_tensor("h1_s", [T, D], F32).ap()
    b_h0 = Buf("h0_s"); b_xpad = Buf("xpad"); b_ypad = Buf("ypad"); b_mT = Buf("mT_s"); b_h1s = Buf("h1_s")
    dbg_out = {}
    out_bufs = []

    with contextlib.ExitStack() as st:
        def sb(name, shape, dt=F32, nb=1):
            return Tl(st.enter_context(nc.sbuf_tensor(name, list(shape), dt)), name, nb)

        def ps(name, shape, dt=F32):
            return Tl(st.enter_context(nc.psum_tensor(name, list(shape), dt)), name)

        def dump(name, src_ap, shape, bufs, dt=F32):
            if not dbg:
                return
            d = nc.dram_tensor("dbg_" + name, list(shape), dt, kind="ExternalOutput").ap()
            dbg_out[name] = d
            b = Buf("dbg_" + name)
            out_bufs.append(P.dma("sp", lambda e: e.dma_start(out=d, in_=src_ap), reads=bufs, writes=[b]))

        def MM(o, ob, l, lb, r, rb, start=True, stop=True):
            return P.op("pe", lambda e: e.matmul(o, lhsT=l, rhs=r, start=start, stop=stop),
                        reads=[lb, rb], writes=[ob])

        def TR(o, ob, i, ib, idt):
            return P.op("pe", lambda e: e.transpose(out=o, in_=i, identity=idt), reads=[ib, b_cst], writes=[ob])

        def ACT(o, ob, i, ib, func, bias=None, scale=None, xr=()):
            kw = {}
            if bias is not None:
                kw["bias"] = bias
            if scale is not None:
                kw["scale"] = scale
            return P.op("act", lambda e: e.activation(out=o, in_=i, func=func, **kw),
                        reads=[ib] + list(xr), writes=[ob])

        def CP(eng, o, ob, i, ib):
            if eng == "act":
                return P.op("act", lambda e: e.copy(out=o, in_=i), reads=[ib], writes=[ob])
            return P.op(eng, lambda e: e.tensor_copy(out=o, in_=i), reads=[ib], writes=[ob])

        def TT(eng, o, ob, a, ab, b, bb, op):
            return P.op(eng, lambda e: e.tensor_tensor(out=o, in0=a, in1=b, op=op), reads=[ab, bb], writes=[ob])

        def TS(eng, o, ob, a, ab, s1, s2, op0, op1=None, xr=()):
            if op1 is None:
                return P.op(eng, lambda e: e.tensor_scalar(out=o, in0=a, scalar1=s1, scalar2=None, op0=op0),
                            reads=[ab] + list(xr), writes=[ob])
            return P.op(eng, lambda e: e.tensor_scalar(out=o, in0=a, scalar1=s1, scalar2=s2, op0=op0, op1=op1),
                        reads=[ab] + list(xr), writes=[ob])

        def STT(eng, o, ob, a, ab, s, b, bb, op0, op1, xr=()):
            return P.op(eng, lambda e: e.scalar_tensor_tensor(out=o, in0=a, scalar=s, in1=b, op0=op0, op1=op1),
                        reads=[ab, bb] + list(xr), writes=[ob])

        def RED(o, ob, i, ib, op=ALU.add):
            return P.op("dve", lambda e: e.tensor_reduce(out=o, in_=i, axis=AX.X, op=op), reads=[ib], writes=[ob])

        def LD(o, ob, src, eng="sp", reads=()):
            return P.dma(eng, lambda e: e.dma_start(out=o, in_=src), reads=list(reads), writes=[ob])

        def STO(dst, dstb, i, ib, eng="sp"):
            return P.dma(eng, lambda e: e.dma_start(out=dst, in_=i), reads=[ib], writes=[dstb])


        cstf = sb("cstf", [128, 9, 128]); b_cst = cstf.b
        LD(cstf[:], b_cst, cst)
        IDF, CUMT, SEL0, SEL1, NEGT, NEGL, OFFD, CMASK, USTR = [cstf[:, i, :] for i in range(9)]
        cstb = sb("cstb", [128, 3, 128], BF16)
        LD(cstb[:, 0, :], b_cst, cst[:, 0, :], eng="pool")
        LD(cstb[:, 2, :], b_cst, cst[:, 8, :], eng="pool")
        onesf = sb("onesf", [128, 128])
        P.op("dve", lambda e: e.memset(onesf[:], 1.0), writes=[onesf.b])
        P.op("dve", lambda e: e.tensor_copy(out=cstb[:, 1, :], in_=onesf[:]), reads=[onesf.b, b_cst], writes=[b_cst])
        IDB = cstb[:, 0, :]; ONB = cstb[:, 1, :]; USB = cstb[:, 2, :]
        selh = sb("selh_sb", [8, 16, 128])
        LD(selh[:], selh.b, selh_d)
        cp = sb("cp", [128, 96]); LD(cp[:], cp.b, colp)
        cw = sb("cw_sb", [128, 24, 4]); LD(cw[:], cw.b, cw_d)
        nwc = sb("nwc", [128, 1]); LD(nwc[:], nwc.b, nw_col)
        epsb = sb("epsb", [128, 4])
        P.op("dve", lambda e: e.memset(epsb[:, 0:1], 1e-5), writes=[epsb.b])
        P.op("dve", lambda e: e.memset(epsb[:, 1:2], 1e-6), reads=[epsb.b], writes=[epsb.b])
        P.op("dve", lambda e: e.memset(epsb[:, 2:3], 1.0), reads=[epsb.b], writes=[epsb.b])
        P.op("dve", lambda e: e.memset(epsb[:, 3:4], 0.0), reads=[epsb.b], writes=[epsb.b])
        EPS_LN, EPS_6, ONE_C, ZERO_C = [epsb[:, i:i + 1] for i in range(4)]
        idx = sb("idx", [128, 2, NT], I32); gt1 = sb("gt1", [128, NT]); gt2 = sb("gt2", [128, NT])
        lnS = [(sb("lnst%d" % i, [128, 2, 6]), sb("lnmv%d" % i, [128, 2]), sb("lnr%d" % i, [128, 1]),
                sb("lnb%d" % i, [128, 1])) for i in range(4)]

        pq = [ps("pq%d" % i, [128, 512]) for i in range(8)]
        pqb = []
        for i in range(8):
            v = Tl(pq[i][:].bitcast(BF16), "pqb%d" % i)
            v.b = pq[i].b
            v.bs = [v.b]
            pqb.append(v)
        pctx = {"banks": list(range(8)), "i": 0}
        pstate = {}

        def use_banks(banks, tag=None):
            if tag is not None and tag in pstate:
                pctx.update(pstate[tag])
            else:
                pctx["banks"] = list(banks)
                pctx["i"] = 0
            pctx["tag"] = tag

        def save_banks():
            if pctx.get("tag") is not None:
                pstate[pctx["tag"]] = {"banks": pctx["banks"], "i": pctx["i"]}

        def nextp():
            pctx["i"] = (pctx["i"] + 1) % len(pctx["banks"])
            return pq[pctx["banks"][pctx["i"]]]

        def nextpb():
            pctx["i"] = (pctx["i"] + 1) % len(pctx["banks"])
            return pqb[pctx["banks"][pctx["i"]]]

        def par2(fn, n, ns=2):
            P.begin_parallel()
            nb_ = 8 // ns
            for si in range(ns):
                P.set_stream(si)
                use_banks([nb_ * si + i for i in range(nb_)])
                for t in range(si, n, ns):
                    fn(t, si)
            P.end_parallel()
            use_banks(range(8))

        ARW = 49152
        arena = st.enter_context(nc.sbuf_tensor("arena", [128, ARW], F32))
        loc_off = [0]

        def av(name, off, shape, dt=F32, nb=1):
            n = int(np.prod(shape[1:]))
            nbytes = n * (4 if dt in (F32, I32) else 2)
            assert off % 4 == 0 and off + nbytes <= ARW * 4, (name, off, nbytes)
            ap = arena[0:shape[0], off // 4:(off + nbytes + 3) // 4]
            if dt != F32:
                ap = ap.bitcast(dt)
            if dt == BF16 and nbytes % 4:
                ap = ap[:, 0:n]
            if len(shape) == 3:
                ap = ap.rearrange("p (a b) -> p a b", a=shape[1])
            elif len(shape) == 4:
                ap = ap.rearrange("p (a b c) -> p a b c", a=shape[1], b=shape[2])
            return Tl(ap, name, nb)

        def loc(name, shape, dt=F32, nb=1):
            n = int(np.prod(shape[1:]))
            nbytes = (n * (4 if dt in (F32, I32) else 2) + 3) // 4 * 4
            t_ = av(name, loc_off[0], shape, dt, nb)
            loc_off[0] += nbytes
            return t_

        def new_stage(base_kb):
            P.fence()
            loc_off[0] = base_kb * 1024

        KB = 1024
        hT = av("hT", 0, [128, 8, T], BF16, nb=16)
        ybT = av("ybT", 32 * KB, [128, 8, T], BF16, nb=8)
        yaT = av("yaT", 64 * KB, [128, 8, T], BF16, nb=16)

        def load_rows(rows):
            rp_ = loc("rp", [128, len(rows), D])
            for i, r in enumerate(rows):
                LD(rp_[:, i, :], rp_.b, rowp[r:r + 1, :].partition_broadcast(128))
            return rp_
        (R_LNG, R_LNB, R_BV, R_GMG, R_GMB, R_L1G, R_L1B, R_L2G, R_L2B, R_BS) = range(10)

        def layer_norm(src, srcb, dst, dstb, g_ap, b_ap, gb_buf, tmp, tmpb, c=0):
            lnst, lnmv, lnr, lnb = lnS[c]
            def f(e):
                e.bn_stats(out=lnst[:, 0, :], in_=src[:, 0:512])
                return e.bn_stats(out=lnst[:, 1, :], in_=src[:, 512:1024])
            P.op("dve", f, reads=[srcb], writes=[lnst.b])
            P.op("dve", lambda e: e.bn_aggr(out=lnmv[:], in_=lnst[:]), reads=[lnst.b], writes=[lnmv.b])
            ACT(lnr[:], lnr.b, lnmv[:, 1:2], lnmv.b, AF.Sqrt, bias=EPS_LN, scale=1.0, xr=[epsb.b])
            P.op("dve", lambda e: e.reciprocal(out=lnr[:], in_=lnr[:]), reads=[lnr.b], writes=[lnr.b])
            STT("dve", lnb[:], lnb.b, lnmv[:, 0:1], lnmv.b, -1.0, lnr[:], lnr.b, ALU.mult, ALU.mult)
            ACT(tmp, tmpb, src, srcb, AF.Identity, bias=lnb[:, 0:1], scale=lnr[:, 0:1], xr=[lnb.b, lnr.b])
            TT("dve", tmp, tmpb, tmp, tmpb, g_ap, gb_buf, ALU.mult)
            TT(LN_ADD_ENG, dst, dstb, tmp, tmpb, b_ap, gb_buf, ALU.add)

        loc_off[0] = 96 * KB
        rp = load_rows([R_LNG, R_LNB])
        zt = loc("zt", [128, 8192], BF16)
        P.op("dve", lambda e: e.memset(zt[:], 0.0), writes=[zt.b])
        xz = xpad[0:NE * CAP, :].rearrange("(p a) d -> p (a d)", p=128)
        LD(ypad[NE * CAP:NE * CAP + 1, :], b_ypad, zt[0:1, 0:1024], eng="pool", reads=[zt.b])
        LD(xpad[NE * CAP:NE * CAP + 1, :], b_xpad, zt[0:1, 0:1024], reads=[zt.b])
        for i in range(8):
            P.dma("pool", lambda e, i=i: e.dma_start(out=xz[:, i * 8192:(i + 1) * 8192], in_=zt[:]),
                  reads=[zt.b], writes=[b_xpad])
        xin = [loc("xin%d" % i, [128, D]) for i in range(4)]
        h0t = [loc("h0t%d" % i, [128, D]) for i in range(4)]
        x_v = x_d.rearrange("(t p) d -> t p d", p=128)
        h0_v = h0_s.rearrange("(t p) d -> t p d", p=128)
        b_h0t = [Buf("h0s%d" % i) for i in range(NT)]

        def st1(t, si):
            xi = xin[si]; ho = h0t[si]
            LD(xi[:], xi.b, x_v[t])
            layer_norm(xi[:], xi.b, ho[:], ho.b, rp[:, 0, :], rp[:, 1, :], rp.b, xi[:], xi.b, c=si)
            STO(h0_v[t], b_h0t[t], ho[:], ho.b)
            for half in range(2):
                pt = nextp()
                for j in range(4):
                    kc = half * 4 + j
                    TR(pt[:, j * 128:(j + 1) * 128], pt.b, ho[:, kc * 128:(kc + 1) * 128], ho.b, IDF)
                CP("act", hT[:, half * 4:half * 4 + 4, t * 128:(t + 1) * 128], hT.bs[t],
                   pt[:].rearrange("p (j c) -> p j c", j=4), pt.b)
        par2(st1, NT, 4)
        if dbg:
            dump("hT", hT[:], [128, 8, T], hT.bs, BF16)
        if stop_after <= 1:
            P.emit(final_waits=out_bufs)
            return nc, dbg_out

        def L(x):
            return list(x) if isinstance(x, (list, tuple)) else [x]

        def MMl(o, ob, l, lb, r, rb, start=True, stop=True):
            return P.op("pe", lambda e: e.matmul(o, lhsT=l, rhs=r, start=start, stop=stop),
                        reads=L(lb) + L(rb), writes=L(ob))


        def load_w(dst, src2d, off=0, n=512):
            LD(dst[:, :, off:off + n], dst.b, src2d.rearrange("(kc p) c -> p kc c", p=128), eng="pool")

        def seg_bufs(tl, seg):
            return tl.bs[seg * 4:(seg + 1) * 4]

        new_stage(32)
        beta = loc("beta", [128, NT, 8]); Gsb = loc("Gsb", [128, NT, 8])
        nG = loc("nG", [128, NT, 8]); eG = loc("eG", [128, NT, 8]); ekd = loc("ekd", [128, NT, 8])
        GT = loc("GT", [8, T]); cdt = loc("cdt", [128, NT, 2, 8])
        pre_live_end = loc_off[0]
        wab = loc("wab", [128, 8, 16], BF16)
        LD(wab[:], wab.b, w_in[:, 6144:6160].rearrange("(kc p) c -> p kc c", p=128), eng="pool")
        abb = loc("abb", [128, 16]); LD(abb[:], abb.b, ab_b.partition_broadcast(128))
        hpb = loc("hpb", [128, 16]); LD(hpb[:], hpb.b, hp.partition_broadcast(128))
        absb = loc("absb", [128, NT, 16])
        pt = nextp()
        for t in range(NT):
            for kc in range(8):
                MMl(pt[:, t * 16:(t + 1) * 16], pt.b, hT[:, kc, t * 128:(t + 1) * 128], hT.bs[t], wab[:, kc, :], wab.b,
                    start=(kc == 0), stop=(kc == 7))
        TT("dve", absb[:], absb.b, pt[:, 0:256].rearrange("p (t c) -> p t c", t=NT), pt.b,
           abb[:].unsqueeze(1).broadcast_to([128, NT, 16]), abb.b, ALU.add)
        gsb = loc("gsb", [128, NT, 8]); aexp = loc("aexp", [128, 8])
        glo = loc("glo", [128, NT, 8])
        ACT(beta[:], beta.b, absb[:, :, 8:16], absb.b, AF.Sigmoid)
        TT("dve", gsb[:], gsb.b, absb[:, :, 0:8], absb.b, hpb[:, 8:16].unsqueeze(1).broadcast_to([128, NT, 8]), hpb.b, ALU.add)
        ACT(gsb[:], gsb.b, gsb[:], gsb.b, AF.Exp)
        ACT(gsb[:], gsb.b, gsb[:], gsb.b, AF.Ln, bias=ONE_C, scale=1.0, xr=[epsb.b])
        ACT(aexp[:], aexp.b, hpb[:, 0:8], hpb.b, AF.Exp)
        STT("dve", gsb[:], gsb.b, gsb[:], gsb.b, -1.0, aexp[:].unsqueeze(1).broadcast_to([128, NT, 8]), aexp.b,
            ALU.mult, ALU.mult)
        pt = nextp()
        for t in range(NT):
            MMl(pt[:, t * 8:(t + 1) * 8], pt.b, CUMT, b_cst, gsb[:, t, :], gsb.b)
        CP("dve", Gsb[:], Gsb.b, pt[:, 0:128].rearrange("p (t c) -> p t c", t=NT), pt.b)
        for s4 in range(4):
            pt = nextp()
            for j in range(4):
                t = s4 * 4 + j
                MMl(pt[0:8, j * 128:(j + 1) * 128], pt.b, gsb[:, t, :], gsb.b, CUMT, b_cst)
            CP("dve", GT[:, s4 * 512:(s4 + 1) * 512], GT.b, pt[0:8, :], pt.b)
        pt = nextp()
        for t in range(NT):
            for c in range(2):
                MMl(pt[:, (t * 2 + c) * 8:(t * 2 + c + 1) * 8], pt.b, SEL0 if c == 0 else SEL1, b_cst, Gsb[:, t, :], Gsb.b)
        glf = loc("glf", [128, NT, 2, 8])
        CP("dve", glf[:], glf.b, pt[:, 0:256].rearrange("p (t c h) -> p t c h", t=NT, c=2), pt.b)
        ACT(cdt[:], cdt.b, glf[:], glf.b, AF.Exp)
        CP("dve", glo[0:64, :, :], glo.b, glf[0:64, :, 0, :], glf.b)
        CP("dve", glo[64:128, :, :], glo.b, glf[64:128, :, 1, :], glf.b)
        TT("dve", glo[:], glo.b, glo[:], glo.b, Gsb[:], Gsb.b, ALU.subtract)
        ACT(ekd[:], ekd.b, glo[:], glo.b, AF.Exp)
        ACT(eG[:], eG.b, Gsb[:], Gsb.b, AF.Exp)
        TS("dve", nG[:], nG.b, Gsb[:], Gsb.b, -1.0, None, ALU.mult)
        if dbg:
            dump("Gsb", Gsb[:], [128, NT, 8], [Gsb.b])
            dump("beta", beta[:], [128, NT, 8], [beta.b])
            dump("GT", GT[:], [8, T], [GT.b])

        P.fence()
        loc_off[0] = pre_live_end
        from types import SimpleNamespace as NS

        def gdn_alloc(sx):
            X = NS()
            X.wb = loc("wvh" + sx, [128, 8, 512], BF16)
            X.pr = loc("pre" + sx, [128, 3, 516], BF16)
            X.dg = loc("dg" + sx, [128, 12, 128], BF16)
            X.qkvT = [loc("qkvT%d" % i + sx, [128, 3, 512], BF16) for i in range(2)]
            X.szT = [loc("szT%d" % i + sx, [128, 512], BF16) for i in range(3)]
            X.tm = loc("tm" + sx, [128, 3, 4, 128], BF16, nb=3)
            X.sq = loc("sq" + sx, [128, 4, 128]); X.ssq = loc("ssq" + sx, [128, 2, 4]); X.rn = loc("rn" + sx, [128, 2, 4])
            X.scl = loc("scl" + sx, [128, 7, 4], nb=7)
            X.scd = [loc("scd%d" % i + sx, [128, 7, 4, 128], BF16, nb=7) for i in range(2)]
            X.fmT = [loc("fmT%d" % i + sx, [128, 4, 512], BF16, nb=4) for i in range(2)]
            X.DTm = loc("DTm" + sx, [128, 4, 128], nb=4); X.DLm = loc("DLm" + sx, [128, 4, 128], nb=4)
            X.SN = loc("SN" + sx, [128, 4, 128]); X.SL = loc("SL" + sx, [128, 4, 128]); X.Pm = loc("Pm" + sx, [128, 4, 128])
            X.ATb = [loc("ATb%d" % i + sx, [128, 4, 128], BF16) for i in range(2)]
            X.TTb = [loc("TTb%d" % i + sx, [128, 4, 128], BF16) for i in range(2)]
            X.wcT = [loc("wcT%d" % i + sx, [128, 512], BF16) for i in range(2)]
            X.osq = loc("osq" + sx, [128, 4, 128])
            X.S32 = loc("S32" + sx, [128, 128]); X.Sbf = loc("Sbf" + sx, [128, 128], BF16)
            X.vnb = loc("vnb" + sx, [128, 4, 128], BF16); X.osb = loc("osb" + sx, [128, 4, 128])
            X.onb = loc("onb" + sx, [128, 4, 128], BF16); X.orr = loc("orr" + sx, [128, 4])
            X.ybo = [loc("ybo%d" % i + sx, [128, 512], BF16) for i in range(1)]
            return X
        GX = [gdn_alloc("_a"), gdn_alloc("_b")]

        def gdn_proj(h, X, seg):
            wb = X.wb; pr = X.pr; dg = X.dg
            qkvT = X.qkvT[seg % 2]; szT = X.szT[seg % 3]
            sl = slice(seg * 512, (seg + 1) * 512)
            for j in range(4):
                pt = nextp()
                for kc in range(8):
                    MMl(pt[:], pt.b, wb[:, kc, j * 128:(j + 1) * 128], wb.b, hT[:, kc, sl], seg_bufs(hT, seg),
                        start=(kc == 0), stop=(kc == 7))
                bcol = cp[:, 16 + j * 8 + h:16 + j * 8 + h + 1]
                if j < 3:
                    ACT(pr[:, j, 3:515], pr.b, pt[:], pt.b, AF.Identity, bias=bcol, scale=1.0, xr=[cp.b])
                else:
                    ACT(szT[:], szT.b, pt[:], pt.b, AF.Silu, bias=bcol, scale=1.0, xr=[cp.b])
            for j in range(3):
                pt = nextp()
                for k in range(4):
                    MMl(pt[:], pt.b, dg[:, j * 4 + k, :], dg.b, pr[:, j, k:k + 512], pr.b, start=(k == 0), stop=(k == 3))
                ACT(qkvT[:, j, :], qkvT.b, pt[:], pt.b, AF.Silu)
            CP("act", pr[:, :, 0:3], pr.b, pr[:, :, 512:515], pr.b)

        def gdn_decay(h, X, seg):
            DTm = X.DTm; DLm = X.DLm
            pN = nextp(); pL = nextp()
            for i in range(4):
                t = seg * 4 + i
                cs = slice(i * 128, (i + 1) * 128)
                gts = GT[:, t * 128:(t + 1) * 128]
                MMl(pN[:, cs], pN.b, selh[:, h, :], selh.b, gts, GT.b, start=True, stop=False)
                MMl(pN[:, cs], pN.b, IDF, b_cst, NEGT, b_cst, start=False, stop=True)
                MMl(pL[:, cs], pL.b, selh[:, 8 + h, :], selh.b, gts, GT.b, start=True, stop=False)
                MMl(pL[:, cs], pL.b, IDF, b_cst, NEGL, b_cst, start=False, stop=True)
            for i in range(4):
                t = seg * 4 + i
                cs = slice(i * 128, (i + 1) * 128)
                ACT(DTm[:, i, :], DTm.bs[i], pN[:, cs], pN.b, AF.Exp, bias=nG[:, t, h:h + 1], scale=1.0, xr=[nG.b])
                ACT(DLm[:, i, :], DLm.bs[i], pL[:, cs], pL.b, AF.Exp, bias=Gsb[:, t, h:h + 1], scale=1.0, xr=[Gsb.b])

        def gdn_init(h, X):
            wb = X.wb; pr = X.pr; dg = X.dg; tm = X.tm; sq = X.sq; ssq = X.ssq
            rn = X.rn; scl = X.scl; DTm = X.DTm; DLm = X.DLm; SN = X.SN; SL = X.SL
            Pm = X.Pm; S32 = X.S32; Sbf = X.Sbf; vnb = X.vnb; osb = X.osb
            onb = X.onb; orr = X.orr; osq = X.osq
            for j in range(4):
                c0 = 2048 + j * 1024 + h * 128
                load_w(wb, w_in[:, c0:c0 + 128], off=j * 128, n=128)
            P.op("dve", lambda e: e.memset(S32[:], 0.0), writes=[S32.b])
            P.op("dve", lambda e: e.memset(Sbf[:], 0.0), writes=[Sbf.b])
            P.op("dve", lambda e: e.memset(pr[:, :, 0:3], 0.0), writes=[pr.b])
            for j in range(3):
                for k in range(4):
                    ACT(dg[:, j * 4 + k, :], dg.b, IDB, b_cst, AF.Copy, scale=cw[:, j * 8 + h, k:k + 1], xr=[cw.b])
            gdn_proj(h, X, 0)
            gdn_decay(h, X, 0)

        def gdn_prep(h, X, seg):
            wb = X.wb; pr = X.pr; dg = X.dg; tm = X.tm; sq = X.sq; ssq = X.ssq
            rn = X.rn; scl = X.scl; DTm = X.DTm; DLm = X.DLm; SN = X.SN; SL = X.SL
            Pm = X.Pm; S32 = X.S32; Sbf = X.Sbf; vnb = X.vnb; osb = X.osb
            onb = X.onb; orr = X.orr; osq = X.osq
            sp_ = seg % 2
            szT = X.szT[seg % 3]; qkvT = X.qkvT[sp_]; scd = X.scd[sp_]; fmT = X.fmT[sp_]; ATb = X.ATb[sp_]; TTb = X.TTb[sp_]; wcT = X.wcT[sp_]
            sl = slice(seg * 512, (seg + 1) * 512)
            KNT, KBT, QNT, QDT = [fmT[:, m, :] for m in range(4)]
            for j in range(3):
                pbt = nextpb()
                for i in range(4):
                    TR(pbt[:, i * 128:(i + 1) * 128], pbt.b, qkvT[:, j, i * 128:(i + 1) * 128], qkvT.b, IDB)
                CP("act" if j == 1 else "dve", tm[:, j, :, :], tm.bs[j],
                   pbt[:, 0:512].rearrange("p (i c) -> p i c", i=4), pbt.b)
            for j in range(2):
                TT("dve", sq[:], sq.b, tm[:, j, :, :], tm.bs[j], tm[:, j, :, :], tm.bs[j], ALU.mult)
                RED(ssq[:, j, :], ssq.b, sq[:], sq.b)
            ACT(rn[:], rn.b, ssq[:], ssq.b, AF.Sqrt, bias=EPS_6, scale=1.0, xr=[epsb.b])
            P.op("dve", lambda e: e.reciprocal(out=rn[:], in_=rn[:]), reads=[rn.b], writes=[rn.b])
            t4 = slice(seg * 4, seg * 4 + 4)
            bt = beta[:, t4, h]; eg = eG[:, t4, h]; ek = ekd[:, t4, h]
            RQ = rn[:, 0, :]; RK = rn[:, 1, :]
            CP("dve", scl[:, 0, :], scl.bs[0], RK, rn.b)
            TT("dve", scl[:, 1, :], scl.bs[1], RK, rn.b, bt, beta.b, ALU.mult)
            TS("dve", scl[:, 2, :], scl.bs[2], RQ, rn.b, 128 ** -0.5, None, ALU.mult)
            TT("dve", scl[:, 3, :], scl.bs[3], scl[:, 2, :], scl.bs[2], eg, eG.b, ALU.mult)
            TT("dve", scl[:, 4, :], scl.bs[4], scl[:, 1, :], scl.bs[1], eg, eG.b, ALU.mult)
            TT("dve", scl[:, 5, :], scl.bs[5], RK, rn.b, ek, ekd.b, ALU.mult)
            CP("dve", scl[:, 6, :], scl.bs[6], bt, beta.b)
            srcs = [1, 1, 0, 0, 1, 1, 2]
            for m in range(7):
                TT("dve", scd[:, m, :, :], scd.bs[m], tm[:, srcs[m], :, :], tm.bs[srcs[m]],
                   scl[:, m, :].unsqueeze(2).broadcast_to([128, 4, 128]), scl.bs[m], ALU.mult)
            for m in range(4):
                pbt = nextpb()
                for i in range(4):
                    TR(pbt[:, i * 128:(i + 1) * 128], pbt.b, scd[:, m, i, :], scd.bs[m], IDB)
                CP("act" if m % 2 else "dve", fmT[:, m, :], fmT.bs[m], pbt[:, 0:512], pbt.b)
            pkn = nextp(); pkl = nextp(); pat = nextp()
            for i in range(4):
                cs = slice(i * 128, (i + 1) * 128)
                MMl(pkn[:, cs], pkn.b, KNT[:, cs], fmT.bs[0], KBT[:, cs], fmT.bs[1])
                MMl(pkl[:, cs], pkl.b, KBT[:, cs], fmT.bs[1], KNT[:, cs], fmT.bs[0])
                MMl(pat[:, cs], pat.b, KNT[:, cs], fmT.bs[0], QNT[:, cs], fmT.bs[2])
            v4 = lambda ap: ap.rearrange("p (i c) -> p i c", i=4)
            TT("dve", ATb[:], ATb.b, v4(pat[:]), pat.b, DTm[:], DTm.bs, ALU.mult)
            offb = OFFD.unsqueeze(1).broadcast_to([128, 4, 128])
            TT("dve", DTm[:], DTm.bs, DTm[:], DTm.bs, offb, b_cst, ALU.mult)
            TT("dve", DLm[:], DLm.bs, DLm[:], DLm.bs, offb, b_cst, ALU.mult)
            TT("dve", SN[:], SN.b, v4(pkn[:]), pkn.b, DTm[:], DTm.bs, ALU.mult)
            TT("dve", SL[:], SL.b, v4(pkl[:]), pkl.b, DLm[:], DLm.bs, ALU.mult)
            if seg + 1 < 4:
                gdn_decay(h, X, seg + 1)
                gdn_proj(h, X, seg + 1)
            idb4 = IDF.unsqueeze(1).broadcast_to([128, 4, 128])
            TT("dve", Pm[:], Pm.b, idb4, b_cst, SN[:], SN.b, ALU.subtract)
            for it in range(5):
                p1 = nextp(); p2 = nextp()
                for i in range(4):
                    cs = slice(i * 128, (i + 1) * 128)
                    if it < 4:
                        MMl(p1[:, cs], p1.b, SL[:, i, :], SL.b, SN[:, i, :], SN.b)
                    MMl(p2[:, cs], p2.b, SN[:, i, :], SN.b, SL[:, i, :], SL.b)
                if it < 4:
                    CP("act", SN[:], SN.b, v4(p1[:]), p1.b)
                CP("dve", SL[:], SL.b, v4(p2[:]), p2.b)
                p3 = nextp()
                for i in range(4):
                    cs = slice(i * 128, (i + 1) * 128)
                    MMl(p3[:, cs], p3.b, SL[:, i, :], SL.b, Pm[:, i, :], Pm.b)
                TT("dve", Pm[:], Pm.b, v4(p3[:]), p3.b, Pm[:], Pm.b, ALU.add)
            CP("act", TTb[:], TTb.b, Pm[:], Pm.b)
            if dbg and h == 0 and seg == 0:
                dump("TT", Pm[:], [128, 4, 128], [Pm.b])
                dump("AT", ATb[:], [128, 4, 128], [ATb.b], BF16)
            pw = nextp()
            for i in range(4):
                cs = slice(i * 128, (i + 1) * 128)
                MMl(pw[:, cs], pw.b, scd[:, 4, i, :], scd.bs[4], TTb[:, i, :], TTb.b)
            TS("dve", wcT[:], wcT.b, pw[:], pw.b, -1.0, None, ALU.mult)

        def gdn_rec(h, X, seg):
            wb = X.wb; pr = X.pr; dg = X.dg; tm = X.tm; sq = X.sq; ssq = X.ssq
            rn = X.rn; scl = X.scl; DTm = X.DTm; DLm = X.DLm; SN = X.SN; SL = X.SL
            Pm = X.Pm; S32 = X.S32; Sbf = X.Sbf; vnb = X.vnb; osb = X.osb
            onb = X.onb; orr = X.orr; osq = X.osq
            sp_ = seg % 2
            szT = X.szT[seg % 3]; qkvT = X.qkvT[sp_]; scd = X.scd[sp_]; fmT = X.fmT[sp_]; ATb = X.ATb[sp_]; TTb = X.TTb[sp_]; wcT = X.wcT[sp_]
            sl = slice(seg * 512, (seg + 1) * 512)
            KNT, KBT, QNT, QDT = [fmT[:, m, :] for m in range(4)]
            def nextq():
                X.qi = (X.qi + 1) % 4
                return X.recq[X.qi]
            for c in range(8):
                i = c // 2; p0 = 64 * (c % 2); ps_ = slice(p0, p0 + 64)
                cs = slice(i * 128, (i + 1) * 128)
                t = seg * 4 + i
                pv = nextq()
                MMl(pv[:, 0:128], pv.b, TTb[ps_, i, :], TTb.b, scd[ps_, 6, i, :], scd.bs[6], start=True, stop=False)
                MMl(pv[:, 0:128], pv.b, wcT[:, cs], wcT.b, Sbf[:], Sbf.b, start=False, stop=True)
                CP("act", vnb[ps_, i, :], vnb.b, pv[ps_, 0:128], pv.b)
                po = nextq()
                MMl(po[:, 0:128], po.b, QDT[:, cs], fmT.bs[3], Sbf[:], Sbf.b, start=True, stop=False)
                MMl(po[:, 0:128], po.b, ATb[ps_, i, :], ATb.b, vnb[ps_, i, :], vnb.b, start=False, stop=True)
                CP("act", osb[ps_, i, :], osb.b, po[ps_, 0:128], po.b)
                pS = nextq()
                MMl(pS[:, 0:128], pS.b, scd[ps_, 5, i, :], scd.bs[5], vnb[ps_, i, :], vnb.b)
                STT("dve", Sbf[:], Sbf.b, S32[:], S32.b, cdt[:, t, c % 2, h:h + 1], pS[:, 0:128], pS.b,
                    ALU.mult, ALU.add, xr=[cdt.b])
                STT("dve", S32[:], S32.b, S32[:], S32.b, cdt[:, t, c % 2, h:h + 1], pS[:, 0:128], pS.b,
                    ALU.mult, ALU.add, xr=[cdt.b])
            TT("dve", osq[:], osq.b, osb[:], osb.b, osb[:], osb.b, ALU.mult)
            RED(orr[:], orr.b, osq[:], osq.b)
            ACT(orr[:], orr.b, orr[:], orr.b, AF.Sqrt, bias=EPS_6, scale=1.0 / 128.0, xr=[epsb.b])
            P.op("dve", lambda e: e.reciprocal(out=orr[:], in_=orr[:]), reads=[orr.b], writes=[orr.b])
            TT("dve", onb[:], onb.b, osb[:], osb.b, orr[:].unsqueeze(2).broadcast_to([128, 4, 128]), orr.b, ALU.mult)
            pbt = X.recb
            for i in range(4):
                TR(pbt[:, i * 128:(i + 1) * 128], pbt.bs, onb[:, i, :], onb.b, IDB)
            yo_ = X.ybo[0]
            STT("dve", yo_[:], yo_.b, pbt[:, 0:512], pbt.bs, nwc[:, 0:1], szT[:], szT.b,
                ALU.mult, ALU.mult, xr=[nwc.b])
            STO(yb_s[h][:, sl], b_ybs[h], yo_[:], yo_.b)


        GW = 1000.0
        for hp in range(NH // 2):
            P.begin_parallel()
            for si in range(2):
                h = 2 * hp + si
                X = GX[si]
                rb = 4 * si + 3
                X.recq = []
                for q in range(4):
                    v = Tl(pq[rb][:, q * 128:(q + 1) * 128], "recq")
                    v.b = pq[rb].b
                    v.bs = [v.b]
                    X.recq.append(v)
                X.recb = Tl(pqb[rb][:, 0:512], "recb")
                X.recb.b = pq[rb].b
                X.recb.bs = [pq[rb].b]
                X.qi = 0
                sp, sr = 2 * si, 2 * si + 1
                pctx["banks"] = [4 * si, 4 * si + 1, 4 * si + 2]; pctx["i"] = 0
                P.set_stream(sp); P.cnt[sp] = -200.0
                gdn_init(h, X)
                P.step[sr] = REC_STEP
                for seg in range(4):
                    P.set_stream(sp); P.cnt[sp] = seg * GW
                    gdn_prep(h, X, seg)
                    P.set_stream(sr); P.cnt[sr] = (seg + 1) * GW + REC_OFF
                    gdn_rec(h, X, seg)
            P.end_parallel()
            use_banks(range(8))
        new_stage(96)
        rp = load_rows([R_BV, R_GMG, R_GMB, R_BS])
        wv = [loc("wv%d" % i, [128, 8, 512], BF16) for i in range(4)]
        wsf = loc("wsf", [128, 8, 128]); wmT = loc("wmT", [128, 8, 128], BF16)
        LD(wsf[:], wsf.b, gm_ws.rearrange("g t s -> t g s"))
        TT("dve", wsf[:], wsf.b, wsf[:], wsf.b, CMASK.unsqueeze(1).broadcast_to([128, 8, 128]), b_cst, ALU.mult)
        for hf in range(2):
            pt = nextp()
            for j in range(4):
                TR(pt[:, j * 128:(j + 1) * 128], pt.b, wsf[:, hf * 4 + j, :], wsf.b, IDF)
            CP("act", wmT[:, hf * 4:hf * 4 + 4, :], wmT.b, pt[:].rearrange("p (j c) -> p j c", j=4), pt.b)
        for i in range(4):
            load_w(wv[i], w_in[:, i * 512:(i + 1) * 512])
        for cc in range(8):
            for seg in range(4):
                pt = nextp()
                for kc in range(8):
                    MMl(pt[:], pt.b, wv[cc // 4][:, kc, (cc % 4) * 128:(cc % 4 + 1) * 128], wv[cc // 4].b,
                        hT[:, kc, seg * 512:(seg + 1) * 512], seg_bufs(hT, seg), start=(kc == 0), stop=(kc == 7))
                P.op("act", lambda e, pt=pt, cc=cc, seg=seg: e.activation(
                    out=yaT[:, cc, seg * 512:(seg + 1) * 512], in_=pt[:], func=AF.Gelu, bias=cp[:, cc:cc + 1], scale=1.0),
                    reads=[pt.b, cp.b], writes=seg_bufs(yaT, seg))
        vpreS = [loc("vpre%d" % i, [128, D]) for i in range(4)]
        vlnS = [loc("vln%d" % i, [128, D]) for i in range(4)]
        vbtS = [loc("vbt%d" % i, [128, D], BF16) for i in range(4)]

        def gm_v(t, si):
            vpre = vpreS[si]; vln = vlnS[si]; vbt = vbtS[si]
            for hf in range(2):
                pt = nextp()
                for kc in range(8):
                    MMl(pt[:], pt.b, hT[:, kc, t * 128:(t + 1) * 128], hT.bs[t], wv[2 + hf][:, kc, :], wv[2 + hf].b,
                        start=(kc == 0), stop=(kc == 7))
                TT("dve", vpre[:, hf * 512:(hf + 1) * 512], vpre.b, pt[:], pt.b,
                   rp[:, 0, hf * 512:(hf + 1) * 512], rp.b, ALU.add)
            ACT(vpre[:], vpre.b, vpre[:], vpre.b, AF.Gelu)
            layer_norm(vpre[:], vpre.b, vln[:], vln.b, rp[:, 1, :], rp[:, 2, :], rp.b, vln[:], vln.b, c=si)
            CP("act", vbt[:], vbt.b, vln[:], vln.b)
            for g0 in (0, 4):
                pt = nextp()
                for j in range(4):
                    g = g0 + j
                    MMl(pt[:, j * 128:(j + 1) * 128], pt.b, vbt[:, g * 128:(g + 1) * 128], vbt.b, wmT[:, g, :], wmT.b,
                        start=True, stop=True)
                tmpy = vpre[:, g0 * 128:(g0 + 4) * 128]
                TT("dve", tmpy, vpre.b, pt[:], pt.b, rp[:, 3, g0 * 128:(g0 + 4) * 128], rp.b, ALU.add)
                TT("dve", yaT[:, g0:g0 + 4, t * 128:(t + 1) * 128], yaT.bs[t],
                   tmpy.rearrange("p (j c) -> p j c", j=4), vpre.b,
                   yaT[:, g0:g0 + 4, t * 128:(t + 1) * 128], yaT.bs[t], ALU.mult)
        par2(gm_v, NT, 4)
        if dbg:
            dump("yaT", yaT[:], [128, 8, T], yaT.bs, BF16)
        if stop_after <= 3:
            P.emit(final_waits=out_bufs)
            return nc, dbg_out


        new_stage(96)
        wv = [loc("wm%d" % i, [128, 8, 512], BF16) for i in range(3)]
        LD(ybT[:], ybT.b, yb_s.rearrange("c p t -> p c t"), reads=b_ybs)
        gat = [loc("gat%d" % i, [128, 2, 512]) for i in range(2)]
        t1 = loc("t1", [128, 512]); t2 = loc("t2", [128, 512])
        mo = [loc("mo%d" % i, [128, 512], BF16) for i in range(2)]
        for dc in range(8):
            wb = wv[dc % 3]
            load_w(wb, w_pa[:, dc * 128:(dc + 1) * 128], off=0, n=128)
            load_w(wb, w_pb[:, dc * 128:(dc + 1) * 128], off=128, n=128)
            load_w(wb, w_in[:, 6160 + dc * 128:6160 + (dc + 1) * 128], off=256, n=128)
            load_w(wb, w_in[:, 7184 + dc * 128:7184 + (dc + 1) * 128], off=384, n=128)
            for seg in range(4):
                sl = slice(seg * 512, (seg + 1) * 512)
                srcs = [(yaT, seg_bufs(yaT, seg)), (ybT, [ybT.b]), (hT, seg_bufs(hT, seg)), (hT, seg_bufs(hT, seg))]
                pp = []
                for j in range(4):
                    pt = nextp(); pp.append(pt)
                    for kc in range(8):
                        MMl(pt[:], pt.b, wb[:, kc, j * 128:(j + 1) * 128], wb.b, srcs[j][0][:, kc, sl], srcs[j][1],
                            start=(kc == 0), stop=(kc == 7))
                g = gat[seg % 2]; m_ = mo[seg % 2]
                ACT(g[:, 0, :], g.b, pp[2][:], pp[2].b, AF.Sigmoid, bias=cp[:, 48 + dc:49 + dc], scale=1.0, xr=[cp.b])
                ACT(g[:, 1, :], g.b, pp[3][:], pp[3].b, AF.Sigmoid, bias=cp[:, 56 + dc:57 + dc], scale=1.0, xr=[cp.b])
                TT("dve", t1[:], t1.b, pp[0][:], pp[0].b, g[:, 0, :], g.b, ALU.mult)
                TT("dve", t2[:], t2.b, pp[1][:], pp[1].b, g[:, 1, :], g.b, ALU.mult)
                TT("dve", m_[:], m_.b, t1[:], t1.b, t2[:], t2.b, ALU.add)
                STO(mT_s[dc][:, sl], b_mT, m_[:], m_.b)

        new_stage(96)
        h1 = av("h1", 0, [128, NT, D], F32, nb=16)
        mT = av("mT", 64 * KB, [128, 8, T], BF16)
        LD(mT[:], mT.b, mT_s.rearrange("c p t -> p c t"), reads=[b_mT])
        rp = load_rows([R_L1G, R_L1B])
        wo = [loc("wo%d" % i, [128, 8, 512], BF16) for i in range(2)]
        for i in range(2):
            load_w(wo[i], w_o[:, i * 512:(i + 1) * 512])
        h0r = [loc("h0r%d" % i, [128, D]) for i in range(4)]
        r1S = [loc("r1%d" % i, [128, D]) for i in range(4)]
        b_h1t = [Buf("h1s%d" % i) for i in range(NT)]
        h1_v = h1_s.rearrange("(t p) d -> t p d", p=128)

        def st4b(t, si):
            hr = h0r[si]; r1 = r1S[si]
            LD(hr[:], hr.b, h0_v[t], reads=[b_h0t[t]])
            for hf in range(2):
                pt = nextp()
                for kc in range(8):
                    MMl(pt[:], pt.b, mT[:, kc, t * 128:(t + 1) * 128], mT.b, wo[hf][:, kc, :], wo[hf].b,
                        start=(kc == 0), stop=(kc == 7))
                STT("dve", r1[:, hf * 512:(hf + 1) * 512], r1.b, hr[:, hf * 512:(hf + 1) * 512], hr.b, ALPHA,
                    pt[:], pt.b, ALU.mult, ALU.add)
            layer_norm(r1[:], r1.b, h1[:, t, :], h1.bs[t], rp[:, 0, :], rp[:, 1, :], rp.b, r1[:], r1.b, c=si)
            STO(h1_v[t], b_h1t[t], h1[:, t, :], h1.bs[t])
        par2(st4b, NT, 4)
        if dbg:
            dump("h1", h1[:], [128, NT, D], h1.bs)
        if stop_after <= 4:
            P.emit(final_waits=out_bufs)
            return nc, dbg_out

        new_stage(96)
        PF = {}
        pf_off = 140 * KB
        for si_ in range(2):
            for nm_, shp_, src_ in (("wA", [128, 8, DF], w1[si_].rearrange("(kc p) f -> p kc f", p=128)),
                                    ("wB", [128, 8, DF], w3[si_].rearrange("(kc p) f -> p kc f", p=128)),
                                    ("wC", [128, 4, D], w2[si_].rearrange("(fc p) d -> p fc d", p=128))):
                t_ = av("pf_%s%d" % (nm_, si_), pf_off, shp_, BF16)
                pf_off += 8 * KB
                LD(t_[:], t_.b, src_, eng="pool")
                PF[(si_, nm_)] = t_
        wrs = loc("wrs", [128, 8, 36]); LD(wrs[:], wrs.b, wr.rearrange("(kc p) c -> p kc c", p=128))
        brb = loc("brb", [128, 36]); LD(brb[:], brb.b, br.partition_broadcast(128))
        ecb = loc("ecb", [128, 32]); LD(ecb[:], ecb.b, ecol_d.partition_broadcast(128))
        lg = loc("lg", [128, NT, 36], nb=NT)
        h1T = [loc("h1T%d" % i, [128, 8, 128]) for i in range(4)]

        def st5a(t, si):
            hT_ = h1T[si]
            for half in range(2):
                pt = nextp()
                for j in range(4):
                    kc = half * 4 + j
                    TR(pt[:, j * 128:(j + 1) * 128], pt.b, h1[:, t, kc * 128:(kc + 1) * 128], h1.bs[t], IDF)
                CP("act", hT_[:, half * 4:half * 4 + 4, :], hT_.b, pt[:].rearrange("p (j c) -> p j c", j=4), pt.b)
            pl = nextp()
            for kc in range(8):
                MMl(pl[:, 0:36], pl.b, hT_[:, kc, :], hT_.b, wrs[:, kc, :], wrs.b, start=(kc == 0), stop=(kc == 7))
            TT("dve", lg[:, t, :], lg.bs[t], pl[:, 0:36], pl.b, brb[:], brb.b, ALU.add)
        par2(st5a, NT, 4)
        mg = loc("mg", [128, NT]); gh = loc("gh", [128, NT, 4]); egx = loc("egx", [128, NT, 4])
        sg = loc("sg", [128, NT]); pgrp = loc("pgrp", [128, NT])
        t48 = loc("t48", [128, NT, 4, 8]); les = loc("les", [128, NT, 8]); les2 = loc("les2", [128, NT, 8])
        m1 = loc("m1", [128, NT]); m2 = loc("m2", [128, NT]); oh1 = loc("oh1", [128, NT, 8]); oh2 = loc("oh2", [128, NT, 8])
        A1 = loc("A1", [128, NT, 4, 8]); A2 = loc("A2", [128, NT, 4, 8]); Ab = loc("Ab", [128, NT, 32], BF16)
        pos = loc("pos", [128, NT, 32]); pe_ = loc("pe_", [128, NT, 32]); t32 = loc("t32", [128, NT, 32])
        slf = loc("slf", [128, 2, NT]); pv_ = loc("pv_", [128, 2, NT]); val = loc("val", [128, 2, NT])
        lgg = lg[:, :, 0:4]
        le4 = lg[:, :, 4:36].rearrange("p t (g e) -> p t g e", g=4)

        def bc(ap, shape, ax):
            return ap.unsqueeze(ax).broadcast_to(shape)
        P.op("dve", lambda e: e.tensor_reduce(out=mg[:], in_=lgg, axis=AX.X, op=ALU.max), reads=[lg.bs], writes=[mg.b])
        TT("dve", gh[:], gh.b, lgg, lg.bs, bc(mg[:], [128, NT, 4], 2), mg.b, ALU.is_equal)
        TT("dve", egx[:], egx.b, lgg, lg.bs, bc(mg[:], [128, NT, 4], 2), mg.b, ALU.subtract)
        ACT(egx[:], egx.b, egx[:], egx.b, AF.Exp)
        RED(sg[:], sg.b, egx[:], egx.b)
        P.op("dve", lambda e: e.reciprocal(out=pgrp[:], in_=sg[:]), reads=[sg.b], writes=[pgrp.b])
        TT("dve", t48[:], t48.b, le4, lg.bs, bc(gh[:], [128, NT, 4, 8], 3), gh.b, ALU.mult)
        RED(les[:], les.b, t48[:].rearrange("p t g e -> p t e g"), t48.b)
        P.op("dve", lambda e: e.tensor_reduce(out=m1[:], in_=les[:], axis=AX.X, op=ALU.max), reads=[les.b], writes=[m1.b])
        TT("dve", oh1[:], oh1.b, les[:], les.b, bc(m1[:], [128, NT, 8], 2), m1.b, ALU.is_equal)
        STT("dve", les2[:], les2.b, oh1[:], oh1.b, -1e30, les[:], les.b, ALU.mult, ALU.add)
        P.op("dve", lambda e: e.tensor_reduce(out=m2[:], in_=les2[:], axis=AX.X, op=ALU.max), reads=[les2.b], writes=[m2.b])
        TT("dve", oh2[:], oh2.b, les2[:], les2.b, bc(m2[:], [128, NT, 8], 2), m2.b, ALU.is_equal)
        TT("dve", m2[:], m2.b, m1[:], m1.b, m2[:], m2.b, ALU.subtract)
        ACT(m2[:], m2.b, m2[:], m2.b, AF.Sigmoid)
        TT("dve", gt1[:], gt1.b, pgrp[:], pgrp.b, m2[:], m2.b, ALU.mult)
        TT("dve", gt2[:], gt2.b, pgrp[:], pgrp.b, gt1[:], gt1.b, ALU.subtract)
        TT("dve", A1[:], A1.b, bc(gh[:], [128, NT, 4, 8], 3), gh.b, bc(oh1[:], [128, NT, 4, 8], 2), oh1.b, ALU.mult)
        TT("dve", A2[:], A2.b, bc(gh[:], [128, NT, 4, 8], 3), gh.b, bc(oh2[:], [128, NT, 4, 8], 2), oh2.b, ALU.mult)
        A1f = A1[:].rearrange("p t g e -> p t (g e)"); A2f = A2[:].rearrange("p t g e -> p t (g e)")
        TT("dve", Ab[:], Ab.b, A1f, A1.b, A2f, A2.b, ALU.add)
        ppos = nextp()
        for t in range(NT):
            for tp in range(t):
                MMl(ppos[:, t * 32:(t + 1) * 32], ppos.b, ONB, b_cst, Ab[:, tp, :], Ab.b, start=(tp == 0), stop=False)
            MMl(ppos[:, t * 32:(t + 1) * 32], ppos.b, USB, b_cst, Ab[:, t, :], Ab.b, start=(t == 0), stop=True)
        CP("dve", pos[:], pos.b, ppos[:].rearrange("p (t e) -> p t e", t=NT), ppos.b)
        TT("dve", pe_[:], pe_.b, pos[:], pos.b, bc(ecb[:], [128, NT, 32], 1), ecb.b, ALU.add)
        for k, (Af, Ab_) in enumerate(((A1f, A1.b), (A2f, A2.b))):
            TT("dve", t32[:], t32.b, pe_[:], pe_.b, Af, Ab_, ALU.mult)
            RED(slf[:, k, :], slf.b, t32[:], t32.b)
            TT("dve", t32[:], t32.b, pos[:], pos.b, Af, Ab_, ALU.mult)
            RED(pv_[:, k, :], pv_.b, t32[:], t32.b)
        TS("dve", val[:], val.b, pv_[:], pv_.b, float(CAP) - 0.5, None, ALU.is_lt)
        TT("dve", slf[:], slf.b, slf[:], slf.b, val[:], val.b, ALU.mult)
        TS("dve", pv_[:], pv_.b, val[:], val.b, -float(NE * CAP), float(NE * CAP), ALU.mult, ALU.add)
        TT("dve", slf[:], slf.b, slf[:], slf.b, pv_[:], pv_.b, ALU.add)
        CP("dve", idx[:], idx.b, slf[:], slf.b)
        TT("dve", gt1[:], gt1.b, gt1[:], gt1.b, val[:, 0, :], val.b, ALU.mult)
        TT("dve", gt2[:], gt2.b, gt2[:], gt2.b, val[:, 1, :], val.b, ALU.mult)
        if dbg:
            dump("idx", idx[:], [128, 2, NT], [idx.b], I32)
            dump("gt1", gt1[:], [128, NT], [gt1.b])
            dump("gt2", gt2[:], [128, NT], [gt2.b])
        xbt = [loc("xbt%d" % i, [128, D], BF16) for i in range(2)]
        assert loc_off[0] <= 140 * KB, loc_off[0]
        P.fence()
        for t in range(NT):
            xb_ = xbt[t % 2]
            CP("act" if t % 2 else "dve", xb_[:], xb_.b, h1[:, t, :], h1.bs[t])
            for k in range(2):
                P.dma("pool", lambda e, t=t, k=k, xb_=xb_: e.indirect_dma_start(
                    out=xpad, out_offset=bass.IndirectOffsetOnAxis(ap=idx[:, k, t:t + 1], axis=0),
                    in_=xb_[:], in_offset=None, oob_is_err=False),
                    reads=[xb_.b, idx.b], writes=[b_xpad])

        new_stage(0)
        def ex_alloc(sx, si_):
            X = NS()
            X.wA = [PF[(si_, "wA")], loc("wA1" + sx, [128, 8, DF], BF16)]
            X.wB = [PF[(si_, "wB")], loc("wB1" + sx, [128, 8, DF], BF16)]
            X.wC = [PF[(si_, "wC")], loc("wC1" + sx, [128, 4, D], BF16)]
            X.xe = [loc("xe%d" % i + sx, [128, 2, D], BF16) for i in range(2)]
            X.xeT = [loc("xeT%d" % i + sx, [128, 8, CAP], BF16) for i in range(2)]
            X.hs = loc("hs" + sx, [128, CAP]); X.hbT = loc("hbT" + sx, [128, 4, CAP], BF16)
            X.yo = [loc("yo%d" % i + sx, [128, 2, D]) for i in range(2)]
            return X
        EX = [ex_alloc("_a", 0), ex_alloc("_b", 1)]
        assert loc_off[0] <= 140 * KB, loc_off[0]
        b_ypt = [Buf("ypad%d" % i) for i in range(NE)]

        def expert(ex, si):
            X = EX[si]
            wA = X.wA; wB = X.wB; wC = X.wC; xe = X.xe; xeT = X.xeT; hs = X.hs; hbT = X.hbT; yo = X.yo
            k = (ex // 2) % 2
            if ex >= 2:
                LD(wA[k][:], wA[k].b, w1[ex].rearrange("(kc p) f -> p kc f", p=128), eng="pool")
                LD(wB[k][:], wB[k].b, w3[ex].rearrange("(kc p) f -> p kc f", p=128), eng="pool")
                LD(wC[k][:], wC[k].b, w2[ex].rearrange("(fc p) d -> p fc d", p=128), eng="pool")
            LD(xe[k][:], xe[k].b, xpad[ex * CAP:(ex + 1) * CAP, :].rearrange("(s p) d -> p s d", p=128), reads=[b_xpad])
            for s in range(2):
                for half in range(2):
                    pbt = nextpb()
                    for j in range(4):
                        kc = half * 4 + j
                        TR(pbt[:, j * 128:(j + 1) * 128], pbt.b, xe[k][:, s, kc * 128:(kc + 1) * 128], xe[k].b, IDB)
                    CP("act" if half else "dve", xeT[k][:, half * 4:half * 4 + 4, s * 128:(s + 1) * 128], xeT[k].b,
                       pbt[:, 0:512].rearrange("p (j c) -> p j c", j=4), pbt.b)
            for fc in range(4):
                pa = nextp()
                for kc in range(8):
                    MMl(pa[:, 0:CAP], pa.b, wA[k][:, kc, fc * 128:(fc + 1) * 128], wA[k].b, xeT[k][:, kc, :], xeT[k].b,
                        start=(kc == 0), stop=(kc == 7))
                for kc in range(8):
                    MMl(pa[:, CAP:2 * CAP], pa.b, wB[k][:, kc, fc * 128:(fc + 1) * 128], wB[k].b, xeT[k][:, kc, :], xeT[k].b,
                        start=(kc == 0), stop=(kc == 7))
                ACT(hs[:], hs.b, pa[:, 0:CAP], pa.b, AF.Silu)
                TT("dve", hbT[:, fc, :], hbT.b, hs[:], hs.b, pa[:, CAP:2 * CAP], pa.b, ALU.mult)
            for s in range(2):
                for dh in range(2):
                    py = nextp()
                    for fc in range(4):
                        MMl(py[:], py.b, hbT[:, fc, s * 128:(s + 1) * 128], hbT.b, wC[k][:, fc, dh * 512:(dh + 1) * 512], wC[k].b,
                            start=(fc == 0), stop=(fc == 3))
                    CP("act" if dh else "dve", yo[k][:, s, dh * 512:(dh + 1) * 512], yo[k].b, py[:], py.b)
            STO(ypad[ex * CAP:(ex + 1) * CAP, :].rearrange("(s p) d -> p s d", p=128), b_ypt[ex], yo[k][:], yo[k].b)

        par2(expert, NE)

        new_stage(0)
        rp = load_rows([R_L2G, R_L2B])
        y1 = [loc("y1%d" % i, [128, D]) for i in range(4)]
        y2 = [loc("y2%d" % i, [128, D]) for i in range(4)]
        hrr = [loc("hrr%d" % i, [128, D]) for i in range(4)]
        r2S = [loc("r2%d" % i, [128, D]) for i in range(4)]; ot = [loc("ot%d" % i, [128, D]) for i in range(4)]
        for yy in y1 + y2:
            P.op("dve", lambda e, yy=yy: e.memset(yy[:], 0.0), writes=[yy.b])
        out_v = out_d.rearrange("(t p) d -> t p d", p=128)
        b_outt = [Buf("out%d" % i) for i in range(NT)]

        def st5c(t, si):
            a = si; r2 = r2S[si]
            LD(hrr[a][:], hrr[a].b, h1_v[t], reads=[b_h1t[t]])
            for k, yy in enumerate((y1[a], y2[a])):
                P.dma("pool", lambda e, t=t, k=k, yy=yy: e.indirect_dma_start(
                    out=yy[:], out_offset=None, in_=ypad,
                    in_offset=bass.IndirectOffsetOnAxis(ap=idx[:, k, t:t + 1], axis=0),
                    oob_is_err=False),
                    reads=[b_ypad, idx.b], writes=[yy.b])
            ACT(r2[:], r2.b, hrr[a][:], hrr[a].b, AF.Copy, scale=ALPHA)
            STT("dve", r2[:], r2.b, y1[a][:], y1[a].b, gt1[:, t:t + 1], r2[:], r2.b, ALU.mult, ALU.add, xr=[gt1.b])
            STT("dve", r2[:], r2.b, y2[a][:], y2[a].b, gt2[:, t:t + 1], r2[:], r2.b, ALU.mult, ALU.add, xr=[gt2.b])
            layer_norm(r2[:], r2.b, ot[a][:], ot[a].b, rp[:, 0, :], rp[:, 1, :], rp.b, r2[:], r2.b, c=si)
            out_bufs.append(STO(out_v[t], b_outt[t], ot[a][:], ot[a].b))
        par2(st5c, NT, 4)
        P.emit(final_waits=out_bufs)
    return nc, dbg_out


_CACHE = {}


def _consts():
    c = np.zeros((128, 9, 128), np.float32)
    i = np.arange(128)
    same = (i[:, None] // 64) == (i[None, :] // 64)
    c[:, 0] = np.eye(128)
    c[:, 1] = (same & (i[:, None] <= i[None, :]))
    c[:, 2] = (i[:, None] == 63) * np.ones((1, 128))
    c[:, 3] = (i[:, None] == 127) * np.ones((1, 128))
    c[:, 4] = np.where(same & (i[None, :] >= i[:, None]), 0.0, NEG)
    c[:, 5] = np.where(same & (i[None, :] <= i[:, None]), 0.0, NEG)
    c[:, 6] = 1.0 - np.eye(128)
    c[:, 7] = ((i[None, :] // 64) <= (i[:, None] // 64))
    c[:, 8] = (i[:, None] < i[None, :])
    selh = np.zeros((8, 16, 128), np.float32)
    for h in range(8):
        selh[h, h, :] = 1.0
        selh[h, 8 + h, :] = -1.0
    ecol = (np.arange(32, dtype=np.float32) * CAP).reshape(1, 32)
    return c, selh, ecol


def kernel(x, ln_in_g, ln_in_b, w_in, b_in, gm_ln_g, gm_ln_b, gm_w_s, gm_b_s, dn_conv_w, dn_a_log, dn_dt_bias,
           dn_norm_w, w_pa, w_pb, w_o, ln1_g, ln1_b, w_rg, b_rg, w_re, b_re, w1, w3, w2, ln2_g, ln2_b):
    f = lambda a: np.ascontiguousarray(np.asarray(a, dtype=np.float32))
    if "nc" not in _CACHE:
        _CACHE["nc"] = build()[0]
    nc = _CACHE["nc"]
    b_in0 = f(b_in)[0]
    rowp = np.stack([f(ln_in_g), f(ln_in_b), b_in0[1024:2048], f(gm_ln_g)[0], f(gm_ln_b)[0], f(ln1_g)[0], f(ln1_b)[0],
                     f(ln2_g)[0], f(ln2_b)[0], f(gm_b_s)[0].reshape(-1)], 0)
    colp = np.zeros((128, 96), np.float32)
    colp[:, 0:48] = b_in0[0:6144].reshape(48, 128).T
    colp[:, 48:56] = b_in0[6160:7184].reshape(8, 128).T
    colp[:, 56:64] = b_in0[7184:8208].reshape(8, 128).T
    cw = np.ascontiguousarray(f(dn_conv_w)[0].reshape(4, 24, 128).transpose(2, 1, 0))
    cst, selh, ecol = _consts()
    shared = {
        "w_in": f(w_in)[0], "w_pa": f(w_pa)[0], "w_pb": f(w_pb)[0], "w_o": f(w_o)[0],
        "w1": f(w1)[0], "w3": f(w3)[0], "w2": f(w2)[0], "gm_w_s": f(gm_w_s)[0],
        "rowp": np.ascontiguousarray(rowp), "colp": colp, "cw": cw,
        "ab_b": np.ascontiguousarray(b_in0[6144:6160].reshape(1, 16)),
        "hp": np.concatenate([f(dn_a_log)[0], f(dn_dt_bias)[0]]).reshape(1, 16),
        "nw_col": f(dn_norm_w)[0].reshape(128, 1),
        "wr": np.ascontiguousarray(np.concatenate([f(w_rg)[0], f(w_re)[0]], 1)),
        "br": np.concatenate([f(b_rg)[0], f(b_re)[0]]).reshape(1, 36),
        "cst": cst, "selh": selh, "ecol": ecol,
    }
    xs = f(x)
    in_maps = [dict(shared, x=xs[b]) for b in range(8)]
    res = run_bass_kernel_spmd(nc, in_maps, core_ids=list(range(8)))
    return np.stack([np.asarray(r["out"], dtype=np.float32) for r in res.results], 0)
```

```python
import contextlib
import numpy as np
import concourse.bass as bass
import concourse.mybir as mybir
from concourse.bass_utils import run_bass_kernel_spmd

F32 = mybir.dt.float32
BF16 = mybir.dt.bfloat16
I32 = mybir.dt.int32
AF = mybir.ActivationFunctionType
ALU = mybir.AluOpType
AX = mybir.AxisListType

T = 2048
D = 1024
NT = 16
NH = 8
INW = 8208
NE = 32
REC_STEP = 2.5
REC_OFF = 0.5
LN_ADD_ENG = "dve"
CAP = 256
DF = 512
ALPHA = 2 ** 0.25
NEG = -60000.0


class Buf:
    __slots__ = ("name", "last_w", "readers")

    def __init__(self, name=""):
        self.name = name
        self.last_w = None
        self.readers = {}


class Op:
    __slots__ = ("eng", "fn", "deps", "key", "signal", "semval", "dma", "dslot", "dval", "prewait", "fence", "pos")


ENGS = ("pe", "act", "dve", "pool", "sp")
DMA_K = 6


class Prog:
    def __init__(self, nc):
        self.nc = nc
        self.ops = {e: [] for e in ENGS}
        self.phase = 0
        self.seq = 0
        self.par = False
        self.sid = 0
        self.cnt = {}
        self.step = {}

    def begin_parallel(self):
        self.phase += 1
        self.par = True
        self.cnt = {}
        self.step = {}

    def set_stream(self, sid):
        self.sid = sid

    def end_parallel(self):
        self.phase += 1
        self.par = False
        self.sid = 0

    def _key(self):
        if self.par:
            k = self.cnt.get(self.sid, 0)
            self.cnt[self.sid] = k + self.step.get(self.sid, 1)
            return (self.phase, k, self.sid)
        self.seq += 1
        return (self.phase, self.seq, 0)

    def _deps(self, reads, writes):
        deps = set()
        for b in reads:
            if b.last_w is not None:
                deps.add(b.last_w)
        for b in writes:
            if b.last_w is not None:
                deps.add(b.last_w)
            for r in b.readers.values():
                deps.add(r)
        return deps

    def _mark(self, op, reads, writes):
        for b in reads:
            if op.dma:
                b.readers[("d" + op.eng, id(op))] = op
            else:
                b.readers[op.eng] = op
        for b in writes:
            b.last_w = op
            b.readers = {}

    @staticmethod
    def _flat(xs):
        out = []
        for x in xs:
            if isinstance(x, (list, tuple)):
                out.extend(Prog._flat(x))
            else:
                out.append(x)
        return out

    def _new(self, eng, fn, dma, reads, writes):
        reads = self._flat(reads)
        writes = self._flat(writes)
        o = Op()
        o.eng, o.fn, o.dma, o.fence = eng, fn, dma, False
        o.deps = self._deps(reads, writes)
        o.key = self._key()
        o.signal = dma
        o.semval = None
        o.prewait = None
        self.ops[eng].append(o)
        self._mark(o, reads, writes)
        return o

    def op(self, eng, fn, reads=(), writes=()):
        return self._new(eng, fn, False, reads, writes)

    def dma(self, eng, fn, reads=(), writes=()):
        return self._new(eng, fn, True, reads, writes)

    def fence(self):
        assert not self.par
        self.phase += 1
        for e in ENGS:
            o = Op()
            o.eng, o.fn, o.dma, o.fence = e, None, False, True
            o.deps = set()
            o.key = self._key()
            o.signal = False
            o.semval = None
            o.prewait = None
            self.ops[e].append(o)
        self.phase += 1

    def emit(self, final_waits=()):
        nc = self.nc
        for e in ENGS:
            self.ops[e].sort(key=lambda o: o.key)
        last_comp = {e: None for e in ENGS}
        last_dmas = {e: [] for e in ENGS}
        merged = sorted((o for e in ENGS for o in self.ops[e]), key=lambda o: o.key)
        i = 0
        while i < len(merged):
            o = merged[i]
            if o.fence:
                deps = set()
                for e in ENGS:
                    if last_comp[e] is not None:
                        deps.add(last_comp[e])
                    deps.update(last_dmas[e][-DMA_K:])
                j = i
                while j < len(merged) and merged[j].fence:
                    merged[j].deps = set(deps)
                    j += 1
                i = j
                continue
            if o.dma:
                last_dmas[o.eng].append(o)
            else:
                last_comp[o.eng] = o
            i += 1
        for e in ENGS:
            hist = [o for o in self.ops[e] if o.dma]
            for n, o in enumerate(hist):
                o.dslot = n % DMA_K
                o.dval = 16 * (n // DMA_K + 1)
                o.prewait = hist[n - DMA_K] if n >= DMA_K else None
        ndma = {e: sum(1 for o in self.ops[e] if o.dma) for e in ENGS}
        for e in ENGS:
            for o in self.ops[e]:
                for d in o.deps:
                    assert d.key < o.key, ("dependency ordered after dependent", d.eng, d.key, o.eng, o.key)
        for e in ENGS:
            for o in self.ops[e]:
                for d in o.deps:
                    if not d.dma:
                        if d.eng == "pe" and o.eng == "pe" and not o.dma and not o.fence:
                            continue
                        d.signal = True
        for e in ENGS:
            c = 0
            for o in self.ops[e]:
                if o.dma or o.fence:
                    continue
                if o.signal:
                    c += 1
                    o.semval = c
        with contextlib.ExitStack() as st:
            sem = {e: st.enter_context(nc.semaphore("s_" + e)) for e in ENGS}
            dsem = {e: [st.enter_context(nc.semaphore("d_%s%d" % (e, k))) for k in range(DMA_K)]
                    for e in ENGS if ndma[e] > 0}
            block = st.enter_context(nc.Block())

            def run(e, engine):
                waited = {}

                def wait_for(d):
                    if d.dma:
                        key = ("d", d.eng, d.dslot)
                        s, v = dsem[d.eng][d.dslot], d.dval
                    else:
                        key = d.eng
                        s, v = sem[d.eng], d.semval
                    if waited.get(key, 0) >= v:
                        return
                    waited[key] = v
                    engine.wait_ge(s, v)

                for o in self.ops[e]:
                    for d in sorted(o.deps, key=lambda d: d.key):
                        if d.eng == "pe" and e == "pe" and not d.dma and not o.dma and not o.fence:
                            continue
                        wait_for(d)
                    if o.fence:
                        continue
                    if o.dma:
                        if o.prewait is not None:
                            wait_for(o.prewait)
                        o.fn(engine).then_inc(dsem[e][o.dslot], 16)
                    else:
                        ins = o.fn(engine)
                        if o.signal:
                            ins.then_inc(sem[e], 1)
                if e == "sp":
                    for d in final_waits:
                        wait_for(d)

            @block.tensor
            def _(eng):
                run("pe", eng)

            @block.scalar
            def _(eng):
                run("act", eng)

            @block.vector
            def _(eng):
                run("dve", eng)

            @block.gpsimd
            def _(eng):
                run("pool", eng)

            @block.sync
            def _(eng):
                run("sp", eng)


class Tl:
    def __init__(self, h, name, nb=1):
        self.h = h
        self.b = Buf(name)
        self.bs = [Buf(name + str(i)) for i in range(nb)] if nb > 1 else [self.b]

    def __getitem__(self, k):
        return self.h[k]


def build(stop_after=99, dbg=False):
    nc = bass.Bass("TRN2", target_bir_lowering=False)
    P = Prog(nc)

    def din(name, shape, dt=F32):
        return nc.dram_tensor(name, list(shape), dt, kind="ExternalInput").ap()

    x_d = din("x", [T, D])
    w_in = din("w_in", [D, INW])
    w_pa = din("w_pa", [D, D]); w_pb = din("w_pb", [D, D]); w_o = din("w_o", [D, D])
    w1 = din("w1", [NE, D, DF]); w3 = din("w3", [NE, D, DF]); w2 = din("w2", [NE, DF, D])
    gm_ws = din("gm_w_s", [8, 128, 128])
    rowp = din("rowp", [10, D])
    colp = din("colp", [128, 96])
    ab_b = din("ab_b", [1, 16])
    hp = din("hp", [1, 16])
    nw_col = din("nw_col", [128, 1])
    wr = din("wr", [D, 36])
    br = din("br", [1, 36])
    cst = din("cst", [128, 9, 128])
    cw_d = din("cw", [128, 24, 4])
    selh_d = din("selh", [8, 16, 128])
    ecol_d = din("ecol", [1, 32])
    out_d = nc.dram_tensor("out", [T, D], F32, kind="ExternalOutput").ap()
    h0_s = nc.dram_tensor("h0_s", [T, D], F32).ap()
    xpad = nc.dram_tensor("xpad", [NE * CAP + 1, D], BF16).ap()
    ypad = nc.dram_tensor("ypad", [NE * CAP + 1, D], F32).ap()
    mT_s = nc.dram_tensor("mT_s", [8, 128, T], BF16).ap()
    yb_s = nc.dram_tensor("yb_s", [8, 128, T], BF16).ap()
    b_ybs = [Buf("yb_s%d" % i) for i in range(8)]
    h1_s = nc.dram_tensor("h1_s", [T, D], F32).ap()
    b_h0 = Buf("h0_s"); b_xpad = Buf("xpad"); b_ypad = Buf("ypad"); b_mT = Buf("mT_s"); b_h1s = Buf("h1_s")
    dbg_out = {}
    out_bufs = []

    with contextlib.ExitStack() as st:
        def sb(name, shape, dt=F32, nb=1):
            return Tl(st.enter_context(nc.sbuf_tensor(name, list(shape), dt)), name, nb)

        def ps(name, shape, dt=F32):
            return Tl(st.enter_context(nc.psum_tensor(name, list(shape), dt)), name)

        def dump(name, src_ap, shape, bufs, dt=F32):
            if not dbg:
                return
            d = nc.dram_tensor("dbg_" + name, list(shape), dt, kind="ExternalOutput").ap()
            dbg_out[name] = d
            b = Buf("dbg_" + name)
            out_bufs.append(P.dma("sp", lambda e: e.dma_start(out=d, in_=src_ap), reads=bufs, writes=[b]))

        def MM(o, ob, l, lb, r, rb, start=True, stop=True):
            return P.op("pe", lambda e: e.matmul(o, lhsT=l, rhs=r, start=start, stop=stop),
                        reads=[lb, rb], writes=[ob])

        def TR(o, ob, i, ib, idt):
            return P.op("pe", lambda e: e.transpose(out=o, in_=i, identity=idt), reads=[ib, b_cst], writes=[ob])

        def ACT(o, ob, i, ib, func, bias=None, scale=None, xr=()):
            kw = {}
            if bias is not None:
                kw["bias"] = bias
            if scale is not None:
                kw["scale"] = scale
            return P.op("act", lambda e: e.activation(out=o, in_=i, func=func, **kw),
                        reads=[ib] + list(xr), writes=[ob])

        def CP(eng, o, ob, i, ib):
            if eng == "act":
                return P.op("act", lambda e: e.copy(out=o, in_=i), reads=[ib], writes=[ob])
            return P.op(eng, lambda e: e.tensor_copy(out=o, in_=i), reads=[ib], writes=[ob])

        def TT(eng, o, ob, a, ab, b, bb, op):
            return P.op(eng, lambda e: e.tensor_tensor(out=o, in0=a, in1=b, op=op), reads=[ab, bb], writes=[ob])

        def TS(eng, o, ob, a, ab, s1, s2, op0, op1=None, xr=()):
            if op1 is None:
                return P.op(eng, lambda e: e.tensor_scalar(out=o, in0=a, scalar1=s1, scalar2=None, op0=op0),
                            reads=[ab] + list(xr), writes=[ob])
            return P.op(eng, lambda e: e.tensor_scalar(out=o, in0=a, scalar1=s1, scalar2=s2, op0=op0, op1=op1),
                        reads=[ab] + list(xr), writes=[ob])

        def STT(eng, o, ob, a, ab, s, b, bb, op0, op1, xr=()):
            return P.op(eng, lambda e: e.scalar_tensor_tensor(out=o, in0=a, scalar=s, in1=b, op0=op0, op1=op1),
                        reads=[ab, bb] + list(xr), writes=[ob])

        def RED(o, ob, i, ib, op=ALU.add):
            return P.op("dve", lambda e: e.tensor_reduce(out=o, in_=i, axis=AX.X, op=op), reads=[ib], writes=[ob])

        def LD(o, ob, src, eng="sp", reads=()):
            return P.dma(eng, lambda e: e.dma_start(out=o, in_=src), reads=list(reads), writes=[ob])

        def STO(dst, dstb, i, ib, eng="sp"):
            return P.dma(eng, lambda e: e.dma_start(out=dst, in_=i), reads=[ib], writes=[dstb])


        cstf = sb("cstf", [128, 9, 128]); b_cst = cstf.b
        LD(cstf[:], b_cst, cst)
        IDF, CUMT, SEL0, SEL1, NEGT, NEGL, OFFD, CMASK, USTR = [cstf[:, i, :] for i in range(9)]
        cstb = sb("cstb", [128, 3, 128], BF16)
        LD(cstb[:, 0, :], b_cst, cst[:, 0, :], eng="pool")
        LD(cstb[:, 2, :], b_cst, cst[:, 8, :], eng="pool")
        onesf = sb("onesf", [128, 128])
        P.op("dve", lambda e: e.memset(onesf[:], 1.0), writes=[onesf.b])
        P.op("dve", lambda e: e.tensor_copy(out=cstb[:, 1, :], in_=onesf[:]), reads=[onesf.b, b_cst], writes=[b_cst])
        IDB = cstb[:, 0, :]; ONB = cstb[:, 1, :]; USB = cstb[:, 2, :]
        selh = sb("selh_sb", [8, 16, 128])
        LD(selh[:], selh.b, selh_d)
        cp = sb("cp", [128, 96]); LD(cp[:], cp.b, colp)
        cw = sb("cw_sb", [128, 24, 4]); LD(cw[:], cw.b, cw_d)
        nwc = sb("nwc", [128, 1]); LD(nwc[:], nwc.b, nw_col)
        epsb = sb("epsb", [128, 4])
        P.op("dve", lambda e: e.memset(epsb[:, 0:1], 1e-5), writes=[epsb.b])
        P.op("dve", lambda e: e.memset(epsb[:, 1:2], 1e-6), reads=[epsb.b], writes=[epsb.b])
        P.op("dve", lambda e: e.memset(epsb[:, 2:3], 1.0), reads=[epsb.b], writes=[epsb.b])
        P.op("dve", lambda e: e.memset(epsb[:, 3:4], 0.0), reads=[epsb.b], writes=[epsb.b])
        EPS_LN, EPS_6, ONE_C, ZERO_C = [epsb[:, i:i + 1] for i in range(4)]
        idx = sb("idx", [128, 2, NT], I32); gt1 = sb("gt1", [128, NT]); gt2 = sb("gt2", [128, NT])
        lnS = [(sb("lnst%d" % i, [128, 2, 6]), sb("lnmv%d" % i, [128, 2]), sb("lnr%d" % i, [128, 1]),
                sb("lnb%d" % i, [128, 1])) for i in range(4)]

        pq = [ps("pq%d" % i, [128, 512]) for i in range(8)]
        pqb = []
        for i in range(8):
            v = Tl(pq[i][:].bitcast(BF16), "pqb%d" % i)
            v.b = pq[i].b
            v.bs = [v.b]
            pqb.append(v)
        pctx = {"banks": list(range(8)), "i": 0}
        pstate = {}

        def use_banks(banks, tag=None):
            if tag is not None and tag in pstate:
                pctx.update(pstate[tag])
            else:
                pctx["banks"] = list(banks)
                pctx["i"] = 0
            pctx["tag"] = tag

        def save_banks():
            if pctx.get("tag") is not None:
                pstate[pctx["tag"]] = {"banks": pctx["banks"], "i": pctx["i"]}

        def nextp():
            pctx["i"] = (pctx["i"] + 1) % len(pctx["banks"])
            return pq[pctx["banks"][pctx["i"]]]

        def nextpb():
            pctx["i"] = (pctx["i"] + 1) % len(pctx["banks"])
            return pqb[pctx["banks"][pctx["i"]]]

        def par2(fn, n, ns=2):
            P.begin_parallel()
            nb_ = 8 // ns
            for si in range(ns):
                P.set_stream(si)
                use_banks([nb_ * si + i for i in range(nb_)])
                for t in range(si, n, ns):
                    fn(t, si)
            P.end_parallel()
            use_banks(range(8))

        ARW = 49152
        arena = st.enter_context(nc.sbuf_tensor("arena", [128, ARW], F32))
        loc_off = [0]

        def av(name, off, shape, dt=F32, nb=1):
            n = int(np.prod(shape[1:]))
            nbytes = n * (4 if dt in (F32, I32) else 2)
            assert off % 4 == 0 and off + nbytes <= ARW * 4, (name, off, nbytes)
            ap = arena[0:shape[0], off // 4:(off + nbytes + 3) // 4]
            if dt != F32:
                ap = ap.bitcast(dt)
            if dt == BF16 and nbytes % 4:
                ap = ap[:, 0:n]
            if len(shape) == 3:
                ap = ap.rearrange("p (a b) -> p a b", a=shape[1])
            elif len(shape) == 4:
                ap = ap.rearrange("p (a b c) -> p a b c", a=shape[1], b=shape[2])
            return Tl(ap, name, nb)

        def loc(name, shape, dt=F32, nb=1):
            n = int(np.prod(shape[1:]))
            nbytes = (n * (4 if dt in (F32, I32) else 2) + 3) // 4 * 4
            t_ = av(name, loc_off[0], shape, dt, nb)
            loc_off[0] += nbytes
            return t_

        def new_stage(base_kb):
            P.fence()
            loc_off[0] = base_kb * 1024

        KB = 1024
        hT = av("hT", 0, [128, 8, T], BF16, nb=16)
        ybT = av("ybT", 32 * KB, [128, 8, T], BF16, nb=8)
        yaT = av("yaT", 64 * KB, [128, 8, T], BF16, nb=16)

        def load_rows(rows):
            rp_ = loc("rp", [128, len(rows), D])
            for i, r in enumerate(rows):
                LD(rp_[:, i, :], rp_.b, rowp[r:r + 1, :].partition_broadcast(128))
            return rp_
        (R_LNG, R_LNB, R_BV, R_GMG, R_GMB, R_L1G, R_L1B, R_L2G, R_L2B, R_BS) = range(10)

        def layer_norm(src, srcb, dst, dstb, g_ap, b_ap, gb_buf, tmp, tmpb, c=0):
            lnst, lnmv, lnr, lnb = lnS[c]
            def f(e):
                e.bn_stats(out=lnst[:, 0, :], in_=src[:, 0:512])
                return e.bn_stats(out=lnst[:, 1, :], in_=src[:, 512:1024])
            P.op("dve", f, reads=[srcb], writes=[lnst.b])
            P.op("dve", lambda e: e.bn_aggr(out=lnmv[:], in_=lnst[:]), reads=[lnst.b], writes=[lnmv.b])
            ACT(lnr[:], lnr.b, lnmv[:, 1:2], lnmv.b, AF.Sqrt, bias=EPS_LN, scale=1.0, xr=[epsb.b])
            P.op("dve", lambda e: e.reciprocal(out=lnr[:], in_=lnr[:]), reads=[lnr.b], writes=[lnr.b])
            STT("dve", lnb[:], lnb.b, lnmv[:, 0:1], lnmv.b, -1.0, lnr[:], lnr.b, ALU.mult, ALU.mult)
            ACT(tmp, tmpb, src, srcb, AF.Identity, bias=lnb[:, 0:1], scale=lnr[:, 0:1], xr=[lnb.b, lnr.b])
            TT("dve", tmp, tmpb, tmp, tmpb, g_ap, gb_buf, ALU.mult)
            TT(LN_ADD_ENG, dst, dstb, tmp, tmpb, b_ap, gb_buf, ALU.add)

        loc_off[0] = 96 * KB
        rp = load_rows([R_LNG, R_LNB])
        zt = loc("zt", [128, 8192], BF16)
        P.op("dve", lambda e: e.memset(zt[:], 0.0), writes=[zt.b])
        xz = xpad[0:NE * CAP, :].rearrange("(p a) d -> p (a d)", p=128)
        LD(ypad[NE * CAP:NE * CAP + 1, :], b_ypad, zt[0:1, 0:1024], eng="pool", reads=[zt.b])
        LD(xpad[NE * CAP:NE * CAP + 1, :], b_xpad, zt[0:1, 0:1024], reads=[zt.b])
        for i in range(8):
            P.dma("pool", lambda e, i=i: e.dma_start(out=xz[:, i * 8192:(i + 1) * 8192], in_=zt[:]),
                  reads=[zt.b], writes=[b_xpad])
        xin = [loc("xin%d" % i, [128, D]) for i in range(4)]
        h0t = [loc("h0t%d" % i, [128, D]) for i in range(4)]
        x_v = x_d.rearrange("(t p) d -> t p d", p=128)
        h0_v = h0_s.rearrange("(t p) d -> t p d", p=128)
        b_h0t = [Buf("h0s%d" % i) for i in range(NT)]

        def st1(t, si):
            xi = xin[si]; ho = h0t[si]
            LD(xi[:], xi.b, x_v[t])
            layer_norm(xi[:], xi.b, ho[:], ho.b, rp[:, 0, :], rp[:, 1, :], rp.b, xi[:], xi.b, c=si)
            STO(h0_v[t], b_h0t[t], ho[:], ho.b)
            for half in range(2):
                pt = nextp()
                for j in range(4):
                    kc = half * 4 + j
                    TR(pt[:, j * 128:(j + 1) * 128], pt.b, ho[:, kc * 128:(kc + 1) * 128], ho.b, IDF)
                CP("act", hT[:, half * 4:half * 4 + 4, t * 128:(t + 1) * 128], hT.bs[t],
                   pt[:].rearrange("p (j c) -> p j c", j=4), pt.b)
        par2(st1, NT, 4)
        if dbg:
            dump("hT", hT[:], [128, 8, T], hT.bs, BF16)
        if stop_after <= 1:
            P.emit(final_waits=out_bufs)
            return nc, dbg_out

        def L(x):
            return list(x) if isinstance(x, (list, tuple)) else [x]

        def MMl(o, ob, l, lb, r, rb, start=True, stop=True):
            return P.op("pe", lambda e: e.matmul(o, lhsT=l, rhs=r, start=start, stop=stop),
                        reads=L(lb) + L(rb), writes=L(ob))


        def load_w(dst, src2d, off=0, n=512):
            LD(dst[:, :, off:off + n], dst.b, src2d.rearrange("(kc p) c -> p kc c", p=128), eng="pool")

        def seg_bufs(tl, seg):
            return tl.bs[seg * 4:(seg + 1) * 4]

        new_stage(32)
        beta = loc("beta", [128, NT, 8]); Gsb = loc("Gsb", [128, NT, 8])
        nG = loc("nG", [128, NT, 8]); eG = loc("eG", [128, NT, 8]); ekd = loc("ekd", [128, NT, 8])
        GT = loc("GT", [8, T]); cdt = loc("cdt", [128, NT, 2, 8])
        pre_live_end = loc_off[0]
        wab = loc("wab", [128, 8, 16], BF16)
        LD(wab[:], wab.b, w_in[:, 6144:6160].rearrange("(kc p) c -> p kc c", p=128), eng="pool")
        abb = loc("abb", [128, 16]); LD(abb[:], abb.b, ab_b.partition_broadcast(128))
        hpb = loc("hpb", [128, 16]); LD(hpb[:], hpb.b, hp.partition_broadcast(128))
        absb = loc("absb", [128, NT, 16])
        pt = nextp()
        for t in range(NT):
            for kc in range(8):
                MMl(pt[:, t * 16:(t + 1) * 16], pt.b, hT[:, kc, t * 128:(t + 1) * 128], hT.bs[t], wab[:, kc, :], wab.b,
                    start=(kc == 0), stop=(kc == 7))
        TT("dve", absb[:], absb.b, pt[:, 0:256].rearrange("p (t c) -> p t c", t=NT), pt.b,
           abb[:].unsqueeze(1).broadcast_to([128, NT, 16]), abb.b, ALU.add)
        gsb = loc("gsb", [128, NT, 8]); aexp = loc("aexp", [128, 8])
        glo = loc("glo", [128, NT, 8])
        ACT(beta[:], beta.b, absb[:, :, 8:16], absb.b, AF.Sigmoid)
        TT("dve", gsb[:], gsb.b, absb[:, :, 0:8], absb.b, hpb[:, 8:16].unsqueeze(1).broadcast_to([128, NT, 8]), hpb.b, ALU.add)
        ACT(gsb[:], gsb.b, gsb[:], gsb.b, AF.Exp)
        ACT(gsb[:], gsb.b, gsb[:], gsb.b, AF.Ln, bias=ONE_C, scale=1.0, xr=[epsb.b])
        ACT(aexp[:], aexp.b, hpb[:, 0:8], hpb.b, AF.Exp)
        STT("dve", gsb[:], gsb.b, gsb[:], gsb.b, -1.0, aexp[:].unsqueeze(1).broadcast_to([128, NT, 8]), aexp.b,
            ALU.mult, ALU.mult)
        pt = nextp()
        for t in range(NT):
            MMl(pt[:, t * 8:(t + 1) * 8], pt.b, CUMT, b_cst, gsb[:, t, :], gsb.b)
        CP("dve", Gsb[:], Gsb.b, pt[:, 0:128].rearrange("p (t c) -> p t c", t=NT), pt.b)
        for s4 in range(4):
            pt = nextp()
            for j in range(4):
                t = s4 * 4 + j
                MMl(pt[0:8, j * 128:(j + 1) * 128], pt.b, gsb[:, t, :], gsb.b, CUMT, b_cst)
            CP("dve", GT[:, s4 * 512:(s4 + 1) * 512], GT.b, pt[0:8, :], pt.b)
        pt = nextp()
        for t in range(NT):
            for c in range(2):
                MMl(pt[:, (t * 2 + c) * 8:(t * 2 + c + 1) * 8], pt.b, SEL0 if c == 0 else SEL1, b_cst, Gsb[:, t, :], Gsb.b)
        glf = loc("glf", [128, NT, 2, 8])
        CP("dve", glf[:], glf.b, pt[:, 0:256].rearrange("p (t c h) -> p t c h", t=NT, c=2), pt.b)
        ACT(cdt[:], cdt.b, glf[:], glf.b, AF.Exp)
        CP("dve", glo[0:64, :, :], glo.b, glf[0:64, :, 0, :], glf.b)
        CP("dve", glo[64:128, :, :], glo.b, glf[64:128, :, 1, :], glf.b)
        TT("dve", glo[:], glo.b, glo[:], glo.b, Gsb[:], Gsb.b, ALU.subtract)
        ACT(ekd[:], ekd.b, glo[:], glo.b, AF.Exp)
        ACT(eG[:], eG.b, Gsb[:], Gsb.b, AF.Exp)
        TS("dve", nG[:], nG.b, Gsb[:], Gsb.b, -1.0, None, ALU.mult)
        if dbg:
            dump("Gsb", Gsb[:], [128, NT, 8], [Gsb.b])
            dump("beta", beta[:], [128, NT, 8], [beta.b])
            dump("GT", GT[:], [8, T], [GT.b])

        P.fence()
        loc_off[0] = pre_live_end
        from types import SimpleNamespace as NS

        def gdn_alloc(sx):
            X = NS()
            X.wb = loc("wvh" + sx, [128, 8, 512], BF16)
            X.pr = loc("pre" + sx, [128, 3, 516], BF16)
            X.dg = loc("dg" + sx, [128, 12, 128], BF16)
            X.qkvT = [loc("qkvT%d" % i + sx, [128, 3, 512], BF16) for i in range(2)]
            X.szT = [loc("szT%d" % i + sx, [128, 512], BF16) for i in range(3)]
            X.tm = loc("tm" + sx, [128, 3, 4, 128], BF16, nb=3)
            X.sq = loc("sq" + sx, [128, 4, 128]); X.ssq = loc("ssq" + sx, [128, 2, 4]); X.rn = loc("rn" + sx, [128, 2, 4])
            X.scl = loc("scl" + sx, [128, 7, 4], nb=7)
            X.scd = [loc("scd%d" % i + sx, [128, 7, 4, 128], BF16, nb=7) for i in range(2)]
            X.fmT = [loc("fmT%d" % i + sx, [128, 4, 512], BF16, nb=4) for i in range(2)]
            X.DTm = loc("DTm" + sx, [128, 4, 128], nb=4); X.DLm = loc("DLm" + sx, [128, 4, 128], nb=4)
            X.SN = loc("SN" + sx, [128, 4, 128]); X.SL = loc("SL" + sx, [128, 4, 128]); X.Pm = loc("Pm" + sx, [128, 4, 128])
            X.ATb = [loc("ATb%d" % i + sx, [128, 4, 128], BF16) for i in range(2)]
            X.TTb = [loc("TTb%d" % i + sx, [128, 4, 128], BF16) for i in range(2)]
            X.wcT = [loc("wcT%d" % i + sx, [128, 512], BF16) for i in range(2)]
            X.osq = loc("osq" + sx, [128, 4, 128])
            X.S32 = loc("S32" + sx, [128, 128]); X.Sbf = loc("Sbf" + sx, [128, 128], BF16)
            X.vnb = loc("vnb" + sx, [128, 4, 128], BF16); X.osb = loc("osb" + sx, [128, 4, 128])
            X.onb = loc("onb" + sx, [128, 4, 128], BF16); X.orr = loc("orr" + sx, [128, 4])
            X.ybo = [loc("ybo%d" % i + sx, [128, 512], BF16) for i in range(1)]
            return X
        GX = [gdn_alloc("_a"), gdn_alloc("_b")]

        def gdn_proj_steps(h, X, seg):
            wb = X.wb; pr = X.pr; dg = X.dg
            qkvT = X.qkvT[seg % 2]; szT = X.szT[seg % 3]
            sl = slice(seg * 512, (seg + 1) * 512)

            def pg(j):
                pt = nextp()
                for kc in range(8):
                    MMl(pt[:], pt.b, wb[:, kc, j * 128:(j + 1) * 128], wb.b, hT[:, kc, sl], seg_bufs(hT, seg),
                        start=(kc == 0), stop=(kc == 7))
                bcol = cp[:, 16 + j * 8 + h:16 + j * 8 + h + 1]
                if j < 3:
                    ACT(pr[:, j, 3:515], pr.b, pt[:], pt.b, AF.Identity, bias=bcol, scale=1.0, xr=[cp.b])
                else:
                    ACT(szT[:], szT.b, pt[:], pt.b, AF.Silu, bias=bcol, scale=1.0, xr=[cp.b])

            def cv(j):
                pt = nextp()
                for k in range(4):
                    MMl(pt[:], pt.b, dg[:, j * 4 + k, :], dg.b, pr[:, j, k:k + 512], pr.b, start=(k == 0), stop=(k == 3))
                ACT(qkvT[:, j, :], qkvT.b, pt[:], pt.b, AF.Silu)
                if j == 2:
                    CP("act", pr[:, :, 0:3], pr.b, pr[:, :, 512:515], pr.b)
            return [lambda j=j: pg(j) for j in range(4)] + [lambda j=j: cv(j) for j in range(3)]

        def gdn_proj(h, X, seg):
            for st_ in gdn_proj_steps(h, X, seg):
                st_()

        def gdn_decay(h, X, seg):
            DTm = X.DTm; DLm = X.DLm
            pN = nextp(); pL = nextp()
            for i in range(4):
                t = seg * 4 + i
                cs = slice(i * 128, (i + 1) * 128)
                gts = GT[:, t * 128:(t + 1) * 128]
                MMl(pN[:, cs], pN.b, selh[:, h, :], selh.b, gts, GT.b, start=True, stop=False)
                MMl(pN[:, cs], pN.b, IDF, b_cst, NEGT, b_cst, start=False, stop=True)
                MMl(pL[:, cs], pL.b, selh[:, 8 + h, :], selh.b, gts, GT.b, start=True, stop=False)
                MMl(pL[:, cs], pL.b, IDF, b_cst, NEGL, b_cst, start=False, stop=True)
            for i in range(4):
                t = seg * 4 + i
                cs = slice(i * 128, (i + 1) * 128)
                ACT(DTm[:, i, :], DTm.bs[i], pN[:, cs], pN.b, AF.Exp, bias=nG[:, t, h:h + 1], scale=1.0, xr=[nG.b])
                ACT(DLm[:, i, :], DLm.bs[i], pL[:, cs], pL.b, AF.Exp, bias=Gsb[:, t, h:h + 1], scale=1.0, xr=[Gsb.b])
            TT("dve", DLm[:], DLm.bs, DLm[:], DLm.bs, OFFD.unsqueeze(1).broadcast_to([128, 4, 128]), b_cst, ALU.mult)

        def gdn_init(h, X):
            wb = X.wb; pr = X.pr; dg = X.dg; tm = X.tm; sq = X.sq; ssq = X.ssq
            rn = X.rn; scl = X.scl; DTm = X.DTm; DLm = X.DLm; SN = X.SN; SL = X.SL
            Pm = X.Pm; S32 = X.S32; Sbf = X.Sbf; vnb = X.vnb; osb = X.osb
            onb = X.onb; orr = X.orr; osq = X.osq
            for j in range(4):
                c0 = 2048 + j * 1024 + h * 128
                load_w(wb, w_in[:, c0:c0 + 128], off=j * 128, n=128)
            P.op("dve", lambda e: e.memset(S32[:], 0.0), writes=[S32.b])
            P.op("dve", lambda e: e.memset(Sbf[:], 0.0), writes=[Sbf.b])
            P.op("dve", lambda e: e.memset(pr[:, :, 0:3], 0.0), writes=[pr.b])
            for j in range(3):
                for k in range(4):
                    ACT(dg[:, j * 4 + k, :], dg.b, IDB, b_cst, AF.Copy, scale=cw[:, j * 8 + h, k:k + 1], xr=[cw.b])
            gdn_proj(h, X, 0)
            gdn_decay(h, X, 0)

        def gdn_prep(h, X, seg):
            wb = X.wb; pr = X.pr; dg = X.dg; tm = X.tm; sq = X.sq; ssq = X.ssq
            rn = X.rn; scl = X.scl; DTm = X.DTm; DLm = X.DLm; SN = X.SN; SL = X.SL
            Pm = X.Pm; S32 = X.S32; Sbf = X.Sbf; vnb = X.vnb; osb = X.osb
            onb = X.onb; orr = X.orr; osq = X.osq
            sp_ = seg % 2
            szT = X.szT[seg % 3]; qkvT = X.qkvT[sp_]; scd = X.scd[sp_]; fmT = X.fmT[sp_]; ATb = X.ATb[sp_]; TTb = X.TTb[sp_]; wcT = X.wcT[sp_]
            sl = slice(seg * 512, (seg + 1) * 512)
            KNT, KBT, QNT, QDT = [fmT[:, m, :] for m in range(4)]
            for j in range(3):
                pbt = nextpb()
                for i in range(4):
                    TR(pbt[:, i * 128:(i + 1) * 128], pbt.b, qkvT[:, j, i * 128:(i + 1) * 128], qkvT.b, IDB)
                CP("act" if j == 1 else "dve", tm[:, j, :, :], tm.bs[j],
                   pbt[:, 0:512].rearrange("p (i c) -> p i c", i=4), pbt.b)
            for j in range(2):
                TT("dve", sq[:], sq.b, tm[:, j, :, :], tm.bs[j], tm[:, j, :, :], tm.bs[j], ALU.mult)
                RED(ssq[:, j, :], ssq.b, sq[:], sq.b)
            ACT(rn[:], rn.b, ssq[:], ssq.b, AF.Sqrt, bias=EPS_6, scale=1.0, xr=[epsb.b])
            P.op("dve", lambda e: e.reciprocal(out=rn[:], in_=rn[:]), reads=[rn.b], writes=[rn.b])
            t4 = slice(seg * 4, seg * 4 + 4)
            bt = beta[:, t4, h]; eg = eG[:, t4, h]; ek = ekd[:, t4, h]
            RQ = rn[:, 0, :]; RK = rn[:, 1, :]
            CP("dve", scl[:, 0, :], scl.bs[0], RK, rn.b)
            TT("dve", scl[:, 1, :], scl.bs[1], RK, rn.b, bt, beta.b, ALU.mult)
            TS("dve", scl[:, 2, :], scl.bs[2], RQ, rn.b, 128 ** -0.5, None, ALU.mult)
            TT("dve", scl[:, 3, :], scl.bs[3], scl[:, 2, :], scl.bs[2], eg, eG.b, ALU.mult)
            TT("dve", scl[:, 4, :], scl.bs[4], scl[:, 1, :], scl.bs[1], eg, eG.b, ALU.mult)
            TT("dve", scl[:, 5, :], scl.bs[5], RK, rn.b, ek, ekd.b, ALU.mult)
            CP("dve", scl[:, 6, :], scl.bs[6], bt, beta.b)
            srcs = [1, 1, 0, 0, 1, 1, 2]
            for m in range(4):
                TT("dve", scd[:, m, :, :], scd.bs[m], tm[:, srcs[m], :, :], tm.bs[srcs[m]],
                   scl[:, m, :].unsqueeze(2).broadcast_to([128, 4, 128]), scl.bs[m], ALU.mult)
            for m in range(4, 7):
                for i in range(4):
                    ACT(scd[:, m, i, :], scd.bs[m], tm[:, srcs[m], i, :], tm.bs[srcs[m]], AF.Copy,
                        scale=scl[:, m, i:i + 1], xr=[scl.bs[m]])
            for m in range(4):
                pbt = nextpb()
                for i in range(4):
                    TR(pbt[:, i * 128:(i + 1) * 128], pbt.b, scd[:, m, i, :], scd.bs[m], IDB)
                CP("act" if m % 2 else "dve", fmT[:, m, :], fmT.bs[m], pbt[:, 0:512], pbt.b)
            pkn = nextp(); pkl = nextp(); pat = nextp()
            for i in range(4):
                cs = slice(i * 128, (i + 1) * 128)
                MMl(pkn[:, cs], pkn.b, KNT[:, cs], fmT.bs[0], KBT[:, cs], fmT.bs[1])
                MMl(pkl[:, cs], pkl.b, KBT[:, cs], fmT.bs[1], KNT[:, cs], fmT.bs[0])
                MMl(pat[:, cs], pat.b, KNT[:, cs], fmT.bs[0], QNT[:, cs], fmT.bs[2])
            v4 = lambda ap: ap.rearrange("p (i c) -> p i c", i=4)
            TT("dve", ATb[:], ATb.b, v4(pat[:]), pat.b, DTm[:], DTm.bs, ALU.mult)
            offb = OFFD.unsqueeze(1).broadcast_to([128, 4, 128])
            TT("dve", DTm[:], DTm.bs, DTm[:], DTm.bs, offb, b_cst, ALU.mult)
            TT("dve", SN[:], SN.b, v4(pkn[:]), pkn.b, DTm[:], DTm.bs, ALU.mult)
            TT("dve", SL[:], SL.b, v4(pkl[:]), pkl.b, DLm[:], DLm.bs, ALU.mult)
            hoist = gdn_proj_steps(h, X, seg + 1) if seg + 1 < 4 else []
            hsched = {0: [0], 1: [1], 2: [2], 3: [3], 4: [4, 5]}
            idb4 = IDF.unsqueeze(1).broadcast_to([128, 4, 128])
            TT("dve", Pm[:], Pm.b, idb4, b_cst, SN[:], SN.b, ALU.subtract)
            for it in range(5):
                p1 = nextp(); p2 = nextp()
                for i in range(4):
                    cs = slice(i * 128, (i + 1) * 128)
                    if it < 4:
                        MMl(p1[:, cs], p1.b, SL[:, i, :], SL.b, SN[:, i, :], SN.b)
                    MMl(p2[:, cs], p2.b, SN[:, i, :], SN.b, SL[:, i, :], SL.b)
                if it < 4:
                    CP("act", SN[:], SN.b, v4(p1[:]), p1.b)
                CP("dve", SL[:], SL.b, v4(p2[:]), p2.b)
                if hoist:
                    for q_ in hsched[it]:
                        hoist[q_]()
                p3 = nextp()
                for i in range(4):
                    cs = slice(i * 128, (i + 1) * 128)
                    MMl(p3[:, cs], p3.b, SL[:, i, :], SL.b, Pm[:, i, :], Pm.b)
                TT("dve", Pm[:], Pm.b, v4(p3[:]), p3.b, Pm[:], Pm.b, ALU.add)
            CP("act", TTb[:], TTb.b, Pm[:], Pm.b)
            if dbg and h == 0 and seg == 0:
                dump("TT", Pm[:], [128, 4, 128], [Pm.b])
                dump("AT", ATb[:], [128, 4, 128], [ATb.b], BF16)
            pw = nextp()
            for i in range(4):
                cs = slice(i * 128, (i + 1) * 128)
                MMl(pw[:, cs], pw.b, scd[:, 4, i, :], scd.bs[4], TTb[:, i, :], TTb.b)
            TS("dve", wcT[:], wcT.b, pw[:], pw.b, -1.0, None, ALU.mult)
            if hoist:
                hoist[6]()
                gdn_decay(h, X, seg + 1)

        def gdn_rec(h, X, seg):
            wb = X.wb; pr = X.pr; dg = X.dg; tm = X.tm; sq = X.sq; ssq = X.ssq
            rn = X.rn; scl = X.scl; DTm = X.DTm; DLm = X.DLm; SN = X.SN; SL = X.SL
            Pm = X.Pm; S32 = X.S32; Sbf = X.Sbf; vnb = X.vnb; osb = X.osb
            onb = X.onb; orr = X.orr; osq = X.osq
            sp_ = seg % 2
            szT = X.szT[seg % 3]; qkvT = X.qkvT[sp_]; scd = X.scd[sp_]; fmT = X.fmT[sp_]; ATb = X.ATb[sp_]; TTb = X.TTb[sp_]; wcT = X.wcT[sp_]
            sl = slice(seg * 512, (seg + 1) * 512)
            KNT, KBT, QNT, QDT = [fmT[:, m, :] for m in range(4)]
            def nextq():
                X.qi = (X.qi + 1) % 4
                return X.recq[X.qi]
            for c in range(8):
                i = c // 2; p0 = 64 * (c % 2); ps_ = slice(p0, p0 + 64)
                cs = slice(i * 128, (i + 1) * 128)
                t = seg * 4 + i
                pv = nextq()
                MMl(pv[:, 0:128], pv.b, TTb[ps_, i, :], TTb.b, scd[ps_, 6, i, :], scd.bs[6], start=True, stop=False)
                MMl(pv[:, 0:128], pv.b, wcT[:, cs], wcT.b, Sbf[:], Sbf.b, start=False, stop=True)
                CP("act", vnb[ps_, i, :], vnb.b, pv[ps_, 0:128], pv.b)
                po = nextq()
                MMl(po[:, 0:128], po.b, QDT[:, cs], fmT.bs[3], Sbf[:], Sbf.b, start=True, stop=False)
                MMl(po[:, 0:128], po.b, ATb[ps_, i, :], ATb.b, vnb[ps_, i, :], vnb.b, start=False, stop=True)
                CP("act", osb[ps_, i, :], osb.b, po[ps_, 0:128], po.b)
                pS = nextq()
                MMl(pS[:, 0:128], pS.b, scd[ps_, 5, i, :], scd.bs[5], vnb[ps_, i, :], vnb.b)
                STT("dve", Sbf[:], Sbf.b, S32[:], S32.b, cdt[:, t, c % 2, h:h + 1], pS[:, 0:128], pS.b,
                    ALU.mult, ALU.add, xr=[cdt.b])
                STT("dve", S32[:], S32.b, S32[:], S32.b, cdt[:, t, c % 2, h:h + 1], pS[:, 0:128], pS.b,
                    ALU.mult, ALU.add, xr=[cdt.b])
            TT("dve", osq[:], osq.b, osb[:], osb.b, osb[:], osb.b, ALU.mult)
            RED(orr[:], orr.b, osq[:], osq.b)
            ACT(orr[:], orr.b, orr[:], orr.b, AF.Sqrt, bias=EPS_6, scale=1.0 / 128.0, xr=[epsb.b])
            P.op("dve", lambda e: e.reciprocal(out=orr[:], in_=orr[:]), reads=[orr.b], writes=[orr.b])
            TT("dve", onb[:], onb.b, osb[:], osb.b, orr[:].unsqueeze(2).broadcast_to([128, 4, 128]), orr.b, ALU.mult)
            pbt = X.recb
            for i in range(4):
                TR(pbt[:, i * 128:(i + 1) * 128], pbt.bs, onb[:, i, :], onb.b, IDB)
            yo_ = X.ybo[0]
            STT("dve", yo_[:], yo_.b, pbt[:, 0:512], pbt.bs, nwc[:, 0:1], szT[:], szT.b,
                ALU.mult, ALU.mult, xr=[nwc.b])
            STO(yb_s[h][:, sl], b_ybs[h], yo_[:], yo_.b)


        GW = 1000.0
        for hp in range(NH // 2):
            P.begin_parallel()
            for si in range(2):
                h = 2 * hp + si
                X = GX[si]
                rb = 4 * si + 3
                X.recq = []
                for q in range(4):
                    v = Tl(pq[rb][:, q * 128:(q + 1) * 128], "recq")
                    v.b = pq[rb].b
                    v.bs = [v.b]
                    X.recq.append(v)
                X.recb = Tl(pqb[rb][:, 0:512], "recb")
                X.recb.b = pq[rb].b
                X.recb.bs = [pq[rb].b]
                X.qi = 0
                sp, sr = 2 * si, 2 * si + 1
                pctx["banks"] = [4 * si, 4 * si + 1, 4 * si + 2]; pctx["i"] = 0
                P.set_stream(sp); P.cnt[sp] = -200.0
                gdn_init(h, X)
                P.step[sr] = REC_STEP
                for seg in range(4):
                    P.set_stream(sp); P.cnt[sp] = seg * GW
                    gdn_prep(h, X, seg)
                    P.set_stream(sr); P.cnt[sr] = (seg + 1) * GW + REC_OFF
                    gdn_rec(h, X, seg)
            P.end_parallel()
            use_banks(range(8))
        new_stage(96)
        rp = load_rows([R_BV, R_GMG, R_GMB, R_BS])
        wv = [loc("wv%d" % i, [128, 8, 512], BF16) for i in range(4)]
        wsf = loc("wsf", [128, 8, 128]); wmT = loc("wmT", [128, 8, 128], BF16)
        LD(wsf[:], wsf.b, gm_ws.rearrange("g t s -> t g s"))
        TT("dve", wsf[:], wsf.b, wsf[:], wsf.b, CMASK.unsqueeze(1).broadcast_to([128, 8, 128]), b_cst, ALU.mult)
        for hf in range(2):
            pt = nextp()
            for j in range(4):
                TR(pt[:, j * 128:(j + 1) * 128], pt.b, wsf[:, hf * 4 + j, :], wsf.b, IDF)
            CP("act", wmT[:, hf * 4:hf * 4 + 4, :], wmT.b, pt[:].rearrange("p (j c) -> p j c", j=4), pt.b)
        for i in range(4):
            load_w(wv[i], w_in[:, i * 512:(i + 1) * 512])
        for cc in range(8):
            for seg in range(4):
                pt = nextp()
                for kc in range(8):
                    MMl(pt[:], pt.b, wv[cc // 4][:, kc, (cc % 4) * 128:(cc % 4 + 1) * 128], wv[cc // 4].b,
                        hT[:, kc, seg * 512:(seg + 1) * 512], seg_bufs(hT, seg), start=(kc == 0), stop=(kc == 7))
                P.op("act", lambda e, pt=pt, cc=cc, seg=seg: e.activation(
                    out=yaT[:, cc, seg * 512:(seg + 1) * 512], in_=pt[:], func=AF.Gelu, bias=cp[:, cc:cc + 1], scale=1.0),
                    reads=[pt.b, cp.b], writes=seg_bufs(yaT, seg))
        vpreS = [loc("vpre%d" % i, [128, D]) for i in range(4)]
        vlnS = [loc("vln%d" % i, [128, D]) for i in range(4)]
        vbtS = [loc("vbt%d" % i, [128, D], BF16) for i in range(4)]

        def gm_v(t, si):
            vpre = vpreS[si]; vln = vlnS[si]; vbt = vbtS[si]
            for hf in range(2):
                pt = nextp()
                for kc in range(8):
                    MMl(pt[:], pt.b, hT[:, kc, t * 128:(t + 1) * 128], hT.bs[t], wv[2 + hf][:, kc, :], wv[2 + hf].b,
                        start=(kc == 0), stop=(kc == 7))
                TT("dve", vpre[:, hf * 512:(hf + 1) * 512], vpre.b, pt[:], pt.b,
                   rp[:, 0, hf * 512:(hf + 1) * 512], rp.b, ALU.add)
            ACT(vpre[:], vpre.b, vpre[:], vpre.b, AF.Gelu)
            layer_norm(vpre[:], vpre.b, vln[:], vln.b, rp[:, 1, :], rp[:, 2, :], rp.b, vln[:], vln.b, c=si)
            CP("act", vbt[:], vbt.b, vln[:], vln.b)
            for g0 in (0, 4):
                pt = nextp()
                for j in range(4):
                    g = g0 + j
                    MMl(pt[:, j * 128:(j + 1) * 128], pt.b, vbt[:, g * 128:(g + 1) * 128], vbt.b, wmT[:, g, :], wmT.b,
                        start=True, stop=True)
                tmpy = vpre[:, g0 * 128:(g0 + 4) * 128]
                TT("dve", tmpy, vpre.b, pt[:], pt.b, rp[:, 3, g0 * 128:(g0 + 4) * 128], rp.b, ALU.add)
                TT("dve", yaT[:, g0:g0 + 4, t * 128:(t + 1) * 128], yaT.bs[t],
                   tmpy.rearrange("p (j c) -> p j c", j=4), vpre.b,
                   yaT[:, g0:g0 + 4, t * 128:(t + 1) * 128], yaT.bs[t], ALU.mult)
        par2(gm_v, NT, 4)
        if dbg:
            dump("yaT", yaT[:], [128, 8, T], yaT.bs, BF16)
        if stop_after <= 3:
            P.emit(final_waits=out_bufs)
            return nc, dbg_out


        new_stage(96)
        wv = [loc("wm%d" % i, [128, 8, 512], BF16) for i in range(3)]
        LD(ybT[:], ybT.b, yb_s.rearrange("c p t -> p c t"), reads=b_ybs)
        gat = [loc("gat%d" % i, [128, 2, 512]) for i in range(2)]
        t1 = loc("t1", [128, 512]); t2 = loc("t2", [128, 512])
        mo = [loc("mo%d" % i, [128, 512], BF16) for i in range(2)]
        for dc in range(8):
            wb = wv[dc % 3]
            load_w(wb, w_pa[:, dc * 128:(dc + 1) * 128], off=0, n=128)
            load_w(wb, w_pb[:, dc * 128:(dc + 1) * 128], off=128, n=128)
            load_w(wb, w_in[:, 6160 + dc * 128:6160 + (dc + 1) * 128], off=256, n=128)
            load_w(wb, w_in[:, 7184 + dc * 128:7184 + (dc + 1) * 128], off=384, n=128)
            for seg in range(4):
                sl = slice(seg * 512, (seg + 1) * 512)
                srcs = [(yaT, seg_bufs(yaT, seg)), (ybT, [ybT.b]), (hT, seg_bufs(hT, seg)), (hT, seg_bufs(hT, seg))]
                pp = []
                for j in range(4):
                    pt = nextp(); pp.append(pt)
                    for kc in range(8):
                        MMl(pt[:], pt.b, wb[:, kc, j * 128:(j + 1) * 128], wb.b, srcs[j][0][:, kc, sl], srcs[j][1],
                            start=(kc == 0), stop=(kc == 7))
                g = gat[seg % 2]; m_ = mo[seg % 2]
                ACT(g[:, 0, :], g.b, pp[2][:], pp[2].b, AF.Sigmoid, bias=cp[:, 48 + dc:49 + dc], scale=1.0, xr=[cp.b])
                ACT(g[:, 1, :], g.b, pp[3][:], pp[3].b, AF.Sigmoid, bias=cp[:, 56 + dc:57 + dc], scale=1.0, xr=[cp.b])
                TT("dve", t1[:], t1.b, pp[0][:], pp[0].b, g[:, 0, :], g.b, ALU.mult)
                TT("dve", t2[:], t2.b, pp[1][:], pp[1].b, g[:, 1, :], g.b, ALU.mult)
                TT("dve", m_[:], m_.b, t1[:], t1.b, t2[:], t2.b, ALU.add)
                STO(mT_s[dc][:, sl], b_mT, m_[:], m_.b)

        new_stage(96)
        h1 = av("h1", 0, [128, NT, D], F32, nb=16)
        mT = av("mT", 64 * KB, [128, 8, T], BF16)
        LD(mT[:], mT.b, mT_s.rearrange("c p t -> p c t"), reads=[b_mT])
        rp = load_rows([R_L1G, R_L1B])
        wo = [loc("wo%d" % i, [128, 8, 512], BF16) for i in range(2)]
        for i in range(2):
            load_w(wo[i], w_o[:, i * 512:(i + 1) * 512])
        h0r = [loc("h0r%d" % i, [128, D]) for i in range(4)]
        r1S = [loc("r1%d" % i, [128, D]) for i in range(4)]
        b_h1t = [Buf("h1s%d" % i) for i in range(NT)]
        h1_v = h1_s.rearrange("(t p) d -> t p d", p=128)

        def st4b(t, si):
            hr = h0r[si]; r1 = r1S[si]
            LD(hr[:], hr.b, h0_v[t], reads=[b_h0t[t]])
            for hf in range(2):
                pt = nextp()
                for kc in range(8):
                    MMl(pt[:], pt.b, mT[:, kc, t * 128:(t + 1) * 128], mT.b, wo[hf][:, kc, :], wo[hf].b,
                        start=(kc == 0), stop=(kc == 7))
                STT("dve", r1[:, hf * 512:(hf + 1) * 512], r1.b, hr[:, hf * 512:(hf + 1) * 512], hr.b, ALPHA,
                    pt[:], pt.b, ALU.mult, ALU.add)
            layer_norm(r1[:], r1.b, h1[:, t, :], h1.bs[t], rp[:, 0, :], rp[:, 1, :], rp.b, r1[:], r1.b, c=si)
            STO(h1_v[t], b_h1t[t], h1[:, t, :], h1.bs[t])
        par2(st4b, NT, 4)
        if dbg:
            dump("h1", h1[:], [128, NT, D], h1.bs)
        if stop_after <= 4:
            P.emit(final_waits=out_bufs)
            return nc, dbg_out

        new_stage(96)
        PF = {}
        pf_off = 140 * KB
        for si_ in range(2):
            for nm_, shp_, src_ in (("wA", [128, 8, DF], w1[si_].rearrange("(kc p) f -> p kc f", p=128)),
                                    ("wB", [128, 8, DF], w3[si_].rearrange("(kc p) f -> p kc f", p=128)),
                                    ("wC", [128, 4, D], w2[si_].rearrange("(fc p) d -> p fc d", p=128))):
                t_ = av("pf_%s%d" % (nm_, si_), pf_off, shp_, BF16)
                pf_off += 8 * KB
                LD(t_[:], t_.b, src_, eng="pool")
                PF[(si_, nm_)] = t_
        wrs = loc("wrs", [128, 8, 36]); LD(wrs[:], wrs.b, wr.rearrange("(kc p) c -> p kc c", p=128))
        brb = loc("brb", [128, 36]); LD(brb[:], brb.b, br.partition_broadcast(128))
        ecb = loc("ecb", [128, 32]); LD(ecb[:], ecb.b, ecol_d.partition_broadcast(128))
        lg = loc("lg", [128, NT, 36], nb=NT)
        h1T = [loc("h1T%d" % i, [128, 8, 128]) for i in range(4)]

        def st5a(t, si):
            hT_ = h1T[si]
            for half in range(2):
                pt = nextp()
                for j in range(4):
                    kc = half * 4 + j
                    TR(pt[:, j * 128:(j + 1) * 128], pt.b, h1[:, t, kc * 128:(kc + 1) * 128], h1.bs[t], IDF)
                CP("act", hT_[:, half * 4:half * 4 + 4, :], hT_.b, pt[:].rearrange("p (j c) -> p j c", j=4), pt.b)
            pl = nextp()
            for kc in range(8):
                MMl(pl[:, 0:36], pl.b, hT_[:, kc, :], hT_.b, wrs[:, kc, :], wrs.b, start=(kc == 0), stop=(kc == 7))
            TT("dve", lg[:, t, :], lg.bs[t], pl[:, 0:36], pl.b, brb[:], brb.b, ALU.add)
        par2(st5a, NT, 4)
        mg = loc("mg", [128, NT]); gh = loc("gh", [128, NT, 4]); egx = loc("egx", [128, NT, 4])
        sg = loc("sg", [128, NT]); pgrp = loc("pgrp", [128, NT])
        t48 = loc("t48", [128, NT, 4, 8]); les = loc("les", [128, NT, 8]); les2 = loc("les2", [128, NT, 8])
        m1 = loc("m1", [128, NT]); m2 = loc("m2", [128, NT]); oh1 = loc("oh1", [128, NT, 8]); oh2 = loc("oh2", [128, NT, 8])
        A1 = loc("A1", [128, NT, 4, 8]); A2 = loc("A2", [128, NT, 4, 8]); Ab = loc("Ab", [128, NT, 32], BF16)
        pos = loc("pos", [128, NT, 32]); pe_ = loc("pe_", [128, NT, 32]); t32 = loc("t32", [128, NT, 32])
        slf = loc("slf", [128, 2, NT]); pv_ = loc("pv_", [128, 2, NT]); val = loc("val", [128, 2, NT])
        lgg = lg[:, :, 0:4]
        le4 = lg[:, :, 4:36].rearrange("p t (g e) -> p t g e", g=4)

        def bc(ap, shape, ax):
            return ap.unsqueeze(ax).broadcast_to(shape)
        P.op("dve", lambda e: e.tensor_reduce(out=mg[:], in_=lgg, axis=AX.X, op=ALU.max), reads=[lg.bs], writes=[mg.b])
        TT("dve", gh[:], gh.b, lgg, lg.bs, bc(mg[:], [128, NT, 4], 2), mg.b, ALU.is_equal)
        TT("dve", egx[:], egx.b, lgg, lg.bs, bc(mg[:], [128, NT, 4], 2), mg.b, ALU.subtract)
        ACT(egx[:], egx.b, egx[:], egx.b, AF.Exp)
        RED(sg[:], sg.b, egx[:], egx.b)
        P.op("dve", lambda e: e.reciprocal(out=pgrp[:], in_=sg[:]), reads=[sg.b], writes=[pgrp.b])
        TT("dve", t48[:], t48.b, le4, lg.bs, bc(gh[:], [128, NT, 4, 8], 3), gh.b, ALU.mult)
        RED(les[:], les.b, t48[:].rearrange("p t g e -> p t e g"), t48.b)
        P.op("dve", lambda e: e.tensor_reduce(out=m1[:], in_=les[:], axis=AX.X, op=ALU.max), reads=[les.b], writes=[m1.b])
        TT("dve", oh1[:], oh1.b, les[:], les.b, bc(m1[:], [128, NT, 8], 2), m1.b, ALU.is_equal)
        STT("dve", les2[:], les2.b, oh1[:], oh1.b, -1e30, les[:], les.b, ALU.mult, ALU.add)
        P.op("dve", lambda e: e.tensor_reduce(out=m2[:], in_=les2[:], axis=AX.X, op=ALU.max), reads=[les2.b], writes=[m2.b])
        TT("dve", oh2[:], oh2.b, les2[:], les2.b, bc(m2[:], [128, NT, 8], 2), m2.b, ALU.is_equal)
        TT("dve", m2[:], m2.b, m1[:], m1.b, m2[:], m2.b, ALU.subtract)
        ACT(m2[:], m2.b, m2[:], m2.b, AF.Sigmoid)
        TT("dve", gt1[:], gt1.b, pgrp[:], pgrp.b, m2[:], m2.b, ALU.mult)
        TT("dve", gt2[:], gt2.b, pgrp[:], pgrp.b, gt1[:], gt1.b, ALU.subtract)
        TT("dve", A1[:], A1.b, bc(gh[:], [128, NT, 4, 8], 3), gh.b, bc(oh1[:], [128, NT, 4, 8], 2), oh1.b, ALU.mult)
        TT("dve", A2[:], A2.b, bc(gh[:], [128, NT, 4, 8], 3), gh.b, bc(oh2[:], [128, NT, 4, 8], 2), oh2.b, ALU.mult)
        A1f = A1[:].rearrange("p t g e -> p t (g e)"); A2f = A2[:].rearrange("p t g e -> p t (g e)")
        TT("dve", Ab[:], Ab.b, A1f, A1.b, A2f, A2.b, ALU.add)
        ppos = nextp()
        for t in range(NT):
            for tp in range(t):
                MMl(ppos[:, t * 32:(t + 1) * 32], ppos.b, ONB, b_cst, Ab[:, tp, :], Ab.b, start=(tp == 0), stop=False)
            MMl(ppos[:, t * 32:(t + 1) * 32], ppos.b, USB, b_cst, Ab[:, t, :], Ab.b, start=(t == 0), stop=True)
        CP("dve", pos[:], pos.b, ppos[:].rearrange("p (t e) -> p t e", t=NT), ppos.b)
        TT("dve", pe_[:], pe_.b, pos[:], pos.b, bc(ecb[:], [128, NT, 32], 1), ecb.b, ALU.add)
        for k, (Af, Ab_) in enumerate(((A1f, A1.b), (A2f, A2.b))):
            TT("dve", t32[:], t32.b, pe_[:], pe_.b, Af, Ab_, ALU.mult)
            RED(slf[:, k, :], slf.b, t32[:], t32.b)
            TT("dve", t32[:], t32.b, pos[:], pos.b, Af, Ab_, ALU.mult)
            RED(pv_[:, k, :], pv_.b, t32[:], t32.b)
        TS("dve", val[:], val.b, pv_[:], pv_.b, float(CAP) - 0.5, None, ALU.is_lt)
        TT("dve", slf[:], slf.b, slf[:], slf.b, val[:], val.b, ALU.mult)
        TS("dve", pv_[:], pv_.b, val[:], val.b, -float(NE * CAP), float(NE * CAP), ALU.mult, ALU.add)
        TT("dve", slf[:], slf.b, slf[:], slf.b, pv_[:], pv_.b, ALU.add)
        CP("dve", idx[:], idx.b, slf[:], slf.b)
        TT("dve", gt1[:], gt1.b, gt1[:], gt1.b, val[:, 0, :], val.b, ALU.mult)
        TT("dve", gt2[:], gt2.b, gt2[:], gt2.b, val[:, 1, :], val.b, ALU.mult)
        if dbg:
            dump("idx", idx[:], [128, 2, NT], [idx.b], I32)
            dump("gt1", gt1[:], [128, NT], [gt1.b])
            dump("gt2", gt2[:], [128, NT], [gt2.b])
        xbt = [loc("xbt%d" % i, [128, D], BF16) for i in range(2)]
        assert loc_off[0] <= 140 * KB, loc_off[0]
        P.fence()
        for t in range(NT):
            xb_ = xbt[t % 2]
            CP("act" if t % 2 else "dve", xb_[:], xb_.b, h1[:, t, :], h1.bs[t])
            for k in range(2):
                P.dma("pool", lambda e, t=t, k=k, xb_=xb_: e.indirect_dma_start(
                    out=xpad, out_offset=bass.IndirectOffsetOnAxis(ap=idx[:, k, t:t + 1], axis=0),
                    in_=xb_[:], in_offset=None, oob_is_err=False),
                    reads=[xb_.b, idx.b], writes=[b_xpad])

        new_stage(0)
        def ex_alloc(sx, si_):
            X = NS()
            X.wA = [PF[(si_, "wA")], loc("wA1" + sx, [128, 8, DF], BF16)]
            X.wB = [PF[(si_, "wB")], loc("wB1" + sx, [128, 8, DF], BF16)]
            X.wC = [PF[(si_, "wC")], loc("wC1" + sx, [128, 4, D], BF16)]
            X.xe = [loc("xe%d" % i + sx, [128, 2, D], BF16) for i in range(2)]
            X.xeT = [loc("xeT%d" % i + sx, [128, 8, CAP], BF16) for i in range(2)]
            X.hs = loc("hs" + sx, [128, CAP]); X.hbT = loc("hbT" + sx, [128, 4, CAP], BF16)
            X.yo = [loc("yo%d" % i + sx, [128, 2, D]) for i in range(2)]
            return X
        EX = [ex_alloc("_a", 0), ex_alloc("_b", 1)]
        assert loc_off[0] <= 140 * KB, loc_off[0]
        b_ypt = [Buf("ypad%d" % i) for i in range(NE)]

        def expert(ex, si):
            X = EX[si]
            wA = X.wA; wB = X.wB; wC = X.wC; xe = X.xe; xeT = X.xeT; hs = X.hs; hbT = X.hbT; yo = X.yo
            k = (ex // 2) % 2
            if ex >= 2:
                LD(wA[k][:], wA[k].b, w1[ex].rearrange("(kc p) f -> p kc f", p=128), eng="pool")
                LD(wB[k][:], wB[k].b, w3[ex].rearrange("(kc p) f -> p kc f", p=128), eng="pool")
                LD(wC[k][:], wC[k].b, w2[ex].rearrange("(fc p) d -> p fc d", p=128), eng="pool")
            LD(xe[k][:], xe[k].b, xpad[ex * CAP:(ex + 1) * CAP, :].rearrange("(s p) d -> p s d", p=128), reads=[b_xpad])
            for s in range(2):
                for half in range(2):
                    pbt = nextpb()
                    for j in range(4):
                        kc = half * 4 + j
                        TR(pbt[:, j * 128:(j + 1) * 128], pbt.b, xe[k][:, s, kc * 128:(kc + 1) * 128], xe[k].b, IDB)
                    CP("act" if half else "dve", xeT[k][:, half * 4:half * 4 + 4, s * 128:(s + 1) * 128], xeT[k].b,
                       pbt[:, 0:512].rearrange("p (j c) -> p j c", j=4), pbt.b)
            for fc in range(4):
                pa = nextp()
                for kc in range(8):
                    MMl(pa[:, 0:CAP], pa.b, wA[k][:, kc, fc * 128:(fc + 1) * 128], wA[k].b, xeT[k][:, kc, :], xeT[k].b,
                        start=(kc == 0), stop=(kc == 7))
                for kc in range(8):
                    MMl(pa[:, CAP:2 * CAP], pa.b, wB[k][:, kc, fc * 128:(fc + 1) * 128], wB[k].b, xeT[k][:, kc, :], xeT[k].b,
                        start=(kc == 0), stop=(kc == 7))
                ACT(hs[:], hs.b, pa[:, 0:CAP], pa.b, AF.Silu)
                TT("dve", hbT[:, fc, :], hbT.b, hs[:], hs.b, pa[:, CAP:2 * CAP], pa.b, ALU.mult)
            for s in range(2):
                for dh in range(2):
                    py = nextp()
                    for fc in range(4):
                        MMl(py[:], py.b, hbT[:, fc, s * 128:(s + 1) * 128], hbT.b, wC[k][:, fc, dh * 512:(dh + 1) * 512], wC[k].b,
                            start=(fc == 0), stop=(fc == 3))
                    CP("act" if dh else "dve", yo[k][:, s, dh * 512:(dh + 1) * 512], yo[k].b, py[:], py.b)
            STO(ypad[ex * CAP:(ex + 1) * CAP, :].rearrange("(s p) d -> p s d", p=128), b_ypt[ex], yo[k][:], yo[k].b)

        par2(expert, NE)

        new_stage(0)
        rp = load_rows([R_L2G, R_L2B])
        y1 = [loc("y1%d" % i, [128, D]) for i in range(4)]
        y2 = [loc("y2%d" % i, [128, D]) for i in range(4)]
        hrr = [loc("hrr%d" % i, [128, D]) for i in range(4)]
        r2S = [loc("r2%d" % i, [128, D]) for i in range(4)]; ot = [loc("ot%d" % i, [128, D]) for i in range(4)]
        for yy in y1 + y2:
            P.op("dve", lambda e, yy=yy: e.memset(yy[:], 0.0), writes=[yy.b])
        out_v = out_d.rearrange("(t p) d -> t p d", p=128)
        b_outt = [Buf("out%d" % i) for i in range(NT)]

        def st5c(t, si):
            a = si; r2 = r2S[si]
            LD(hrr[a][:], hrr[a].b, h1_v[t], reads=[b_h1t[t]])
            for k, yy in enumerate((y1[a], y2[a])):
                P.dma("pool", lambda e, t=t, k=k, yy=yy: e.indirect_dma_start(
                    out=yy[:], out_offset=None, in_=ypad,
                    in_offset=bass.IndirectOffsetOnAxis(ap=idx[:, k, t:t + 1], axis=0),
                    oob_is_err=False),
                    reads=[b_ypad, idx.b], writes=[yy.b])
            ACT(r2[:], r2.b, hrr[a][:], hrr[a].b, AF.Copy, scale=ALPHA)
            STT("dve", r2[:], r2.b, y1[a][:], y1[a].b, gt1[:, t:t + 1], r2[:], r2.b, ALU.mult, ALU.add, xr=[gt1.b])
            STT("dve", r2[:], r2.b, y2[a][:], y2[a].b, gt2[:, t:t + 1], r2[:], r2.b, ALU.mult, ALU.add, xr=[gt2.b])
            layer_norm(r2[:], r2.b, ot[a][:], ot[a].b, rp[:, 0, :], rp[:, 1, :], rp.b, r2[:], r2.b, c=si)
            out_bufs.append(STO(out_v[t], b_outt[t], ot[a][:], ot[a].b))
        par2(st5c, NT, 4)
        P.emit(final_waits=out_bufs)
    return nc, dbg_out


_CACHE = {}


def _consts():
    c = np.zeros((128, 9, 128), np.float32)
    i = np.arange(128)
    same = (i[:, None] // 64) == (i[None, :] // 64)
    c[:, 0] = np.eye(128)
    c[:, 1] = (same & (i[:, None] <= i[None, :]))
    c[:, 2] = (i[:, None] == 63) * np.ones((1, 128))
    c[:, 3] = (i[:, None] == 127) * np.ones((1, 128))
    c[:, 4] = np.where(same & (i[None, :] >= i[:, None]), 0.0, NEG)
    c[:, 5] = np.where(same & (i[None, :] <= i[:, None]), 0.0, NEG)
    c[:, 6] = 1.0 - np.eye(128)
    c[:, 7] = ((i[None, :] // 64) <= (i[:, None] // 64))
    c[:, 8] = (i[:, None] < i[None, :])
    selh = np.zeros((8, 16, 128), np.float32)
    for h in range(8):
        selh[h, h, :] = 1.0
        selh[h, 8 + h, :] = -1.0
    ecol = (np.arange(32, dtype=np.float32) * CAP).reshape(1, 32)
    return c, selh, ecol


def kernel(x, ln_in_g, ln_in_b, w_in, b_in, gm_ln_g, gm_ln_b, gm_w_s, gm_b_s, dn_conv_w, dn_a_log, dn_dt_bias,
           dn_norm_w, w_pa, w_pb, w_o, ln1_g, ln1_b, w_rg, b_rg, w_re, b_re, w1, w3, w2, ln2_g, ln2_b):
    f = lambda a: np.ascontiguousarray(np.asarray(a, dtype=np.float32))
    if "nc" not in _CACHE:
        _CACHE["nc"] = build()[0]
    nc = _CACHE["nc"]
    b_in0 = f(b_in)[0]
    rowp = np.stack([f(ln_in_g), f(ln_in_b), b_in0[1024:2048], f(gm_ln_g)[0], f(gm_ln_b)[0], f(ln1_g)[0], f(ln1_b)[0],
                     f(ln2_g)[0], f(ln2_b)[0], f(gm_b_s)[0].reshape(-1)], 0)
    colp = np.zeros((128, 96), np.float32)
    colp[:, 0:48] = b_in0[0:6144].reshape(48, 128).T
    colp[:, 48:56] = b_in0[6160:7184].reshape(8, 128).T
    colp[:, 56:64] = b_in0[7184:8208].reshape(8, 128).T
    cw = np.ascontiguousarray(f(dn_conv_w)[0].reshape(4, 24, 128).transpose(2, 1, 0))
    cst, selh, ecol = _consts()
    shared = {
        "w_in": f(w_in)[0], "w_pa": f(w_pa)[0], "w_pb": f(w_pb)[0], "w_o": f(w_o)[0],
        "w1": f(w1)[0], "w3": f(w3)[0], "w2": f(w2)[0], "gm_w_s": f(gm_w_s)[0],
        "rowp": np.ascontiguousarray(rowp), "colp": colp, "cw": cw,
        "ab_b": np.ascontiguousarray(b_in0[6144:6160].reshape(1, 16)),
        "hp": np.concatenate([f(dn_a_log)[0], f(dn_dt_bias)[0]]).reshape(1, 16),
        "nw_col": f(dn_norm_w)[0].reshape(128, 1),
        "wr": np.ascontiguousarray(np.concatenate([f(w_rg)[0], f(w_re)[0]], 1)),
        "br": np.concatenate([f(b_rg)[0], f(b_re)[0]]).reshape(1, 36),
        "cst": cst, "selh": selh, "ecol": ecol,
    }
    xs = f(x)
    in_maps = [dict(shared, x=xs[b]) for b in range(8)]
    res = run_bass_kernel_spmd(nc, in_maps, core_ids=list(range(8)))
    return np.stack([np.asarray(r["out"], dtype=np.float32) for r in res.results], 0)
```

```python
import contextlib
import numpy as np
import concourse.bass as bass
import concourse.mybir as mybir
from concourse.bass_utils import run_bass_kernel_spmd

F32 = mybir.dt.float32
BF16 = mybir.dt.bfloat16
I32 = mybir.dt.int32
AF = mybir.ActivationFunctionType
ALU = mybir.AluOpType
AX = mybir.AxisListType

T = 2048
D = 1024
NT = 16
NH = 8
INW = 8208
NE = 32
REC_STEP = 2.5
REC_OFF = 0.5
LN_ADD_ENG = "dve"
CAP = 256
DF = 512
ALPHA = 2 ** 0.25
NEG = -60000.0


class Buf:
    __slots__ = ("name", "last_w", "readers")

    def __init__(self, name=""):
        self.name = name
        self.last_w = None
        self.readers = {}


class Op:
    __slots__ = ("eng", "fn", "deps", "key", "signal", "semval", "dma", "dslot", "dval", "prewait", "fence", "pos")


ENGS = ("pe", "act", "dve", "pool", "sp")
DMA_K = 6


class Prog:
    def __init__(self, nc):
        self.nc = nc
        self.ops = {e: [] for e in ENGS}
        self.phase = 0
        self.seq = 0
        self.par = False
        self.sid = 0
        self.cnt = {}
        self.step = {}

    def begin_parallel(self):
        self.phase += 1
        self.par = True
        self.cnt = {}
        self.step = {}

    def set_stream(self, sid):
        self.sid = sid

    def end_parallel(self):
        self.phase += 1
        self.par = False
        self.sid = 0

    def _key(self):
        if self.par:
            k = self.cnt.get(self.sid, 0)
            self.cnt[self.sid] = k + self.step.get(self.sid, 1)
            return (self.phase, k, self.sid)
        self.seq += 1
        return (self.phase, self.seq, 0)

    def _deps(self, reads, writes):
        deps = set()
        for b in reads:
            if b.last_w is not None:
                deps.add(b.last_w)
        for b in writes:
            if b.last_w is not None:
                deps.add(b.last_w)
            for r in b.readers.values():
                deps.add(r)
        return deps

    def _mark(self, op, reads, writes):
        for b in reads:
            if op.dma:
                b.readers[("d" + op.eng, id(op))] = op
            else:
                b.readers[op.eng] = op
        for b in writes:
            b.last_w = op
            b.readers = {}

    @staticmethod
    def _flat(xs):
        out = []
        for x in xs:
            if isinstance(x, (list, tuple)):
                out.extend(Prog._flat(x))
            else:
                out.append(x)
        return out

    def _new(self, eng, fn, dma, reads, writes):
        reads = self._flat(reads)
        writes = self._flat(writes)
        o = Op()
        o.eng, o.fn, o.dma, o.fence = eng, fn, dma, False
        o.deps = self._deps(reads, writes)
        o.key = self._key()
        o.signal = dma
        o.semval = None
        o.prewait = None
        self.ops[eng].append(o)
        self._mark(o, reads, writes)
        return o

    def op(self, eng, fn, reads=(), writes=()):
        return self._new(eng, fn, False, reads, writes)

    def dma(self, eng, fn, reads=(), writes=()):
        return self._new(eng, fn, True, reads, writes)

    def fence(self):
        assert not self.par
        self.phase += 1
        for e in ENGS:
            o = Op()
            o.eng, o.fn, o.dma, o.fence = e, None, False, True
            o.deps = set()
            o.key = self._key()
            o.signal = False
            o.semval = None
            o.prewait = None
            self.ops[e].append(o)
        self.phase += 1

    def emit(self, final_waits=()):
        nc = self.nc
        for e in ENGS:
            self.ops[e].sort(key=lambda o: o.key)
        last_comp = {e: None for e in ENGS}
        last_dmas = {e: [] for e in ENGS}
        merged = sorted((o for e in ENGS for o in self.ops[e]), key=lambda o: o.key)
        i = 0
        while i < len(merged):
            o = merged[i]
            if o.fence:
                deps = set()
                for e in ENGS:
                    if last_comp[e] is not None:
                        deps.add(last_comp[e])
                    deps.update(last_dmas[e][-DMA_K:])
                j = i
                while j < len(merged) and merged[j].fence:
                    merged[j].deps = set(deps)
                    j += 1
                i = j
                continue
            if o.dma:
                last_dmas[o.eng].append(o)
            else:
                last_comp[o.eng] = o
            i += 1
        for e in ENGS:
            hist = [o for o in self.ops[e] if o.dma]
            for n, o in enumerate(hist):
                o.dslot = n % DMA_K
                o.dval = 16 * (n // DMA_K + 1)
                o.prewait = hist[n - DMA_K] if n >= DMA_K else None
        ndma = {e: sum(1 for o in self.ops[e] if o.dma) for e in ENGS}
        for e in ENGS:
            for o in self.ops[e]:
                for d in o.deps:
                    assert d.key < o.key, ("dependency ordered after dependent", d.eng, d.key, o.eng, o.key)
        for e in ENGS:
            for o in self.ops[e]:
                for d in o.deps:
                    if not d.dma:
                        if d.eng == "pe" and o.eng == "pe" and not o.dma and not o.fence:
                            continue
                        d.signal = True
        for e in ENGS:
            c = 0
            for o in self.ops[e]:
                if o.dma or o.fence:
                    continue
                if o.signal:
                    c += 1
                    o.semval = c
        with contextlib.ExitStack() as st:
            sem = {e: st.enter_context(nc.semaphore("s_" + e)) for e in ENGS}
            dsem = {e: [st.enter_context(nc.semaphore("d_%s%d" % (e, k))) for k in range(DMA_K)]
                    for e in ENGS if ndma[e] > 0}
            block = st.enter_context(nc.Block())

            def run(e, engine):
                waited = {}

                def wait_for(d):
                    if d.dma:
                        key = ("d", d.eng, d.dslot)
                        s, v = dsem[d.eng][d.dslot], d.dval
                    else:
                        key = d.eng
                        s, v = sem[d.eng], d.semval
                    if waited.get(key, 0) >= v:
                        return
                    waited[key] = v
                    engine.wait_ge(s, v)

                for o in self.ops[e]:
                    for d in sorted(o.deps, key=lambda d: d.key):
                        if d.eng == "pe" and e == "pe" and not d.dma and not o.dma and not o.fence:
                            continue
                        wait_for(d)
                    if o.fence:
                        continue
                    if o.dma:
                        if o.prewait is not None:
                            wait_for(o.prewait)
                        o.fn(engine).then_inc(dsem[e][o.dslot], 16)
                    else:
                        ins = o.fn(engine)
                        if o.signal:
                            ins.then_inc(sem[e], 1)
                if e == "sp":
                    for d in final_waits:
                        wait_for(d)

            @block.tensor
            def _(eng):
                run("pe", eng)

            @block.scalar
            def _(eng):
                run("act", eng)

            @block.vector
            def _(eng):
                run("dve", eng)

            @block.gpsimd
            def _(eng):
                run("pool", eng)

            @block.sync
            def _(eng):
                run("sp", eng)


class Tl:
    def __init__(self, h, name, nb=1):
        self.h = h
        self.b = Buf(name)
        self.bs = [Buf(name + str(i)) for i in range(nb)] if nb > 1 else [self.b]

    def __getitem__(self, k):
        return self.h[k]


def build(stop_after=99, dbg=False):
    nc = bass.Bass("TRN2", target_bir_lowering=False)
    P = Prog(nc)

    def din(name, shape, dt=F32):
        return nc.dram_tensor(name, list(shape), dt, kind="ExternalInput").ap()

    x_d = din("x", [T, D])
    w_in = din("w_in", [D, INW])
    w_pa = din("w_pa", [D, D]); w_pb = din("w_pb", [D, D]); w_o = din("w_o", [D, D])
    w1 = din("w1", [NE, D, DF]); w3 = din("w3", [NE, D, DF]); w2 = din("w2", [NE, DF, D])
    gm_ws = din("gm_w_s", [8, 128, 128])
    rowp = din("rowp", [10, D])
    colp = din("colp", [128, 96])
    ab_b = din("ab_b", [1, 16])
    hp = din("hp", [1, 16])
    nw_col = din("nw_col", [128, 1])
    wr = din("wr", [D, 36])
    br = din("br", [1, 36])
    cst = din("cst", [128, 9, 128])
    cw_d = din("cw", [128, 24, 4])
    selh_d = din("selh", [8, 16, 128])
    ecol_d = din("ecol", [1, 32])
    out_d = nc.dram_tensor("out", [T, D], F32, kind="ExternalOutput").ap()
    h0_s = nc.dram_tensor("h0_s", [T, D], F32).ap()
    xpad = nc.dram_tensor("xpad", [NE * CAP + 1, D], BF16).ap()
    ypad = nc.dram_tensor("ypad", [NE * CAP + 1, D], F32).ap()
    mT_s = nc.dram_tensor("mT_s", [8, 128, T], BF16).ap()
    yb_s = nc.dram_tensor("yb_s", [8, 128, T], BF16).ap()
    b_ybs = [Buf("yb_s%d" % i) for i in range(8)]
    h1_s = nc.dram_tensor("h1_s", [T, D], F32).ap()
    b_h0 = Buf("h0_s"); b_xpad = Buf("xpad"); b_ypad = Buf("ypad"); b_mT = Buf("mT_s"); b_h1s = Buf("h1_s")
    dbg_out = {}
    out_bufs = []

    with contextlib.ExitStack() as st:
        def sb(name, shape, dt=F32, nb=1):
            return Tl(st.enter_context(nc.sbuf_tensor(name, list(shape), dt)), name, nb)

        def ps(name, shape, dt=F32):
            return Tl(st.enter_context(nc.psum_tensor(name, list(shape), dt)), name)

        def dump(name, src_ap, shape, bufs, dt=F32):
            if not dbg:
                return
            d = nc.dram_tensor("dbg_" + name, list(shape), dt, kind="ExternalOutput").ap()
            dbg_out[name] = d
            b = Buf("dbg_" + name)
            out_bufs.append(P.dma("sp", lambda e: e.dma_start(out=d, in_=src_ap), reads=bufs, writes=[b]))

        def MM(o, ob, l, lb, r, rb, start=True, stop=True):
            return P.op("pe", lambda e: e.matmul(o, lhsT=l, rhs=r, start=start, stop=stop),
                        reads=[lb, rb], writes=[ob])

        def TR(o, ob, i, ib, idt):
            return P.op("pe", lambda e: e.transpose(out=o, in_=i, identity=idt), reads=[ib, b_cst], writes=[ob])

        def ACT(o, ob, i, ib, func, bias=None, scale=None, xr=()):
            kw = {}
            if bias is not None:
                kw["bias"] = bias
            if scale is not None:
                kw["scale"] = scale
            return P.op("act", lambda e: e.activation(out=o, in_=i, func=func, **kw),
                        reads=[ib] + list(xr), writes=[ob])

        def CP(eng, o, ob, i, ib):
            if eng == "act":
                return P.op("act", lambda e: e.copy(out=o, in_=i), reads=[ib], writes=[ob])
            return P.op(eng, lambda e: e.tensor_copy(out=o, in_=i), reads=[ib], writes=[ob])

        def TT(eng, o, ob, a, ab, b, bb, op):
            return P.op(eng, lambda e: e.tensor_tensor(out=o, in0=a, in1=b, op=op), reads=[ab, bb], writes=[ob])

        def TS(eng, o, ob, a, ab, s1, s2, op0, op1=None, xr=()):
            if op1 is None:
                return P.op(eng, lambda e: e.tensor_scalar(out=o, in0=a, scalar1=s1, scalar2=None, op0=op0),
                            reads=[ab] + list(xr), writes=[ob])
            return P.op(eng, lambda e: e.tensor_scalar(out=o, in0=a, scalar1=s1, scalar2=s2, op0=op0, op1=op1),
                        reads=[ab] + list(xr), writes=[ob])

        def STT(eng, o, ob, a, ab, s, b, bb, op0, op1, xr=()):
            return P.op(eng, lambda e: e.scalar_tensor_tensor(out=o, in0=a, scalar=s, in1=b, op0=op0, op1=op1),
                        reads=[ab, bb] + list(xr), writes=[ob])

        def RED(o, ob, i, ib, op=ALU.add):
            return P.op("dve", lambda e: e.tensor_reduce(out=o, in_=i, axis=AX.X, op=op), reads=[ib], writes=[ob])

        def LD(o, ob, src, eng="sp", reads=()):
            return P.dma(eng, lambda e: e.dma_start(out=o, in_=src), reads=list(reads), writes=[ob])

        def STO(dst, dstb, i, ib, eng="sp"):
            return P.dma(eng, lambda e: e.dma_start(out=dst, in_=i), reads=[ib], writes=[dstb])


        cstf = sb("cstf", [128, 9, 128]); b_cst = cstf.b
        LD(cstf[:], b_cst, cst)
        IDF, CUMT, SEL0, SEL1, NEGT, NEGL, OFFD, CMASK, USTR = [cstf[:, i, :] for i in range(9)]
        cstb = sb("cstb", [128, 3, 128], BF16)
        LD(cstb[:, 0, :], b_cst, cst[:, 0, :], eng="pool")
        LD(cstb[:, 2, :], b_cst, cst[:, 8, :], eng="pool")
        onesf = sb("onesf", [128, 128])
        P.op("dve", lambda e: e.memset(onesf[:], 1.0), writes=[onesf.b])
        P.op("dve", lambda e: e.tensor_copy(out=cstb[:, 1, :], in_=onesf[:]), reads=[onesf.b, b_cst], writes=[b_cst])
        IDB = cstb[:, 0, :]; ONB = cstb[:, 1, :]; USB = cstb[:, 2, :]
        selh = sb("selh_sb", [8, 16, 128])
        LD(selh[:], selh.b, selh_d)
        cp = sb("cp", [128, 96]); LD(cp[:], cp.b, colp)
        cw = sb("cw_sb", [128, 24, 4]); LD(cw[:], cw.b, cw_d)
        nwc = sb("nwc", [128, 1]); LD(nwc[:], nwc.b, nw_col)
        epsb = sb("epsb", [128, 4])
        P.op("dve", lambda e: e.memset(epsb[:, 0:1], 1e-5), writes=[epsb.b])
        P.op("dve", lambda e: e.memset(epsb[:, 1:2], 1e-6), reads=[epsb.b], writes=[epsb.b])
        P.op("dve", lambda e: e.memset(epsb[:, 2:3], 1.0), reads=[epsb.b], writes=[epsb.b])
        P.op("dve", lambda e: e.memset(epsb[:, 3:4], 0.0), reads=[epsb.b], writes=[epsb.b])
        EPS_LN, EPS_6, ONE_C, ZERO_C = [epsb[:, i:i + 1] for i in range(4)]
        idx = sb("idx", [128, 2, NT], I32); gt1 = sb("gt1", [128, NT]); gt2 = sb("gt2", [128, NT])
        lnS = [(sb("lnst%d" % i, [128, 2, 6]), sb("lnmv%d" % i, [128, 2]), sb("lnr%d" % i, [128, 1]),
                sb("lnb%d" % i, [128, 1])) for i in range(4)]

        pq = [ps("pq%d" % i, [128, 512]) for i in range(8)]
        pqb = []
        for i in range(8):
            v = Tl(pq[i][:].bitcast(BF16), "pqb%d" % i)
            v.b = pq[i].b
            v.bs = [v.b]
            pqb.append(v)
        pctx = {"banks": list(range(8)), "i": 0}
        pstate = {}

        def use_banks(banks, tag=None):
            if tag is not None and tag in pstate:
                pctx.update(pstate[tag])
            else:
                pctx["banks"] = list(banks)
                pctx["i"] = 0
            pctx["tag"] = tag

        def save_banks():
            if pctx.get("tag") is not None:
                pstate[pctx["tag"]] = {"banks": pctx["banks"], "i": pctx["i"]}

        def nextp():
            pctx["i"] = (pctx["i"] + 1) % len(pctx["banks"])
            return pq[pctx["banks"][pctx["i"]]]

        def nextpb():
            pctx["i"] = (pctx["i"] + 1) % len(pctx["banks"])
            return pqb[pctx["banks"][pctx["i"]]]

        def par2(fn, n, ns=2):
            P.begin_parallel()
            nb_ = 8 // ns
            for si in range(ns):
                P.set_stream(si)
                use_banks([nb_ * si + i for i in range(nb_)])
                for t in range(si, n, ns):
                    fn(t, si)
            P.end_parallel()
            use_banks(range(8))

        ARW = 49152
        arena = st.enter_context(nc.sbuf_tensor("arena", [128, ARW], F32))
        loc_off = [0]

        def av(name, off, shape, dt=F32, nb=1):
            n = int(np.prod(shape[1:]))
            nbytes = n * (4 if dt in (F32, I32) else 2)
            assert off % 4 == 0 and off + nbytes <= ARW * 4, (name, off, nbytes)
            ap = arena[0:shape[0], off // 4:(off + nbytes + 3) // 4]
            if dt != F32:
                ap = ap.bitcast(dt)
            if dt == BF16 and nbytes % 4:
                ap = ap[:, 0:n]
            if len(shape) == 3:
                ap = ap.rearrange("p (a b) -> p a b", a=shape[1])
            elif len(shape) == 4:
                ap = ap.rearrange("p (a b c) -> p a b c", a=shape[1], b=shape[2])
            return Tl(ap, name, nb)

        def loc(name, shape, dt=F32, nb=1):
            n = int(np.prod(shape[1:]))
            nbytes = (n * (4 if dt in (F32, I32) else 2) + 3) // 4 * 4
            t_ = av(name, loc_off[0], shape, dt, nb)
            loc_off[0] += nbytes
            return t_

        def new_stage(base_kb):
            P.fence()
            loc_off[0] = base_kb * 1024

        KB = 1024
        hT = av("hT", 0, [128, 8, T], BF16, nb=16)
        ybT = av("ybT", 32 * KB, [128, 8, T], BF16, nb=8)
        yaT = av("yaT", 64 * KB, [128, 8, T], BF16, nb=16)

        def load_rows(rows):
            rp_ = loc("rp", [128, len(rows), D])
            for i, r in enumerate(rows):
                LD(rp_[:, i, :], rp_.b, rowp[r:r + 1, :].partition_broadcast(128))
            return rp_
        (R_LNG, R_LNB, R_BV, R_GMG, R_GMB, R_L1G, R_L1B, R_L2G, R_L2B, R_BS) = range(10)

        def layer_norm(src, srcb, dst, dstb, g_ap, b_ap, gb_buf, tmp, tmpb, c=0):
            lnst, lnmv, lnr, lnb = lnS[c]
            def f(e):
                e.bn_stats(out=lnst[:, 0, :], in_=src[:, 0:512])
                return e.bn_stats(out=lnst[:, 1, :], in_=src[:, 512:1024])
            P.op("dve", f, reads=[srcb], writes=[lnst.b])
            P.op("dve", lambda e: e.bn_aggr(out=lnmv[:], in_=lnst[:]), reads=[lnst.b], writes=[lnmv.b])
            ACT(lnr[:], lnr.b, lnmv[:, 1:2], lnmv.b, AF.Sqrt, bias=EPS_LN, scale=1.0, xr=[epsb.b])
            P.op("dve", lambda e: e.reciprocal(out=lnr[:], in_=lnr[:]), reads=[lnr.b], writes=[lnr.b])
            STT("dve", lnb[:], lnb.b, lnmv[:, 0:1], lnmv.b, -1.0, lnr[:], lnr.b, ALU.mult, ALU.mult)
            ACT(tmp, tmpb, src, srcb, AF.Identity, bias=lnb[:, 0:1], scale=lnr[:, 0:1], xr=[lnb.b, lnr.b])
            TT("dve", tmp, tmpb, tmp, tmpb, g_ap, gb_buf, ALU.mult)
            TT(LN_ADD_ENG, dst, dstb, tmp, tmpb, b_ap, gb_buf, ALU.add)

        loc_off[0] = 96 * KB
        rp = load_rows([R_LNG, R_LNB])
        zt = loc("zt", [128, 8192], BF16)
        P.op("dve", lambda e: e.memset(zt[:], 0.0), writes=[zt.b])
        xz = xpad[0:NE * CAP, :].rearrange("(p a) d -> p (a d)", p=128)
        LD(ypad[NE * CAP:NE * CAP + 1, :], b_ypad, zt[0:1, 0:1024], eng="pool", reads=[zt.b])
        LD(xpad[NE * CAP:NE * CAP + 1, :], b_xpad, zt[0:1, 0:1024], reads=[zt.b])
        for i in range(8):
            P.dma("pool", lambda e, i=i: e.dma_start(out=xz[:, i * 8192:(i + 1) * 8192], in_=zt[:]),
                  reads=[zt.b], writes=[b_xpad])
        xin = [loc("xin%d" % i, [128, D]) for i in range(4)]
        h0t = [loc("h0t%d" % i, [128, D]) for i in range(4)]
        x_v = x_d.rearrange("(t p) d -> t p d", p=128)
        h0_v = h0_s.rearrange("(t p) d -> t p d", p=128)
        b_h0t = [Buf("h0s%d" % i) for i in range(NT)]

        def st1(t, si):
            xi = xin[si]; ho = h0t[si]
            LD(xi[:], xi.b, x_v[t])
            layer_norm(xi[:], xi.b, ho[:], ho.b, rp[:, 0, :], rp[:, 1, :], rp.b, xi[:], xi.b, c=si)
            STO(h0_v[t], b_h0t[t], ho[:], ho.b)
            for half in range(2):
                pt = nextp()
                for j in range(4):
                    kc = half * 4 + j
                    TR(pt[:, j * 128:(j + 1) * 128], pt.b, ho[:, kc * 128:(kc + 1) * 128], ho.b, IDF)
                CP("act", hT[:, half * 4:half * 4 + 4, t * 128:(t + 1) * 128], hT.bs[t],
                   pt[:].rearrange("p (j c) -> p j c", j=4), pt.b)
        par2(st1, NT, 4)
        if dbg:
            dump("hT", hT[:], [128, 8, T], hT.bs, BF16)
        if stop_after <= 1:
            P.emit(final_waits=out_bufs)
            return nc, dbg_out

        def L(x):
            return list(x) if isinstance(x, (list, tuple)) else [x]

        def MMl(o, ob, l, lb, r, rb, start=True, stop=True):
            return P.op("pe", lambda e: e.matmul(o, lhsT=l, rhs=r, start=start, stop=stop),
                        reads=L(lb) + L(rb), writes=L(ob))


        def load_w(dst, src2d, off=0, n=512):
            LD(dst[:, :, off:off + n], dst.b, src2d.rearrange("(kc p) c -> p kc c", p=128), eng="pool")

        def seg_bufs(tl, seg):
            return tl.bs[seg * 4:(seg + 1) * 4]

        new_stage(32)
        beta = loc("beta", [128, NT, 8]); Gsb = loc("Gsb", [128, NT, 8])
        nG = loc("nG", [128, NT, 8]); eG = loc("eG", [128, NT, 8]); ekd = loc("ekd", [128, NT, 8])
        GT = loc("GT", [8, T]); cdt = loc("cdt", [128, NT, 2, 8])
        pre_live_end = loc_off[0]
        wab = loc("wab", [128, 8, 16], BF16)
        LD(wab[:], wab.b, w_in[:, 6144:6160].rearrange("(kc p) c -> p kc c", p=128), eng="pool")
        abb = loc("abb", [128, 16]); LD(abb[:], abb.b, ab_b.partition_broadcast(128))
        hpb = loc("hpb", [128, 16]); LD(hpb[:], hpb.b, hp.partition_broadcast(128))
        absb = loc("absb", [128, NT, 16])
        pt = nextp()
        for t in range(NT):
            for kc in range(8):
                MMl(pt[:, t * 16:(t + 1) * 16], pt.b, hT[:, kc, t * 128:(t + 1) * 128], hT.bs[t], wab[:, kc, :], wab.b,
                    start=(kc == 0), stop=(kc == 7))
        TT("dve", absb[:], absb.b, pt[:, 0:256].rearrange("p (t c) -> p t c", t=NT), pt.b,
           abb[:].unsqueeze(1).broadcast_to([128, NT, 16]), abb.b, ALU.add)
        gsb = loc("gsb", [128, NT, 8]); aexp = loc("aexp", [128, 8])
        glo = loc("glo", [128, NT, 8])
        ACT(beta[:], beta.b, absb[:, :, 8:16], absb.b, AF.Sigmoid)
        TT("dve", gsb[:], gsb.b, absb[:, :, 0:8], absb.b, hpb[:, 8:16].unsqueeze(1).broadcast_to([128, NT, 8]), hpb.b, ALU.add)
        ACT(gsb[:], gsb.b, gsb[:], gsb.b, AF.Exp)
        ACT(gsb[:], gsb.b, gsb[:], gsb.b, AF.Ln, bias=ONE_C, scale=1.0, xr=[epsb.b])
        ACT(aexp[:], aexp.b, hpb[:, 0:8], hpb.b, AF.Exp)
        STT("dve", gsb[:], gsb.b, gsb[:], gsb.b, -1.0, aexp[:].unsqueeze(1).broadcast_to([128, NT, 8]), aexp.b,
            ALU.mult, ALU.mult)
        pt = nextp()
        for t in range(NT):
            MMl(pt[:, t * 8:(t + 1) * 8], pt.b, CUMT, b_cst, gsb[:, t, :], gsb.b)
        CP("dve", Gsb[:], Gsb.b, pt[:, 0:128].rearrange("p (t c) -> p t c", t=NT), pt.b)
        for s4 in range(4):
            pt = nextp()
            for j in range(4):
                t = s4 * 4 + j
                MMl(pt[0:8, j * 128:(j + 1) * 128], pt.b, gsb[:, t, :], gsb.b, CUMT, b_cst)
            CP("dve", GT[:, s4 * 512:(s4 + 1) * 512], GT.b, pt[0:8, :], pt.b)
        pt = nextp()
        for t in range(NT):
            for c in range(2):
                MMl(pt[:, (t * 2 + c) * 8:(t * 2 + c + 1) * 8], pt.b, SEL0 if c == 0 else SEL1, b_cst, Gsb[:, t, :], Gsb.b)
        glf = loc("glf", [128, NT, 2, 8])
        CP("dve", glf[:], glf.b, pt[:, 0:256].rearrange("p (t c h) -> p t c h", t=NT, c=2), pt.b)
        ACT(cdt[:], cdt.b, glf[:], glf.b, AF.Exp)
        CP("dve", glo[0:64, :, :], glo.b, glf[0:64, :, 0, :], glf.b)
        CP("dve", glo[64:128, :, :], glo.b, glf[64:128, :, 1, :], glf.b)
        TT("dve", glo[:], glo.b, glo[:], glo.b, Gsb[:], Gsb.b, ALU.subtract)
        ACT(ekd[:], ekd.b, glo[:], glo.b, AF.Exp)
        ACT(eG[:], eG.b, Gsb[:], Gsb.b, AF.Exp)
        TS("dve", nG[:], nG.b, Gsb[:], Gsb.b, -1.0, None, ALU.mult)
        if dbg:
            dump("Gsb", Gsb[:], [128, NT, 8], [Gsb.b])
            dump("beta", beta[:], [128, NT, 8], [beta.b])
            dump("GT", GT[:], [8, T], [GT.b])

        P.fence()
        loc_off[0] = pre_live_end
        from types import SimpleNamespace as NS

        def gdn_alloc(sx):
            X = NS()
            X.wb = loc("wvh" + sx, [128, 8, 512], BF16)
            X.pr = loc("pre" + sx, [128, 3, 516], BF16)
            X.dg = loc("dg" + sx, [128, 12, 128], BF16)
            X.qkvT = [loc("qkvT%d" % i + sx, [128, 3, 512], BF16) for i in range(2)]
            X.szT = [loc("szT%d" % i + sx, [128, 512], BF16) for i in range(3)]
            X.tm = loc("tm" + sx, [128, 3, 4, 128], BF16, nb=3)
            X.sq = loc("sq" + sx, [128, 4, 128]); X.ssq = loc("ssq" + sx, [128, 2, 4]); X.rn = loc("rn" + sx, [128, 2, 4])
            X.scl = loc("scl" + sx, [128, 7, 4], nb=7)
            X.scd = [loc("scd%d" % i + sx, [128, 7, 4, 128], BF16, nb=7) for i in range(2)]
            X.fmT = [loc("fmT%d" % i + sx, [128, 4, 512], BF16, nb=4) for i in range(2)]
            X.DTm = loc("DTm" + sx, [128, 4, 128], nb=4); X.DLm = loc("DLm" + sx, [128, 4, 128], nb=4)
            X.SN = loc("SN" + sx, [128, 4, 128]); X.SL = loc("SL" + sx, [128, 4, 128]); X.Pm = loc("Pm" + sx, [128, 4, 128])
            X.ATb = [loc("ATb%d" % i + sx, [128, 4, 128], BF16) for i in range(2)]
            X.TTb = [loc("TTb%d" % i + sx, [128, 4, 128], BF16) for i in range(2)]
            X.wcT = [loc("wcT%d" % i + sx, [128, 512], BF16) for i in range(2)]
            X.S32 = [loc("S32%d" % i + sx, [128, 128]) for i in range(2)]
            X.Sbf = [loc("Sbf%d" % i + sx, [128, 128], BF16) for i in range(2)]
            X.vnb = loc("vnb" + sx, [128, 4, 128], BF16); X.osb = loc("osb" + sx, [128, 4, 128])
            X.onb = loc("onb" + sx, [128, 4, 128], BF16); X.orr = loc("orr" + sx, [128, 4])
            X.ybo = [loc("ybo%d" % i + sx, [128, 512], BF16) for i in range(1)]
            return X
        GX = [gdn_alloc("_a"), gdn_alloc("_b")]

        def gdn_proj_steps(h, X, seg):
            wb = X.wb; pr = X.pr; dg = X.dg
            qkvT = X.qkvT[seg % 2]; szT = X.szT[(seg + X.pq) % 3]
            sl = slice(seg * 512, (seg + 1) * 512)

            def pg(j):
                pt = nextp()
                for kc in range(8):
                    MMl(pt[:], pt.b, wb[:, kc, j * 128:(j + 1) * 128], wb.b, hT[:, kc, sl], seg_bufs(hT, seg),
                        start=(kc == 0), stop=(kc == 7))
                bcol = cp[:, 16 + j * 8 + h:16 + j * 8 + h + 1]
                if j < 3:
                    ACT(pr[:, j, 3:515], pr.b, pt[:], pt.b, AF.Identity, bias=bcol, scale=1.0, xr=[cp.b])
                else:
                    ACT(szT[:], szT.b, pt[:], pt.b, AF.Silu, bias=bcol, scale=1.0, xr=[cp.b])

            def cv(j):
                pt = nextp()
                for k in range(4):
                    MMl(pt[:], pt.b, dg[:, j * 4 + k, :], dg.b, pr[:, j, k:k + 512], pr.b, start=(k == 0), stop=(k == 3))
                ACT(qkvT[:, j, :], qkvT.b, pt[:], pt.b, AF.Silu)
                if j == 2:
                    CP("act", pr[:, :, 0:3], pr.b, pr[:, :, 512:515], pr.b)
            return [lambda j=j: pg(j) for j in range(4)] + [lambda j=j: cv(j) for j in range(3)]

        def gdn_proj(h, X, seg):
            for st_ in gdn_proj_steps(h, X, seg):
                st_()

        def gdn_decay(h, X, seg):
            DTm = X.DTm; DLm = X.DLm
            pN = nextp(); pL = nextp()
            for i in range(4):
                t = seg * 4 + i
                cs = slice(i * 128, (i + 1) * 128)
                gts = GT[:, t * 128:(t + 1) * 128]
                MMl(pN[:, cs], pN.b, selh[:, h, :], selh.b, gts, GT.b, start=True, stop=False)
                MMl(pN[:, cs], pN.b, IDF, b_cst, NEGT, b_cst, start=False, stop=True)
                MMl(pL[:, cs], pL.b, selh[:, 8 + h, :], selh.b, gts, GT.b, start=True, stop=False)
                MMl(pL[:, cs], pL.b, IDF, b_cst, NEGL, b_cst, start=False, stop=True)
            for i in range(4):
                t = seg * 4 + i
                cs = slice(i * 128, (i + 1) * 128)
                ACT(DTm[:, i, :], DTm.bs[i], pN[:, cs], pN.b, AF.Exp, bias=nG[:, t, h:h + 1], scale=1.0, xr=[nG.b])
                ACT(DLm[:, i, :], DLm.bs[i], pL[:, cs], pL.b, AF.Exp, bias=Gsb[:, t, h:h + 1], scale=1.0, xr=[Gsb.b])
            TT("dve", DLm[:], DLm.bs, DLm[:], DLm.bs, OFFD.unsqueeze(1).broadcast_to([128, 4, 128]), b_cst, ALU.mult)

        def gdn_init(h, X):
            wb = X.wb; pr = X.pr; dg = X.dg; tm = X.tm; sq = X.sq; ssq = X.ssq
            rn = X.rn; scl = X.scl; DTm = X.DTm; DLm = X.DLm; SN = X.SN; SL = X.SL
            Pm = X.Pm; S32 = X.S32[X.pp]; Sbf = X.Sbf[X.pp]; vnb = X.vnb; osb = X.osb
            onb = X.onb; orr = X.orr; osq = X.onb
            for j in range(4):
                c0 = 2048 + j * 1024 + h * 128
                load_w(wb, w_in[:, c0:c0 + 128], off=j * 128, n=128)
            P.op("dve", lambda e: e.memset(S32[:], 0.0), writes=[S32.b])
            P.op("dve", lambda e: e.memset(Sbf[:], 0.0), writes=[Sbf.b])
            P.op("dve", lambda e: e.memset(pr[:, :, 0:3], 0.0), writes=[pr.b])
            for j in range(3):
                for k in range(4):
                    ACT(dg[:, j * 4 + k, :], dg.b, IDB, b_cst, AF.Copy, scale=cw[:, j * 8 + h, k:k + 1], xr=[cw.b])
            gdn_proj(h, X, 0)
            gdn_decay(h, X, 0)

        def gdn_prep(h, X, seg):
            wb = X.wb; pr = X.pr; dg = X.dg; tm = X.tm; sq = X.sq; ssq = X.ssq
            rn = X.rn; scl = X.scl; DTm = X.DTm; DLm = X.DLm; SN = X.SN; SL = X.SL
            Pm = X.Pm; S32 = X.S32[X.pp]; Sbf = X.Sbf[X.pp]; vnb = X.vnb; osb = X.osb
            onb = X.onb; orr = X.orr; osq = X.onb
            sp_ = seg % 2
            szT = X.szT[(seg + X.pq) % 3]; qkvT = X.qkvT[sp_]; scd = X.scd[sp_]; fmT = X.fmT[sp_]; ATb = X.ATb[sp_]; TTb = X.TTb[sp_]; wcT = X.wcT[sp_]
            sl = slice(seg * 512, (seg + 1) * 512)
            KNT, KBT, QNT, QDT = [fmT[:, m, :] for m in range(4)]
            for j in range(3):
                pbt = nextpb()
                for i in range(4):
                    TR(pbt[:, i * 128:(i + 1) * 128], pbt.b, qkvT[:, j, i * 128:(i + 1) * 128], qkvT.b, IDB)
                CP("act" if j == 1 else "dve", tm[:, j, :, :], tm.bs[j],
                   pbt[:, 0:512].rearrange("p (i c) -> p i c", i=4), pbt.b)
            for j in range(2):
                TT("dve", sq[:], sq.b, tm[:, j, :, :], tm.bs[j], tm[:, j, :, :], tm.bs[j], ALU.mult)
                RED(ssq[:, j, :], ssq.b, sq[:], sq.b)
            ACT(rn[:], rn.b, ssq[:], ssq.b, AF.Sqrt, bias=EPS_6, scale=1.0, xr=[epsb.b])
            P.op("dve", lambda e: e.reciprocal(out=rn[:], in_=rn[:]), reads=[rn.b], writes=[rn.b])
            t4 = slice(seg * 4, seg * 4 + 4)
            bt = beta[:, t4, h]; eg = eG[:, t4, h]; ek = ekd[:, t4, h]
            RQ = rn[:, 0, :]; RK = rn[:, 1, :]
            CP("dve", scl[:, 0, :], scl.bs[0], RK, rn.b)
            TT("dve", scl[:, 1, :], scl.bs[1], RK, rn.b, bt, beta.b, ALU.mult)
            TS("dve", scl[:, 2, :], scl.bs[2], RQ, rn.b, 128 ** -0.5, None, ALU.mult)
            TT("dve", scl[:, 3, :], scl.bs[3], scl[:, 2, :], scl.bs[2], eg, eG.b, ALU.mult)
            TT("dve", scl[:, 4, :], scl.bs[4], scl[:, 1, :], scl.bs[1], eg, eG.b, ALU.mult)
            TT("dve", scl[:, 5, :], scl.bs[5], RK, rn.b, ek, ekd.b, ALU.mult)
            CP("dve", scl[:, 6, :], scl.bs[6], bt, beta.b)
            srcs = [1, 1, 0, 0, 1, 1, 2]
            for m in range(4):
                TT("dve", scd[:, m, :, :], scd.bs[m], tm[:, srcs[m], :, :], tm.bs[srcs[m]],
                   scl[:, m, :].unsqueeze(2).broadcast_to([128, 4, 128]), scl.bs[m], ALU.mult)
            for m in range(4, 7):
                for i in range(4):
                    ACT(scd[:, m, i, :], scd.bs[m], tm[:, srcs[m], i, :], tm.bs[srcs[m]], AF.Copy,
                        scale=scl[:, m, i:i + 1], xr=[scl.bs[m]])
            for m in range(4):
                pbt = nextpb()
                for i in range(4):
                    TR(pbt[:, i * 128:(i + 1) * 128], pbt.b, scd[:, m, i, :], scd.bs[m], IDB)
                CP("act" if m % 2 else "dve", fmT[:, m, :], fmT.bs[m], pbt[:, 0:512], pbt.b)
            pkn = nextp(); pkl = nextp(); pat = nextp()
            offb = OFFD.unsqueeze(1).broadcast_to([128, 4, 128])
            TT("dve", sq[:], sq.b, DTm[:], DTm.bs, offb, b_cst, ALU.mult)
            for i in range(4):
                cs = slice(i * 128, (i + 1) * 128)
                MMl(pkn[:, cs], pkn.b, KNT[:, cs], fmT.bs[0], KBT[:, cs], fmT.bs[1])
            for i in range(4):
                cs = slice(i * 128, (i + 1) * 128)
                MMl(pkl[:, cs], pkl.b, KBT[:, cs], fmT.bs[1], KNT[:, cs], fmT.bs[0])
            for i in range(4):
                cs = slice(i * 128, (i + 1) * 128)
                MMl(pat[:, cs], pat.b, KNT[:, cs], fmT.bs[0], QNT[:, cs], fmT.bs[2])
            v4 = lambda ap: ap.rearrange("p (i c) -> p i c", i=4)
            TT("dve", SN[:], SN.b, v4(pkn[:]), pkn.b, sq[:], sq.b, ALU.mult)
            TT("dve", SL[:], SL.b, v4(pkl[:]), pkl.b, DLm[:], DLm.bs, ALU.mult)
            TT("dve", ATb[:], ATb.b, v4(pat[:]), pat.b, DTm[:], DTm.bs, ALU.mult)
            hoist = gdn_proj_steps(h, X, seg + 1) if seg + 1 < 4 else []
            hsched = {0: [0], 1: [1], 2: [2], 3: [3], 4: [4, 5]}
            idb4 = IDF.unsqueeze(1).broadcast_to([128, 4, 128])
            TT("dve", Pm[:], Pm.b, idb4, b_cst, SN[:], SN.b, ALU.subtract)
            for it in range(5):
                p1 = nextp(); p2 = nextp()
                for i in range(4):
                    cs = slice(i * 128, (i + 1) * 128)
                    if it < 4:
                        MMl(p1[:, cs], p1.b, SL[:, i, :], SL.b, SN[:, i, :], SN.b)
                    MMl(p2[:, cs], p2.b, SN[:, i, :], SN.b, SL[:, i, :], SL.b)
                if it < 4:
                    CP("act", SN[:], SN.b, v4(p1[:]), p1.b)
                CP("dve", SL[:], SL.b, v4(p2[:]), p2.b)
                if hoist:
                    for q_ in hsched[it]:
                        hoist[q_]()
                p3 = nextp()
                for i in range(4):
                    cs = slice(i * 128, (i + 1) * 128)
                    MMl(p3[:, cs], p3.b, SL[:, i, :], SL.b, Pm[:, i, :], Pm.b)
                TT("dve", Pm[:], Pm.b, v4(p3[:]), p3.b, Pm[:], Pm.b, ALU.add)
            CP("act", TTb[:], TTb.b, Pm[:], Pm.b)
            if dbg and h == 0 and seg == 0:
                dump("TT", Pm[:], [128, 4, 128], [Pm.b])
                dump("AT", ATb[:], [128, 4, 128], [ATb.b], BF16)
            pw = nextp()
            for i in range(4):
                cs = slice(i * 128, (i + 1) * 128)
                MMl(pw[:, cs], pw.b, scd[:, 4, i, :], scd.bs[4], TTb[:, i, :], TTb.b)
            TS("dve", wcT[:], wcT.b, pw[:], pw.b, -1.0, None, ALU.mult)
            if hoist:
                hoist[6]()
                gdn_decay(h, X, seg + 1)

        def gdn_rec(h, X, seg):
            wb = X.wb; pr = X.pr; dg = X.dg; tm = X.tm; sq = X.sq; ssq = X.ssq
            rn = X.rn; scl = X.scl; DTm = X.DTm; DLm = X.DLm; SN = X.SN; SL = X.SL
            Pm = X.Pm; S32 = X.S32[X.pp]; Sbf = X.Sbf[X.pp]; vnb = X.vnb; osb = X.osb
            onb = X.onb; orr = X.orr; osq = X.onb
            sp_ = seg % 2
            szT = X.szT[(seg + X.pq) % 3]; qkvT = X.qkvT[sp_]; scd = X.scd[sp_]; fmT = X.fmT[sp_]; ATb = X.ATb[sp_]; TTb = X.TTb[sp_]; wcT = X.wcT[sp_]
            sl = slice(seg * 512, (seg + 1) * 512)
            KNT, KBT, QNT, QDT = [fmT[:, m, :] for m in range(4)]
            def nextq():
                X.qi = (X.qi + 1) % 4
                return X.recq[X.qi]
            for c in range(8):
                i = c // 2; p0 = 64 * (c % 2); ps_ = slice(p0, p0 + 64)
                cs = slice(i * 128, (i + 1) * 128)
                t = seg * 4 + i
                pv = nextq()
                MMl(pv[:, 0:128], pv.b, TTb[ps_, i, :], TTb.b, scd[ps_, 6, i, :], scd.bs[6], start=True, stop=False)
                MMl(pv[:, 0:128], pv.b, wcT[:, cs], wcT.b, Sbf[:], Sbf.b, start=False, stop=True)
                CP("act", vnb[ps_, i, :], vnb.b, pv[ps_, 0:128], pv.b)
                po = nextq()
                MMl(po[:, 0:128], po.b, QDT[:, cs], fmT.bs[3], Sbf[:], Sbf.b, start=True, stop=False)
                MMl(po[:, 0:128], po.b, ATb[ps_, i, :], ATb.b, vnb[ps_, i, :], vnb.b, start=False, stop=True)
                CP("act", osb[ps_, i, :], osb.b, po[ps_, 0:128], po.b)
                pS = nextq()
                MMl(pS[:, 0:128], pS.b, scd[ps_, 5, i, :], scd.bs[5], vnb[ps_, i, :], vnb.b)
                STT("dve", Sbf[:], Sbf.b, S32[:], S32.b, cdt[:, t, c % 2, h:h + 1], pS[:, 0:128], pS.b,
                    ALU.mult, ALU.add, xr=[cdt.b])
                STT("dve", S32[:], S32.b, S32[:], S32.b, cdt[:, t, c % 2, h:h + 1], pS[:, 0:128], pS.b,
                    ALU.mult, ALU.add, xr=[cdt.b])
            TT("dve", osq[:], osq.b, osb[:], osb.b, osb[:], osb.b, ALU.mult)
            RED(orr[:], orr.b, osq[:], osq.b)
            ACT(orr[:], orr.b, orr[:], orr.b, AF.Sqrt, bias=EPS_6, scale=1.0 / 128.0, xr=[epsb.b])
            P.op("dve", lambda e: e.reciprocal(out=orr[:], in_=orr[:]), reads=[orr.b], writes=[orr.b])
            TT("dve", onb[:], onb.b, osb[:], osb.b, orr[:].unsqueeze(2).broadcast_to([128, 4, 128]), orr.b, ALU.mult)
            pbt = X.recb
            for i in range(4):
                TR(pbt[:, i * 128:(i + 1) * 128], pbt.bs, onb[:, i, :], onb.b, IDB)
            yo_ = X.ybo[0]
            STT("dve", yo_[:], yo_.b, pbt[:, 0:512], pbt.bs, nwc[:, 0:1], szT[:], szT.b,
                ALU.mult, ALU.mult, xr=[nwc.b])
            STO(yb_s[h][:, sl], b_ybs[h], yo_[:], yo_.b)


        GW = 1000.0
        P.begin_parallel()
        for hp in range(NH // 2):
            base_ = hp * 4 * GW
            for si in range(2):
                h = 2 * hp + si
                X = GX[si]
                X.pp = hp % 2
                X.pq = hp
                rb = 4 * si + 3
                X.recq = []
                for q in range(4):
                    v = Tl(pq[rb][:, q * 128:(q + 1) * 128], "recq")
                    v.b = pq[rb].b
                    v.bs = [v.b]
                    X.recq.append(v)
                X.recb = Tl(pqb[rb][:, 0:512], "recb")
                X.recb.b = pq[rb].b
                X.recb.bs = [pq[rb].b]
                X.qi = 0
                sp, sr = 2 * si, 2 * si + 1
                pctx["banks"] = [4 * si, 4 * si + 1, 4 * si + 2]; pctx["i"] = 0
                P.set_stream(sp); P.cnt[sp] = base_ - 200.0
                gdn_init(h, X)
                P.step[sr] = REC_STEP
                for seg in range(4):
                    P.set_stream(sp); P.cnt[sp] = base_ + seg * GW
                    gdn_prep(h, X, seg)
                    P.set_stream(sr); P.cnt[sr] = base_ + (seg + 1) * GW + REC_OFF
                    gdn_rec(h, X, seg)
        P.end_parallel()
        use_banks(range(8))
        new_stage(96)
        rp = load_rows([R_BV, R_GMG, R_GMB, R_BS])
        wv = [loc("wv%d" % i, [128, 8, 512], BF16) for i in range(4)]
        wsf = loc("wsf", [128, 8, 128]); wmT = loc("wmT", [128, 8, 128], BF16)
        LD(wsf[:], wsf.b, gm_ws.rearrange("g t s -> t g s"))
        TT("dve", wsf[:], wsf.b, wsf[:], wsf.b, CMASK.unsqueeze(1).broadcast_to([128, 8, 128]), b_cst, ALU.mult)
        for hf in range(2):
            pt = nextp()
            for j in range(4):
                TR(pt[:, j * 128:(j + 1) * 128], pt.b, wsf[:, hf * 4 + j, :], wsf.b, IDF)
            CP("act", wmT[:, hf * 4:hf * 4 + 4, :], wmT.b, pt[:].rearrange("p (j c) -> p j c", j=4), pt.b)
        for i in range(4):
            load_w(wv[i], w_in[:, i * 512:(i + 1) * 512])
        for cc in range(8):
            for seg in range(4):
                pt = nextp()
                for kc in range(8):
                    MMl(pt[:], pt.b, wv[cc // 4][:, kc, (cc % 4) * 128:(cc % 4 + 1) * 128], wv[cc // 4].b,
                        hT[:, kc, seg * 512:(seg + 1) * 512], seg_bufs(hT, seg), start=(kc == 0), stop=(kc == 7))
                P.op("act", lambda e, pt=pt, cc=cc, seg=seg: e.activation(
                    out=yaT[:, cc, seg * 512:(seg + 1) * 512], in_=pt[:], func=AF.Gelu, bias=cp[:, cc:cc + 1], scale=1.0),
                    reads=[pt.b, cp.b], writes=seg_bufs(yaT, seg))
        vpreS = [loc("vpre%d" % i, [128, D]) for i in range(4)]
        vlnS = [loc("vln%d" % i, [128, D]) for i in range(4)]
        vbtS = [loc("vbt%d" % i, [128, D], BF16) for i in range(4)]

        def gm_v(t, si):
            vpre = vpreS[si]; vln = vlnS[si]; vbt = vbtS[si]
            for hf in range(2):
                pt = nextp()
                for kc in range(8):
                    MMl(pt[:], pt.b, hT[:, kc, t * 128:(t + 1) * 128], hT.bs[t], wv[2 + hf][:, kc, :], wv[2 + hf].b,
                        start=(kc == 0), stop=(kc == 7))
                TT("dve", vpre[:, hf * 512:(hf + 1) * 512], vpre.b, pt[:], pt.b,
                   rp[:, 0, hf * 512:(hf + 1) * 512], rp.b, ALU.add)
            ACT(vpre[:], vpre.b, vpre[:], vpre.b, AF.Gelu)
            layer_norm(vpre[:], vpre.b, vln[:], vln.b, rp[:, 1, :], rp[:, 2, :], rp.b, vln[:], vln.b, c=si)
            CP("act", vbt[:], vbt.b, vln[:], vln.b)
            for g0 in (0, 4):
                pt = nextp()
                for j in range(4):
                    g = g0 + j
                    MMl(pt[:, j * 128:(j + 1) * 128], pt.b, vbt[:, g * 128:(g + 1) * 128], vbt.b, wmT[:, g, :], wmT.b,
                        start=True, stop=True)
                tmpy = vpre[:, g0 * 128:(g0 + 4) * 128]
                TT("dve", tmpy, vpre.b, pt[:], pt.b, rp[:, 3, g0 * 128:(g0 + 4) * 128], rp.b, ALU.add)
                TT("dve", yaT[:, g0:g0 + 4, t * 128:(t + 1) * 128], yaT.bs[t],
                   tmpy.rearrange("p (j c) -> p j c", j=4), vpre.b,
                   yaT[:, g0:g0 + 4, t * 128:(t + 1) * 128], yaT.bs[t], ALU.mult)
        par2(gm_v, NT, 4)
        if dbg:
            dump("yaT", yaT[:], [128, 8, T], yaT.bs, BF16)
        if stop_after <= 3:
            P.emit(final_waits=out_bufs)
            return nc, dbg_out


        new_stage(96)
        wv = [loc("wm%d" % i, [128, 8, 512], BF16) for i in range(3)]
        LD(ybT[:], ybT.b, yb_s.rearrange("c p t -> p c t"), reads=b_ybs)
        gat = [loc("gat%d" % i, [128, 2, 512]) for i in range(2)]
        t1 = loc("t1", [128, 512]); t2 = loc("t2", [128, 512])
        mo = [loc("mo%d" % i, [128, 512], BF16) for i in range(2)]
        for dc in range(8):
            wb = wv[dc % 3]
            load_w(wb, w_pa[:, dc * 128:(dc + 1) * 128], off=0, n=128)
            load_w(wb, w_pb[:, dc * 128:(dc + 1) * 128], off=128, n=128)
            load_w(wb, w_in[:, 6160 + dc * 128:6160 + (dc + 1) * 128], off=256, n=128)
            load_w(wb, w_in[:, 7184 + dc * 128:7184 + (dc + 1) * 128], off=384, n=128)
            for seg in range(4):
                sl = slice(seg * 512, (seg + 1) * 512)
                srcs = [(yaT, seg_bufs(yaT, seg)), (ybT, [ybT.b]), (hT, seg_bufs(hT, seg)), (hT, seg_bufs(hT, seg))]
                pp = []
                for j in range(4):
                    pt = nextp(); pp.append(pt)
                    for kc in range(8):
                        MMl(pt[:], pt.b, wb[:, kc, j * 128:(j + 1) * 128], wb.b, srcs[j][0][:, kc, sl], srcs[j][1],
                            start=(kc == 0), stop=(kc == 7))
                g = gat[seg % 2]; m_ = mo[seg % 2]
                ACT(g[:, 0, :], g.b, pp[2][:], pp[2].b, AF.Sigmoid, bias=cp[:, 48 + dc:49 + dc], scale=1.0, xr=[cp.b])
                ACT(g[:, 1, :], g.b, pp[3][:], pp[3].b, AF.Sigmoid, bias=cp[:, 56 + dc:57 + dc], scale=1.0, xr=[cp.b])
                TT("dve", t1[:], t1.b, pp[0][:], pp[0].b, g[:, 0, :], g.b, ALU.mult)
                TT("dve", t2[:], t2.b, pp[1][:], pp[1].b, g[:, 1, :], g.b, ALU.mult)
                TT("dve", m_[:], m_.b, t1[:], t1.b, t2[:], t2.b, ALU.add)
                STO(mT_s[dc][:, sl], b_mT, m_[:], m_.b)

        new_stage(96)
        h1 = av("h1", 0, [128, NT, D], F32, nb=16)
        mT = av("mT", 64 * KB, [128, 8, T], BF16)
        LD(mT[:], mT.b, mT_s.rearrange("c p t -> p c t"), reads=[b_mT])
        rp = load_rows([R_L1G, R_L1B])
        wo = [loc("wo%d" % i, [128, 8, 512], BF16) for i in range(2)]
        for i in range(2):
            load_w(wo[i], w_o[:, i * 512:(i + 1) * 512])
        h0r = [loc("h0r%d" % i, [128, D]) for i in range(4)]
        r1S = [loc("r1%d" % i, [128, D]) for i in range(4)]
        b_h1t = [Buf("h1s%d" % i) for i in range(NT)]
        h1_v = h1_s.rearrange("(t p) d -> t p d", p=128)

        def st4b(t, si):
            hr = h0r[si]; r1 = r1S[si]
            LD(hr[:], hr.b, h0_v[t], reads=[b_h0t[t]])
            for hf in range(2):
                pt = nextp()
                for kc in range(8):
                    MMl(pt[:], pt.b, mT[:, kc, t * 128:(t + 1) * 128], mT.b, wo[hf][:, kc, :], wo[hf].b,
                        start=(kc == 0), stop=(kc == 7))
                STT("dve", r1[:, hf * 512:(hf + 1) * 512], r1.b, hr[:, hf * 512:(hf + 1) * 512], hr.b, ALPHA,
                    pt[:], pt.b, ALU.mult, ALU.add)
            layer_norm(r1[:], r1.b, h1[:, t, :], h1.bs[t], rp[:, 0, :], rp[:, 1, :], rp.b, r1[:], r1.b, c=si)
            STO(h1_v[t], b_h1t[t], h1[:, t, :], h1.bs[t])
        par2(st4b, NT, 4)
        if dbg:
            dump("h1", h1[:], [128, NT, D], h1.bs)
        if stop_after <= 4:
            P.emit(final_waits=out_bufs)
            return nc, dbg_out

        new_stage(96)
        PF = {}
        pf_off = 140 * KB
        for si_ in range(2):
            for nm_, shp_, src_ in (("wA", [128, 8, DF], w1[si_].rearrange("(kc p) f -> p kc f", p=128)),
                                    ("wB", [128, 8, DF], w3[si_].rearrange("(kc p) f -> p kc f", p=128)),
                                    ("wC", [128, 4, D], w2[si_].rearrange("(fc p) d -> p fc d", p=128))):
                t_ = av("pf_%s%d" % (nm_, si_), pf_off, shp_, BF16)
                pf_off += 8 * KB
                LD(t_[:], t_.b, src_, eng="pool")
                PF[(si_, nm_)] = t_
        wrs = loc("wrs", [128, 8, 36]); LD(wrs[:], wrs.b, wr.rearrange("(kc p) c -> p kc c", p=128))
        brb = loc("brb", [128, 36]); LD(brb[:], brb.b, br.partition_broadcast(128))
        ecb = loc("ecb", [128, 32]); LD(ecb[:], ecb.b, ecol_d.partition_broadcast(128))
        lg = loc("lg", [128, NT, 36], nb=NT)
        h1T = [loc("h1T%d" % i, [128, 8, 128]) for i in range(4)]

        def st5a(t, si):
            hT_ = h1T[si]
            for half in range(2):
                pt = nextp()
                for j in range(4):
                    kc = half * 4 + j
                    TR(pt[:, j * 128:(j + 1) * 128], pt.b, h1[:, t, kc * 128:(kc + 1) * 128], h1.bs[t], IDF)
                CP("act", hT_[:, half * 4:half * 4 + 4, :], hT_.b, pt[:].rearrange("p (j c) -> p j c", j=4), pt.b)
            pl = nextp()
            for kc in range(8):
                MMl(pl[:, 0:36], pl.b, hT_[:, kc, :], hT_.b, wrs[:, kc, :], wrs.b, start=(kc == 0), stop=(kc == 7))
            TT("dve", lg[:, t, :], lg.bs[t], pl[:, 0:36], pl.b, brb[:], brb.b, ALU.add)
        par2(st5a, NT, 4)
        mg = loc("mg", [128, NT]); gh = loc("gh", [128, NT, 4]); egx = loc("egx", [128, NT, 4])
        sg = loc("sg", [128, NT]); pgrp = loc("pgrp", [128, NT])
        t48 = loc("t48", [128, NT, 4, 8]); les = loc("les", [128, NT, 8]); les2 = loc("les2", [128, NT, 8])
        m1 = loc("m1", [128, NT]); m2 = loc("m2", [128, NT]); oh1 = loc("oh1", [128, NT, 8]); oh2 = loc("oh2", [128, NT, 8])
        A1 = loc("A1", [128, NT, 4, 8]); A2 = loc("A2", [128, NT, 4, 8]); Ab = loc("Ab", [128, NT, 32], BF16)
        pos = loc("pos", [128, NT, 32]); pe_ = loc("pe_", [128, NT, 32]); t32 = loc("t32", [128, NT, 32])
        slf = loc("slf", [128, 2, NT]); pv_ = loc("pv_", [128, 2, NT]); val = loc("val", [128, 2, NT])
        lgg = lg[:, :, 0:4]
        le4 = lg[:, :, 4:36].rearrange("p t (g e) -> p t g e", g=4)

        def bc(ap, shape, ax):
            return ap.unsqueeze(ax).broadcast_to(shape)
        P.op("dve", lambda e: e.tensor_reduce(out=mg[:], in_=lgg, axis=AX.X, op=ALU.max), reads=[lg.bs], writes=[mg.b])
        TT("dve", gh[:], gh.b, lgg, lg.bs, bc(mg[:], [128, NT, 4], 2), mg.b, ALU.is_equal)
        TT("dve", egx[:], egx.b, lgg, lg.bs, bc(mg[:], [128, NT, 4], 2), mg.b, ALU.subtract)
        ACT(egx[:], egx.b, egx[:], egx.b, AF.Exp)
        RED(sg[:], sg.b, egx[:], egx.b)
        P.op("dve", lambda e: e.reciprocal(out=pgrp[:], in_=sg[:]), reads=[sg.b], writes=[pgrp.b])
        TT("dve", t48[:], t48.b, le4, lg.bs, bc(gh[:], [128, NT, 4, 8], 3), gh.b, ALU.mult)
        RED(les[:], les.b, t48[:].rearrange("p t g e -> p t e g"), t48.b)
        P.op("dve", lambda e: e.tensor_reduce(out=m1[:], in_=les[:], axis=AX.X, op=ALU.max), reads=[les.b], writes=[m1.b])
        TT("dve", oh1[:], oh1.b, les[:], les.b, bc(m1[:], [128, NT, 8], 2), m1.b, ALU.is_equal)
        STT("dve", les2[:], les2.b, oh1[:], oh1.b, -1e30, les[:], les.b, ALU.mult, ALU.add)
        P.op("dve", lambda e: e.tensor_reduce(out=m2[:], in_=les2[:], axis=AX.X, op=ALU.max), reads=[les2.b], writes=[m2.b])
        TT("dve", oh2[:], oh2.b, les2[:], les2.b, bc(m2[:], [128, NT, 8], 2), m2.b, ALU.is_equal)
        TT("dve", m2[:], m2.b, m1[:], m1.b, m2[:], m2.b, ALU.subtract)
        ACT(m2[:], m2.b, m2[:], m2.b, AF.Sigmoid)
        TT("dve", gt1[:], gt1.b, pgrp[:], pgrp.b, m2[:], m2.b, ALU.mult)
        TT("dve", gt2[:], gt2.b, pgrp[:], pgrp.b, gt1[:], gt1.b, ALU.subtract)
        TT("dve", A1[:], A1.b, bc(gh[:], [128, NT, 4, 8], 3), gh.b, bc(oh1[:], [128, NT, 4, 8], 2), oh1.b, ALU.mult)
        TT("dve", A2[:], A2.b, bc(gh[:], [128, NT, 4, 8], 3), gh.b, bc(oh2[:], [128, NT, 4, 8], 2), oh2.b, ALU.mult)
        A1f = A1[:].rearrange("p t g e -> p t (g e)"); A2f = A2[:].rearrange("p t g e -> p t (g e)")
        TT("dve", Ab[:], Ab.b, A1f, A1.b, A2f, A2.b, ALU.add)
        ppos = nextp()
        for t in range(NT):
            for tp in range(t):
                MMl(ppos[:, t * 32:(t + 1) * 32], ppos.b, ONB, b_cst, Ab[:, tp, :], Ab.b, start=(tp == 0), stop=False)
            MMl(ppos[:, t * 32:(t + 1) * 32], ppos.b, USB, b_cst, Ab[:, t, :], Ab.b, start=(t == 0), stop=True)
        CP("dve", pos[:], pos.b, ppos[:].rearrange("p (t e) -> p t e", t=NT), ppos.b)
        TT("dve", pe_[:], pe_.b, pos[:], pos.b, bc(ecb[:], [128, NT, 32], 1), ecb.b, ALU.add)
        for k, (Af, Ab_) in enumerate(((A1f, A1.b), (A2f, A2.b))):
            TT("dve", t32[:], t32.b, pe_[:], pe_.b, Af, Ab_, ALU.mult)
            RED(slf[:, k, :], slf.b, t32[:], t32.b)
            TT("dve", t32[:], t32.b, pos[:], pos.b, Af, Ab_, ALU.mult)
            RED(pv_[:, k, :], pv_.b, t32[:], t32.b)
        TS("dve", val[:], val.b, pv_[:], pv_.b, float(CAP) - 0.5, None, ALU.is_lt)
        TT("dve", slf[:], slf.b, slf[:], slf.b, val[:], val.b, ALU.mult)
        TS("dve", pv_[:], pv_.b, val[:], val.b, -float(NE * CAP), float(NE * CAP), ALU.mult, ALU.add)
        TT("dve", slf[:], slf.b, slf[:], slf.b, pv_[:], pv_.b, ALU.add)
        CP("dve", idx[:], idx.b, slf[:], slf.b)
        TT("dve", gt1[:], gt1.b, gt1[:], gt1.b, val[:, 0, :], val.b, ALU.mult)
        TT("dve", gt2[:], gt2.b, gt2[:], gt2.b, val[:, 1, :], val.b, ALU.mult)
        if dbg:
            dump("idx", idx[:], [128, 2, NT], [idx.b], I32)
            dump("gt1", gt1[:], [128, NT], [gt1.b])
            dump("gt2", gt2[:], [128, NT], [gt2.b])
        xbt = [loc("xbt%d" % i, [128, D], BF16) for i in range(2)]
        assert loc_off[0] <= 140 * KB, loc_off[0]
        P.fence()
        for t in range(NT):
            xb_ = xbt[t % 2]
            CP("act" if t % 2 else "dve", xb_[:], xb_.b, h1[:, t, :], h1.bs[t])
            for k in range(2):
                P.dma("pool", lambda e, t=t, k=k, xb_=xb_: e.indirect_dma_start(
                    out=xpad, out_offset=bass.IndirectOffsetOnAxis(ap=idx[:, k, t:t + 1], axis=0),
                    in_=xb_[:], in_offset=None, oob_is_err=False),
                    reads=[xb_.b, idx.b], writes=[b_xpad])

        new_stage(0)
        def ex_alloc(sx, si_):
            X = NS()
            X.wA = [PF[(si_, "wA")], loc("wA1" + sx, [128, 8, DF], BF16)]
            X.wB = [PF[(si_, "wB")], loc("wB1" + sx, [128, 8, DF], BF16)]
            X.wC = [PF[(si_, "wC")], loc("wC1" + sx, [128, 4, D], BF16)]
            X.xe = [loc("xe%d" % i + sx, [128, 2, D], BF16) for i in range(2)]
            X.xeT = [loc("xeT%d" % i + sx, [128, 8, CAP], BF16) for i in range(2)]
            X.hs = loc("hs" + sx, [128, CAP]); X.hbT = loc("hbT" + sx, [128, 4, CAP], BF16)
            X.yo = [loc("yo%d" % i + sx, [128, 2, D]) for i in range(2)]
            return X
        EX = [ex_alloc("_a", 0), ex_alloc("_b", 1)]
        assert loc_off[0] <= 140 * KB, loc_off[0]
        b_ypt = [Buf("ypad%d" % i) for i in range(NE)]

        def expert(ex, si):
            X = EX[si]
            wA = X.wA; wB = X.wB; wC = X.wC; xe = X.xe; xeT = X.xeT; hs = X.hs; hbT = X.hbT; yo = X.yo
            k = (ex // 2) % 2
            if ex >= 2:
                LD(wA[k][:], wA[k].b, w1[ex].rearrange("(kc p) f -> p kc f", p=128), eng="pool")
                LD(wB[k][:], wB[k].b, w3[ex].rearrange("(kc p) f -> p kc f", p=128), eng="pool")
                LD(wC[k][:], wC[k].b, w2[ex].rearrange("(fc p) d -> p fc d", p=128), eng="pool")
            LD(xe[k][:], xe[k].b, xpad[ex * CAP:(ex + 1) * CAP, :].rearrange("(s p) d -> p s d", p=128), reads=[b_xpad])
            for s in range(2):
                for half in range(2):
                    pbt = nextpb()
                    for j in range(4):
                        kc = half * 4 + j
                        TR(pbt[:, j * 128:(j + 1) * 128], pbt.b, xe[k][:, s, kc * 128:(kc + 1) * 128], xe[k].b, IDB)
                    CP("act" if half else "dve", xeT[k][:, half * 4:half * 4 + 4, s * 128:(s + 1) * 128], xeT[k].b,
                       pbt[:, 0:512].rearrange("p (j c) -> p j c", j=4), pbt.b)
            for fc in range(4):
                pa = nextp()
                for kc in range(8):
                    MMl(pa[:, 0:CAP], pa.b, wA[k][:, kc, fc * 128:(fc + 1) * 128], wA[k].b, xeT[k][:, kc, :], xeT[k].b,
                        start=(kc == 0), stop=(kc == 7))
                for kc in range(8):
                    MMl(pa[:, CAP:2 * CAP], pa.b, wB[k][:, kc, fc * 128:(fc + 1) * 128], wB[k].b, xeT[k][:, kc, :], xeT[k].b,
                        start=(kc == 0), stop=(kc == 7))
                ACT(hs[:], hs.b, pa[:, 0:CAP], pa.b, AF.Silu)
                TT("dve", hbT[:, fc, :], hbT.b, hs[:], hs.b, pa[:, CAP:2 * CAP], pa.b, ALU.mult)
            for s in range(2):
                for dh in range(2):
                    py = nextp()
                    for fc in range(4):
                        MMl(py[:], py.b, hbT[:, fc, s * 128:(s + 1) * 128], hbT.b, wC[k][:, fc, dh * 512:(dh + 1) * 512], wC[k].b,
                            start=(fc == 0), stop=(fc == 3))
                    CP("act" if dh else "dve", yo[k][:, s, dh * 512:(dh + 1) * 512], yo[k].b, py[:], py.b)
            STO(ypad[ex * CAP:(ex + 1) * CAP, :].rearrange("(s p) d -> p s d", p=128), b_ypt[ex], yo[k][:], yo[k].b)

        par2(expert, NE)

        new_stage(0)
        rp = load_rows([R_L2G, R_L2B])
        y1 = [loc("y1%d" % i, [128, D]) for i in range(4)]
        y2 = [loc("y2%d" % i, [128, D]) for i in range(4)]
        hrr = [loc("hrr%d" % i, [128, D]) for i in range(4)]
        r2S = [loc("r2%d" % i, [128, D]) for i in range(4)]; ot = [loc("ot%d" % i, [128, D]) for i in range(4)]
        for yy in y1 + y2:
            P.op("dve", lambda e, yy=yy: e.memset(yy[:], 0.0), writes=[yy.b])
        out_v = out_d.rearrange("(t p) d -> t p d", p=128)
        b_outt = [Buf("out%d" % i) for i in range(NT)]

        def st5c(t, si):
            a = si; r2 = r2S[si]
            LD(hrr[a][:], hrr[a].b, h1_v[t], reads=[b_h1t[t]])
            for k, yy in enumerate((y1[a], y2[a])):
                P.dma("pool", lambda e, t=t, k=k, yy=yy: e.indirect_dma_start(
                    out=yy[:], out_offset=None, in_=ypad,
                    in_offset=bass.IndirectOffsetOnAxis(ap=idx[:, k, t:t + 1], axis=0),
                    oob_is_err=False),
                    reads=[b_ypad, idx.b], writes=[yy.b])
            ACT(r2[:], r2.b, hrr[a][:], hrr[a].b, AF.Copy, scale=ALPHA)
            STT("dve", r2[:], r2.b, y1[a][:], y1[a].b, gt1[:, t:t + 1], r2[:], r2.b, ALU.mult, ALU.add, xr=[gt1.b])
            STT("dve", r2[:], r2.b, y2[a][:], y2[a].b, gt2[:, t:t + 1], r2[:], r2.b, ALU.mult, ALU.add, xr=[gt2.b])
            layer_norm(r2[:], r2.b, ot[a][:], ot[a].b, rp[:, 0, :], rp[:, 1, :], rp.b, r2[:], r2.b, c=si)
            out_bufs.append(STO(out_v[t], b_outt[t], ot[a][:], ot[a].b))
        par2(st5c, NT, 4)
        P.emit(final_waits=out_bufs)
    return nc, dbg_out


_CACHE = {}


def _consts():
    c = np.zeros((128, 9, 128), np.float32)
    i = np.arange(128)
    same = (i[:, None] // 64) == (i[None, :] // 64)
    c[:, 0] = np.eye(128)
    c[:, 1] = (same & (i[:, None] <= i[None, :]))
    c[:, 2] = (i[:, None] == 63) * np.ones((1, 128))
    c[:, 3] = (i[:, None] == 127) * np.ones((1, 128))
    c[:, 4] = np.where(same & (i[None, :] >= i[:, None]), 0.0, NEG)
    c[:, 5] = np.where(same & (i[None, :] <= i[:, None]), 0.0, NEG)
    c[:, 6] = 1.0 - np.eye(128)
    c[:, 7] = ((i[None, :] // 64) <= (i[:, None] // 64))
    c[:, 8] = (i[:, None] < i[None, :])
    selh = np.zeros((8, 16, 128), np.float32)
    for h in range(8):
        selh[h, h, :] = 1.0
        selh[h, 8 + h, :] = -1.0
    ecol = (np.arange(32, dtype=np.float32) * CAP).reshape(1, 32)
    return c, selh, ecol


def kernel(x, ln_in_g, ln_in_b, w_in, b_in, gm_ln_g, gm_ln_b, gm_w_s, gm_b_s, dn_conv_w, dn_a_log, dn_dt_bias,
           dn_norm_w, w_pa, w_pb, w_o, ln1_g, ln1_b, w_rg, b_rg, w_re, b_re, w1, w3, w2, ln2_g, ln2_b):
    f = lambda a: np.ascontiguousarray(np.asarray(a, dtype=np.float32))
    if "nc" not in _CACHE:
        _CACHE["nc"] = build()[0]
    nc = _CACHE["nc"]
    b_in0 = f(b_in)[0]
    rowp = np.stack([f(ln_in_g), f(ln_in_b), b_in0[1024:2048], f(gm_ln_g)[0], f(gm_ln_b)[0], f(ln1_g)[0], f(ln1_b)[0],
                     f(ln2_g)[0], f(ln2_b)[0], f(gm_b_s)[0].reshape(-1)], 0)
    colp = np.zeros((128, 96), np.float32)
    colp[:, 0:48] = b_in0[0:6144].reshape(48, 128).T
    colp[:, 48:56] = b_in0[6160:7184].reshape(8, 128).T
    colp[:, 56:64] = b_in0[7184:8208].reshape(8, 128).T
    cw = np.ascontiguousarray(f(dn_conv_w)[0].reshape(4, 24, 128).transpose(2, 1, 0))
    cst, selh, ecol = _consts()
    shared = {
        "w_in": f(w_in)[0], "w_pa": f(w_pa)[0], "w_pb": f(w_pb)[0], "w_o": f(w_o)[0],
        "w1": f(w1)[0], "w3": f(w3)[0], "w2": f(w2)[0], "gm_w_s": f(gm_w_s)[0],
        "rowp": np.ascontiguousarray(rowp), "colp": colp, "cw": cw,
        "ab_b": np.ascontiguousarray(b_in0[6144:6160].reshape(1, 16)),
        "hp": np.concatenate([f(dn_a_log)[0], f(dn_dt_bias)[0]]).reshape(1, 16),
        "nw_col": f(dn_norm_w)[0].reshape(128, 1),
        "wr": np.ascontiguousarray(np.concatenate([f(w_rg)[0], f(w_re)[0]], 1)),
        "br": np.concatenate([f(b_rg)[0], f(b_re)[0]]).reshape(1, 36),
        "cst": cst, "selh": selh, "ecol": ecol,
    }
    xs = f(x)
    in_maps = [dict(shared, x=xs[b]) for b in range(8)]
    res = run_bass_kernel_spmd(nc, in_maps, core_ids=list(range(8)))
    return np.stack([np.asarray(r["out"], dtype=np.float32) for r in res.results], 0)
```
